# Optimizing a Trainium2 kernel written in Bass

```python
import math
import jax, jax.numpy as jnp
from jax import lax
import numpy as np

D_MODEL = 1024
BATCH = 8
SEQ = 2048
DEPTH = 2
DEC_BATCH = 128
DEC_SEQ = 4
PAST_LEN = 16384
PAGE_SIZE = 128

MIX_W = D_MODEL
RET_W = (3 * MIX_W) // 8
S5_W = MIX_W // 4
GLA_W = MIX_W - RET_W - S5_W
RET_HEAD_DIM = 64
RET_HEADS = RET_W // RET_HEAD_DIM
ROPE_BASE = 10000.0
S5_CH_PER_GROUP = 16
S5_GROUPS = S5_W // S5_CH_PER_GROUP
S5_STATE = 64
S5_MIN_NEG = 1e-4
GLA_HEADS = 4
GLA_KEY_W = GLA_W // 2
GLA_DK = GLA_KEY_W // GLA_HEADS
GLA_DV = GLA_W // GLA_HEADS
GLA_RANK = 16
GLA_GATE_TEMP = 16.0
IN_COLS = 4 * RET_W + S5_W + 2 * GLA_KEY_W + 2 * GLA_W + GLA_RANK
D_FF = ((8 * D_MODEL // 3 + 127) // 128) * 128
CONV_W = 3
CHUNK = 64
EPS = 1e-6

kernel_name = 'hymba_style_retention_s5_gla_convffn_step'


def _in_split_points():
    sizes = [RET_W, RET_W, RET_W, RET_W, S5_W, GLA_KEY_W, GLA_KEY_W, GLA_W, GLA_W, GLA_RANK]
    pts, acc = [], 0
    for s in sizes[:-1]:
        acc += s
        pts.append(acc)
    return pts


def _rmsnorm(x, g):
    xf = x.astype(jnp.float32)
    y = xf * lax.rsqrt(jnp.mean(xf * xf, axis=-1, keepdims=True) + EPS)
    return (y * g.astype(jnp.float32)).astype(x.dtype)


def _rotary(x, pos):
    half = x.shape[-1] // 2
    inv = ROPE_BASE ** (-jnp.arange(half, dtype=jnp.float32) / half)
    ang = pos.astype(jnp.float32)[:, None] * inv[None, :]
    cos = jnp.cos(ang)[None, :, None, :]
    sin = jnp.sin(ang)[None, :, None, :]
    x1, x2 = x[..., :half], x[..., half:]
    return jnp.concatenate([x1 * cos - x2 * sin, x1 * sin + x2 * cos], axis=-1)


def _retention_chunked(q, k, v, s0):
    bsz, h, L, dk = q.shape
    dv = v.shape[-1]
    c = min(CHUNK, L)
    n = L // c
    log_gamma = jnp.log(1.0 - 2.0 ** (-5.0 - jnp.arange(h, dtype=jnp.float32)))
    q = q.reshape(bsz, h, n, c, dk)
    k = k.reshape(bsz, h, n, c, dk)
    v = v.reshape(bsz, h, n, c, dv)
    i = jnp.arange(c, dtype=jnp.float32)
    diff = i[:, None] - i[None, :]
    decay_mask = jnp.where(diff >= 0, jnp.exp(log_gamma[:, None, None] * jnp.maximum(diff, 0.0)), 0.0)
    scores = jnp.einsum('bhnid,bhnjd->bhnij', q, k) * decay_mask[None, :, None]
    o_intra = jnp.einsum('bhnij,bhnjv->bhniv', scores, v)
    k_to_end = k * jnp.exp(log_gamma[:, None] * (c - 1.0 - i)[None, :])[None, :, None, :, None]
    kv = jnp.einsum('bhnjd,bhnjv->bhndv', k_to_end, v)
    chunk_decay = jnp.exp(log_gamma * c)[None, :, None, None]

    def step(s, kv_c):
        return chunk_decay * s + kv_c, s

    s_last, s_prev = lax.scan(step, s0, jnp.moveaxis(kv, 2, 0))
    s_prev = jnp.moveaxis(s_prev, 0, 2)
    q_from_start = q * jnp.exp(log_gamma[:, None] * (i + 1.0)[None, :])[None, :, None, :, None]
    o_cross = jnp.einsum('bhnid,bhndv->bhniv', q_from_start, s_prev)
    return (o_intra + o_cross).reshape(bsz, h, L, dv), s_last


def _gla_chunked(q, k, v, log_g, s0):
    bsz, h, L, dk = q.shape
    dv = v.shape[-1]
    c = min(CHUNK, L)
    n = L // c
    q = q.reshape(bsz, h, n, c, dk)
    k = k.reshape(bsz, h, n, c, dk)
    log_g = log_g.reshape(bsz, h, n, c, dk)
    v = v.reshape(bsz, h, n, c, dv)
    b_cum = jnp.cumsum(log_g, axis=3)
    b_end = b_cum[:, :, :, -1:, :]
    q_in = q * jnp.exp(b_cum)
    k_in = k * jnp.exp(-b_cum)
    causal = jnp.tril(jnp.ones((c, c), dtype=bool))
    scores = jnp.where(causal, jnp.einsum('bhnid,bhnjd->bhnij', q_in, k_in), 0.0)
    o_intra = jnp.einsum('bhnij,bhnjv->bhniv', scores, v)
    kv = jnp.einsum('bhnjd,bhnjv->bhndv', k * jnp.exp(b_end - b_cum), v)
    decay = jnp.exp(b_end[:, :, :, 0, :])

    def step(s, inp):
        d_c, kv_c = inp
        return d_c[..., None] * s + kv_c, s

    s_last, s_prev = lax.scan(step, s0, (jnp.moveaxis(decay, 2, 0), jnp.moveaxis(kv, 2, 0)))
    s_prev = jnp.moveaxis(s_prev, 0, 2)
    o_cross = jnp.einsum('bhnid,bhndv->bhniv', q_in, s_prev)
    return (o_intra + o_cross).reshape(bsz, h, L, dv), s_last


def _s5_scan(u, h0_re, h0_im, lam_re, lam_im, log_dt, b_re, b_im, c_re, c_im, d):
    f32 = jnp.float32
    bsz, L, _ = u.shape
    ug = u.reshape(bsz, L, S5_GROUPS, S5_CH_PER_GROUP)
    lr = jnp.minimum(lam_re.astype(f32), -S5_MIN_NEG)
    li = lam_im.astype(f32)
    dt = jnp.exp(log_dt.astype(f32))[:, None]
    mag = jnp.exp(lr * dt)
    ar = mag * jnp.cos(li * dt)
    ai = mag * jnp.sin(li * dt)
    den = lr * lr + li * li
    cr = ((ar - 1.0) * lr + ai * li) / den
    ci = (ai * lr - (ar - 1.0) * li) / den
    b_re = b_re.astype(f32)
    b_im = b_im.astype(f32)
    bbar_re = cr[..., None] * b_re - ci[..., None] * b_im
    bbar_im = cr[..., None] * b_im + ci[..., None] * b_re
    xr = jnp.einsum('blgh,gph->blgp', ug, bbar_re)
    xi = jnp.einsum('blgh,gph->blgp', ug, bbar_im)
    ar_b = jnp.broadcast_to(ar, xr.shape)
    ai_b = jnp.broadcast_to(ai, xr.shape)

    def combine(e1, e2):
        a1r, a1i, b1r, b1i = e1
        a2r, a2i, b2r, b2i = e2
        return (a2r * a1r - a2i * a1i,
                a2r * a1i + a2i * a1r,
                a2r * b1r - a2i * b1i + b2r,
                a2r * b1i + a2i * b1r + b2i)

    _, _, hr, hi = lax.associative_scan(combine, (ar_b, ai_b, xr, xi), axis=1)
    t1 = jnp.arange(1, L + 1, dtype=f32)[:, None, None]
    pmag = jnp.exp(lr * dt * t1)
    pph = li * dt * t1
    pr = pmag * jnp.cos(pph)
    pi = pmag * jnp.sin(pph)
    h0r = h0_re.astype(f32)[:, None]
    h0i = h0_im.astype(f32)[:, None]
    hr = hr + pr * h0r - pi * h0i
    hi = hi + pr * h0i + pi * h0r
    y = jnp.einsum('gcp,blgp->blgc', c_re.astype(f32), hr) - jnp.einsum('gcp,blgp->blgc', c_im.astype(f32), hi)
    y = y.reshape(bsz, L, S5_W) + d.astype(f32) * u
    return y, hr[:, -1], hi[:, -1]


def _layer(x, pos, s_ret, s5r, s5i, s_gla, conv_buf,
           norm_mix_g, w_in, ret_norm_g, ret_norm_b,
           s5_lambda_re, s5_lambda_im, s5_log_dt, s5_b_re, s5_b_im, s5_c_re, s5_c_im, s5_d,
           s5_glu_w, s5_glu_b, gla_gate_w, gla_gate_b, gla_norm_g, w_out,
           norm_ffn_g, ffn_w_in, ffn_conv_w, ffn_conv_b, ffn_w_out):
    f32 = jnp.float32
    bsz, L, _ = x.shape
    h = _rmsnorm(x, norm_mix_g)
    proj = (h @ w_in).astype(f32)
    rq, rk, rv, rg, su, gq, gk, gv, gg, glr = jnp.split(proj, _in_split_points(), axis=-1)
    rq = _rotary(rq.reshape(bsz, L, RET_HEADS, RET_HEAD_DIM), pos)
    rk = _rotary(rk.reshape(bsz, L, RET_HEADS, RET_HEAD_DIM), pos) * (RET_HEAD_DIM ** -0.5)
    rv = rv.reshape(bsz, L, RET_HEADS, RET_HEAD_DIM)
    o_ret, s_ret_new = _retention_chunked(rq.transpose(0, 2, 1, 3), rk.transpose(0, 2, 1, 3),
                                          rv.transpose(0, 2, 1, 3), s_ret.astype(f32))
    o_ret = o_ret.transpose(0, 2, 1, 3)
    mu = jnp.mean(o_ret, axis=-1, keepdims=True)
    var = jnp.mean(jnp.square(o_ret - mu), axis=-1, keepdims=True)
    o_ret = ((o_ret - mu) * lax.rsqrt(var + EPS)).reshape(bsz, L, RET_W)
    o_ret = (o_ret * ret_norm_g.astype(f32) + ret_norm_b.astype(f32)) * jax.nn.silu(rg)
    y5, s5r_new, s5i_new = _s5_scan(su, s5r, s5i, s5_lambda_re, s5_lambda_im, s5_log_dt,
                                    s5_b_re, s5_b_im, s5_c_re, s5_c_im, s5_d)
    y5 = jax.nn.gelu(y5)
    o_s5 = y5 * jax.nn.sigmoid(y5 @ s5_glu_w.astype(f32) + s5_glu_b.astype(f32))
    log_g = jax.nn.log_sigmoid(glr @ gla_gate_w.astype(f32) + gla_gate_b.astype(f32)) / GLA_GATE_TEMP
    gq = gq.reshape(bsz, L, GLA_HEADS, GLA_DK).transpose(0, 2, 1, 3) * (GLA_DK ** -0.5)
    gk = gk.reshape(bsz, L, GLA_HEADS, GLA_DK).transpose(0, 2, 1, 3)
    gv = gv.reshape(bsz, L, GLA_HEADS, GLA_DV).transpose(0, 2, 1, 3)
    log_g = log_g.reshape(bsz, L, GLA_HEADS, GLA_DK).transpose(0, 2, 1, 3)
    o_gla, s_gla_new = _gla_chunked(gq, gk, gv, log_g, s_gla.astype(f32))
    o_gla = o_gla.transpose(0, 2, 1, 3)
    o_gla = o_gla * lax.rsqrt(jnp.mean(o_gla * o_gla, axis=-1, keepdims=True) + EPS) * gla_norm_g.astype(f32)
    o_gla = o_gla.reshape(bsz, L, GLA_W) * jax.nn.silu(gg)
    mix = jnp.concatenate([o_ret, o_s5, o_gla], axis=-1).astype(x.dtype)
    x = x + mix @ w_out
    h = _rmsnorm(x, norm_ffn_g)
    up = h @ ffn_w_in
    a, gate = jnp.split(up, 2, axis=-1)
    padded = jnp.concatenate([conv_buf.astype(a.dtype), a], axis=1)
    conv = ffn_conv_b + padded[:, 0:L] * ffn_conv_w[0]
    for t in range(1, CONV_W):
        conv = conv + padded[:, t:t + L] * ffn_conv_w[t]
    conv_new = padded[:, L:]
    x = x + (jax.nn.gelu(conv) * gate) @ ffn_w_out
    return x, s_ret_new, s5r_new, s5i_new, s_gla_new, conv_new


def _run_trunk(x, pos, s_ret, s5r, s5i, s_gla, conv, norm_mix_g, w_in, ret_norm_g, ret_norm_b,
               s5_lambda_re, s5_lambda_im, s5_log_dt, s5_b_re, s5_b_im, s5_c_re, s5_c_im, s5_d,
               s5_glu_w, s5_glu_b, gla_gate_w, gla_gate_b, gla_norm_g, w_out,
               norm_ffn_g, ffn_w_in, ffn_conv_w, ffn_conv_b, ffn_w_out, norm_final_g):
    new_ret, new_s5r, new_s5i, new_gla, new_conv = [], [], [], [], []
    for l in range(DEPTH):
        x, a1, a2, a3, a4, a5 = _layer(
            x, pos, s_ret[l], s5r[l], s5i[l], s_gla[l], conv[l],
            norm_mix_g[l], w_in[l], ret_norm_g[l], ret_norm_b[l],
            s5_lambda_re[l], s5_lambda_im[l], s5_log_dt[l], s5_b_re[l], s5_b_im[l],
            s5_c_re[l], s5_c_im[l], s5_d[l], s5_glu_w[l], s5_glu_b[l],
            gla_gate_w[l], gla_gate_b[l], gla_norm_g[l], w_out[l],
            norm_ffn_g[l], ffn_w_in[l], ffn_conv_w[l], ffn_conv_b[l], ffn_w_out[l])
        new_ret.append(a1)
        new_s5r.append(a2)
        new_s5i.append(a3)
        new_gla.append(a4)
        new_conv.append(a5)
    y = _rmsnorm(x, norm_final_g)
    return (y, jnp.stack(new_ret), jnp.stack(new_s5r), jnp.stack(new_s5i),
            jnp.stack(new_gla), jnp.stack(new_conv))


def setup_inputs(seed: int = 0) -> dict:
    key = jax.random.key(seed)
    ks = jax.random.split(key, 32)
    f32 = jnp.float32

    def nrm(i, shape, scale):
        return jax.random.normal(ks[i], shape, f32) * scale

    G, P, H16 = S5_GROUPS, S5_STATE, S5_CH_PER_GROUP
    return {
        'x_prompt': nrm(0, (BATCH, SEQ, D_MODEL), 1.0),
        'x_sample': nrm(1, (DEC_BATCH, DEC_SEQ, D_MODEL), 1.0),
        'state_ret': nrm(2, (DEPTH, DEC_BATCH, RET_HEADS, RET_HEAD_DIM, RET_HEAD_DIM), 0.3),
        'state_s5_re': nrm(3, (DEPTH, DEC_BATCH, G, P), 0.5),
        'state_s5_im': nrm(4, (DEPTH, DEC_BATCH, G, P), 0.5),
        'state_gla': nrm(5, (DEPTH, DEC_BATCH, GLA_HEADS, GLA_DK, GLA_DV), 0.3),
        'state_ffn_conv': nrm(6, (DEPTH, DEC_BATCH, CONV_W - 1, D_FF), 1.0),
        'norm_mix_g': 1.0 + nrm(7, (DEPTH, D_MODEL), 0.02),
        'w_in': nrm(8, (DEPTH, D_MODEL, IN_COLS), D_MODEL ** -0.5),
        'ret_norm_g': 1.0 + nrm(9, (DEPTH, RET_W), 0.02),
        'ret_norm_b': nrm(10, (DEPTH, RET_W), 0.02),
        's5_lambda_re': -0.5 + nrm(11, (DEPTH, G, P), 0.01),
        's5_lambda_im': jnp.pi * jnp.arange(P, dtype=f32) + nrm(12, (DEPTH, G, P), 0.01),
        's5_log_dt': jax.random.uniform(ks[13], (DEPTH, G), f32, math.log(1e-3), math.log(1e-1)),
        's5_b_re': nrm(14, (DEPTH, G, P, H16), (2.0 * H16) ** -0.5),
        's5_b_im': nrm(15, (DEPTH, G, P, H16), (2.0 * H16) ** -0.5),
        's5_c_re': nrm(16, (DEPTH, G, H16, P), (2.0 * P) ** -0.5),
        's5_c_im': nrm(17, (DEPTH, G, H16, P), (2.0 * P) ** -0.5),
        's5_d': nrm(18, (DEPTH, S5_W), 0.5),
        's5_glu_w': nrm(19, (DEPTH, S5_W, S5_W), S5_W ** -0.5),
        's5_glu_b': nrm(20, (DEPTH, S5_W), 0.02),
        'gla_gate_w': nrm(21, (DEPTH, GLA_RANK, GLA_KEY_W), GLA_RANK ** -0.5),
        'gla_gate_b': nrm(22, (DEPTH, GLA_KEY_W), 0.1),
        'gla_norm_g': 1.0 + nrm(23, (DEPTH, GLA_DV), 0.02),
        'w_out': nrm(24, (DEPTH, MIX_W, D_MODEL), MIX_W ** -0.5),
        'norm_ffn_g': 1.0 + nrm(25, (DEPTH, D_MODEL), 0.02),
        'ffn_w_in': nrm(26, (DEPTH, D_MODEL, 2 * D_FF), D_MODEL ** -0.5),
        'ffn_conv_w': nrm(27, (DEPTH, CONV_W, D_FF), CONV_W ** -0.5),
        'ffn_conv_b': nrm(28, (DEPTH, D_FF), 0.02),
        'ffn_w_out': nrm(29, (DEPTH, D_FF, D_MODEL), D_FF ** -0.5),
        'norm_final_g': 1.0 + nrm(30, (D_MODEL,), 0.02),
    }


def reference(x_prompt, x_sample, state_ret, state_s5_re, state_s5_im, state_gla, state_ffn_conv,
              norm_mix_g, w_in, ret_norm_g, ret_norm_b,
              s5_lambda_re, s5_lambda_im, s5_log_dt, s5_b_re, s5_b_im, s5_c_re, s5_c_im, s5_d,
              s5_glu_w, s5_glu_b, gla_gate_w, gla_gate_b, gla_norm_g, w_out,
              norm_ffn_g, ffn_w_in, ffn_conv_w, ffn_conv_b, ffn_w_out, norm_final_g):
    f32 = jnp.float32
    bp, lp, _ = x_prompt.shape
    bs, ls, _ = x_sample.shape
    pos_prompt = jnp.arange(lp, dtype=jnp.int32)
    pos_sample = PAST_LEN + jnp.arange(ls, dtype=jnp.int32)
    z_ret = jnp.zeros((DEPTH, bp, RET_HEADS, RET_HEAD_DIM, RET_HEAD_DIM), f32)
    z_s5 = jnp.zeros((DEPTH, bp, S5_GROUPS, S5_STATE), f32)
    z_gla = jnp.zeros((DEPTH, bp, GLA_HEADS, GLA_DK, GLA_DV), f32)
    z_conv = jnp.zeros((DEPTH, bp, CONV_W - 1, D_FF), x_prompt.dtype)
    y_prompt, ret_p, s5re_p, s5im_p, gla_p, conv_p = _run_trunk(
        x_prompt, pos_prompt, z_ret, z_s5, z_s5, z_gla, z_conv,
        norm_mix_g, w_in, ret_norm_g, ret_norm_b,
        s5_lambda_re, s5_lambda_im, s5_log_dt, s5_b_re, s5_b_im, s5_c_re, s5_c_im, s5_d,
        s5_glu_w, s5_glu_b, gla_gate_w, gla_gate_b, gla_norm_g, w_out,
        norm_ffn_g, ffn_w_in, ffn_conv_w, ffn_conv_b, ffn_w_out, norm_final_g)
    y_sample, ret_s, s5re_s, s5im_s, gla_s, conv_s = _run_trunk(
        x_sample, pos_sample, state_ret, state_s5_re, state_s5_im, state_gla, state_ffn_conv,
        norm_mix_g, w_in, ret_norm_g, ret_norm_b,
        s5_lambda_re, s5_lambda_im, s5_log_dt, s5_b_re, s5_b_im, s5_c_re, s5_c_im, s5_d,
        s5_glu_w, s5_glu_b, gla_gate_w, gla_gate_b, gla_norm_g, w_out,
        norm_ffn_g, ffn_w_in, ffn_conv_w, ffn_conv_b, ffn_w_out, norm_final_g)
    return (y_prompt, y_sample, ret_p, ret_s, s5re_p, s5re_s, s5im_p, s5im_s, gla_p, gla_s, conv_p, conv_s)
```

```python
import math
import numpy as np
import concourse.bass as bass
import concourse.mybir as mybir
from concourse.bass_utils import run_bass_kernel_spmd

F32 = mybir.dt.float32
BF16 = mybir.dt.bfloat16
AF = mybir.ActivationFunctionType
ALU = mybir.AluOpType
AX = mybir.AxisListType

D = 1024
SEQ = 2048
NPT = 16
NT = 17
NTOK = 2112
NS = 16
LS = 4
DEPTH = 2
PAST = 16384
INC = 2960
DFF = 2816
NFC = 22
EPS = 1e-6
GAM = [1.0 - 2.0 ** (-5.0 - h) for h in range(6)]
FGROUPS = [(0, 4), (4, 4), (8, 4), (12, 4), (16, 3), (19, 3)]
HSLOT = [0, 3, 1, 4, 2, 5]

ENGS = ("pe", "dve", "act", "pool", "sp")
import os
REORDER = os.environ.get('K_REORDER', '1') == '1'
WINDOW = int(os.environ.get('K_WINDOW', '1500'))
LAT = float(os.environ.get('K_LAT', '0.8'))
CPW = float(os.environ.get('K_CPW', '0.05'))
STRICT = os.environ.get('K_STRICT', '1') == '1'


class Buf:
    __slots__ = ("name", "excl", "members")

    def __init__(self, name, excl=False, members=None):
        self.name = name
        self.excl = excl
        self.members = members


def _expand(bufs):
    out = []
    for b in bufs:
        if b.members:
            out.extend(b.members)
        else:
            out.append(b)
    return out


class _Probe:
    def __getattr__(self, name):
        def f(*a, **k):
            self.__dict__["call"] = (name, a, k)
            return self
        return f


def _free_elems(ap):
    n = 1
    for s in list(ap.shape)[1:]:
        n *= int(s)
    return n


def _estimate_cost(eng, fn):
    try:
        p = _Probe()
        fn(p)
        name, a, k = p.__dict__["call"]
        out = k.get("out", a[0] if a else None)
        if eng == "pe":
            if name == "transpose":
                return 0.11
            rhs = k.get("rhs")
            n = _free_elems(rhs) if rhs is not None else 128
            f32 = rhs is not None and rhs.dtype == F32
            return n / 2400.0 * (4.0 if f32 else 1.0) + 0.02
        sz = _free_elems(out) if out is not None else 128
        if eng == "dve":
            return 1.1e-3 * sz + 0.07
        if eng == "act":
            return 1.0e-3 * sz + 0.08
        if eng == "pool":
            if k.get("op", None) == ALU.pow:
                return 0.7
            return 2.4e-3 * sz + 0.06
    except Exception:
        pass
    return {"pe": 0.15, "dve": 0.4, "act": 0.45, "pool": 0.7}.get(eng, 0.5)


class Op:
    __slots__ = ("eng", "fn", "reads", "writes", "is_dma", "deps", "raw", "needs_inc", "cnt", "dsem", "dval", "safe", "cost",
                 "tw", "ar")

    def __init__(self, eng, fn, reads, writes, is_dma, safe=False):
        self.safe = safe
        self.cost = 0.5
        self.tw = frozenset(writes)
        self.ar = frozenset(reads)
        self.eng = eng
        self.fn = fn
        self.reads = reads
        self.writes = writes
        self.is_dma = is_dma
        self.deps = None
        self.raw = ()
        self.needs_inc = False
        self.cnt = 0
        self.dsem = None
        self.dval = 0


class Sched:
    def __init__(self, nc, n_dma_sems=28, same_engine_sync=False):
        self.nc = nc
        self.ops = []
        self.n_dma_sems = n_dma_sems
        self.same_engine_sync = same_engine_sync
        self.barriers = []
        self.do_reorder = False
        self.window = 600

    def op(self, eng, fn, reads=(), writes=(), safe=False):
        reads = _expand(reads)
        writes = _expand(writes)
        rd = tuple(b for b in reads if not b.excl)
        wr = tuple(writes) + tuple(b for b in reads if b.excl)
        o = Op(eng, fn, rd, wr, False, safe)
        o.tw = frozenset(writes)
        o.ar = frozenset(reads)
        o.cost = _estimate_cost(eng, fn)
        self.ops.append(o)

    def dma(self, eng, out_ap, in_ap, reads=(), writes=(), **kw):
        def fn(e, out_ap=out_ap, in_ap=in_ap, kw=kw):
            return e.dma_start(out=out_ap, in_=in_ap, **kw)
        o = Op(eng, fn, tuple(_expand(reads)), tuple(_expand(writes)), True)
        try:
            nb = _free_elems(out_ap) * (4 if out_ap.dtype == F32 else 2) * int(out_ap.shape[0])
        except Exception:
            nb = 1 << 20
        o.cost = 2.0 + nb / 1.5e5
        self.ops.append(o)

    def barrier(self):
        self.barriers.append(len(self.ops))

    def _conflict_deps(self, ops):
        last_writer = {}
        readers = {}
        deps = []
        for i, op in enumerate(ops):
            d = set()
            for b in op.reads:
                j = last_writer.get(b)
                if j is not None:
                    d.add(j)
            for b in op.writes:
                j = last_writer.get(b)
                if j is not None:
                    d.add(j)
                d.update(readers.get(b, ()))
            for b in op.reads:
                readers.setdefault(b, []).append(i)
            for b in op.writes:
                last_writer[b] = i
                readers[b] = []
            d.discard(i)
            deps.append(d)
        return deps

    def _list_schedule(self, ops, window=600, lat=LAT):
        n = len(ops)
        deps = self._conflict_deps(ops)
        users = [[] for _ in range(n)]
        ndep = [0] * n
        for i, d in enumerate(deps):
            ndep[i] = len(d)
            for j in d:
                users[j].append(i)
        cp = [0.0] * n
        for i in range(n - 1, -1, -1):
            m = 0.0
            for u in users[i]:
                v = cp[u] + lat
                if v > m:
                    m = v
            cp[i] = ops[i].cost + m
        finish = [0.0] * n
        eng_free = {}
        done = [False] * n
        order = []
        head = 0
        ready = set(i for i in range(min(n, window)) if ndep[i] == 0)
        hi = min(n, window)
        while len(order) < n:
            best = None
            bt = None
            for i in ready:
                op = ops[i]
                t0 = eng_free.get(op.eng, 0.0)
                for j in deps[i]:
                    fj = finish[j] + (lat if ops[j].eng != op.eng or ops[j].is_dma else 0.0)
                    if fj > t0:
                        t0 = fj
                key = (t0 - CPW * cp[i], i)
                if bt is None or key < bt:
                    bt = key
                    best = i
                    bt0 = t0
            if best is None:
                raise RuntimeError("list scheduler stuck")
            i = best
            ready.discard(i)
            op = ops[i]
            t0 = bt0
            if op.is_dma:
                eng_free[op.eng] = t0 + 0.6
                finish[i] = t0 + op.cost
            else:
                finish[i] = t0 + op.cost
                eng_free[op.eng] = finish[i]
            done[i] = True
            order.append(i)
            for u in users[i]:
                ndep[u] -= 1
                if ndep[u] == 0 and u < hi:
                    ready.add(u)
            while head < n and done[head]:
                head += 1
            nhi = min(n, head + window)
            for u in range(hi, nhi):
                if ndep[u] == 0:
                    ready.add(u)
            hi = max(hi, nhi)
        return [ops[i] for i in order]

    def reorder(self, window=600):
        bounds = sorted(set(self.barriers))
        segs = []
        prev = 0
        for b in bounds + [len(self.ops)]:
            segs.append(self.ops[prev:b])
            prev = b
        new_ops = []
        new_barriers = []
        for k, seg in enumerate(segs):
            if k > 0:
                new_barriers.append(len(new_ops))
            new_ops.extend(self._list_schedule(seg, window) if seg else [])
        self.ops = new_ops
        self.barriers = new_barriers

    def finalize(self):
        nc = self.nc
        if self.do_reorder:
            self.reorder(self.window)
        ops = self.ops
        engobj = {"pe": nc.tensor, "dve": nc.vector, "act": nc.scalar, "pool": nc.gpsimd, "sp": nc.sync}
        barrier_set = set(self.barriers)
        last_writer = {}
        readers = {}
        last_on_eng = {}
        open_dmas = []
        pending_barrier = {}
        dma_sem_last = [None] * self.n_dma_sems
        n_sw = self.n_dma_sems // 2
        pools = {"pool": list(range(0, n_sw)), "sp": list(range(n_sw, self.n_dma_sems)),
                 "act": list(range(n_sw, self.n_dma_sems))}
        dma_rr = {"pool": 0, "sp": 0, "act": 0}
        for i, op in enumerate(ops):
            deps = set()
            if i in barrier_set:
                bd = set(last_on_eng.values())
                for j in dma_sem_last:
                    if j is not None:
                        bd.add(j)
                last_writer = {}
                readers = {}
                for e in ENGS:
                    pending_barrier[e] = bd
            pb = pending_barrier.pop(op.eng, None)
            if pb is not None:
                deps.update(pb)
            raw = set()
            for b in op.reads:
                j = last_writer.get(b)
                if j is not None:
                    deps.add(j)
                    raw.add(j)
            op.raw = raw
            for b in op.writes:
                j = last_writer.get(b)
                if j is not None:
                    deps.add(j)
                r = readers.get(b)
                if r:
                    deps.update(r)
            if op.is_dma:
                pl = pools[op.eng]
                s = pl[dma_rr[op.eng] % len(pl)]
                dma_rr[op.eng] += 1
                op.dsem = s
                prev = dma_sem_last[s]
                if prev is not None:
                    deps.add(prev)
                    op.dval = ops[prev].dval + 16
                else:
                    op.dval = 16
                dma_sem_last[s] = i
                open_dmas.append(i)
            for b in op.reads:
                r = readers.setdefault(b, [])
                if not op.is_dma:
                    r[:] = [j for j in r if ops[j].is_dma or ops[j].eng != op.eng]
                r.append(i)
            for b in op.writes:
                last_writer[b] = i
                readers[b] = []
            deps.discard(i)
            op.deps = deps
            if not op.is_dma:
                last_on_eng[op.eng] = i

        def skip_same(d, op, j):
            if d.is_dma or op.is_dma or d.eng != op.eng:
                return False
            if d.eng == "pe":
                return True
            if STRICT:
                return not ((d.tw & op.ar) or (d.tw & op.tw) or (d.ar & op.tw))
            return j not in op.raw

        for op in ops:
            for j in op.deps:
                d = ops[j]
                if d.is_dma or skip_same(d, op, j):
                    continue
                d.needs_inc = True
        cnt = {e: 0 for e in ENGS}
        for op in ops:
            if op.is_dma:
                continue
            if op.needs_inc:
                cnt[op.eng] += 1
            op.cnt = cnt[op.eng]
        esem = {e: nc.alloc_semaphore("s_" + e) for e in ENGS}
        dsem = [nc.alloc_semaphore("d%d" % k) for k in range(self.n_dma_sems)]
        know = {e: {} for e in ENGS}
        know_dma = {e: {} for e in ENGS}
        op_know = [None] * len(ops)
        n_wait = 0
        for i, op in enumerate(ops):
            e = op.eng
            eo = engobj[e]
            k = know[e]
            kd = know_dma[e]
            for j in sorted(op.deps):
                d = ops[j]
                if d.is_dma:
                    if kd.get(d.dsem, 0) >= d.dval:
                        continue
                    eo.wait_ge(dsem[d.dsem], d.dval)
                    n_wait += 1
                    kd[d.dsem] = d.dval
                else:
                    if skip_same(d, op, j):
                        continue
                    if k.get(d.eng, 0) >= d.cnt:
                        continue
                    eo.wait_ge(esem[d.eng], d.cnt)
                    n_wait += 1
                    k[d.eng] = d.cnt
                    ok = op_know[j]
                    if ok is not None:
                        for e2, c2 in ok.items():
                            if k.get(e2, 0) < c2:
                                k[e2] = c2
            ins = op.fn(eo)
            if op.is_dma:
                ins.then_inc(dsem[op.dsem], 16)
            elif op.needs_inc:
                ins.then_inc(esem[e], 1)
                op_know[i] = dict(k)
        sp = nc.sync
        for s in range(self.n_dma_sems):
            j = dma_sem_last[s]
            if j is not None:
                sp.wait_ge(dsem[s], ops[j].dval)
        for e in ENGS:
            if e != "sp" and cnt[e] > 0:
                sp.wait_ge(esem[e], cnt[e])
        self.n_wait = n_wait
        self.counts = cnt
        return self


class Tl:
    def __init__(self, h, name):
        self.h = h
        self.b = Buf(name)

    def __getitem__(self, idx):
        return self.h[idx]


class Arena:
    def __init__(self, nc, nbytes):
        self.nc = nc
        lo, hi = nc.bump_sbuf(nbytes)
        self.lo = lo
        self.hi = hi
        self.cur = lo
        self.n = 0

    def alloc(self, name, shape, dt):
        per = 1
        for s in shape[1:]:
            per *= s
        per *= 4 if dt == F32 else 2
        per = (per + 31) // 32 * 32
        off = self.cur
        if off + per > self.hi:
            raise RuntimeError("arena overflow at %s: need %d, have %d" % (name, per, self.hi - off))
        self.cur += per
        self.n += 1
        h = self.nc.alloc_sbuf_tensor_at("%s_%d" % (name, self.n), list(shape), dt, offset=off)
        return Tl(h, name)

    def mark(self):
        return self.cur

    def release(self, m):
        self.cur = m


def _const_tables():
    f32 = np.float32
    half = 32
    inv = np.power(f32(10000.0), -(np.arange(half, dtype=f32) / f32(half))).astype(f32)
    pos = np.zeros((NT, 128), dtype=f32)
    for t in range(NPT):
        pos[t] = np.arange(t * 128, (t + 1) * 128)
    pos[NPT, :64] = PAST + (np.arange(64) % 4)
    ang = (pos[:, :, None] * inv[None, None, :]).astype(f32)
    rope = np.zeros((128, NT, 2, half), dtype=f32)
    rope[:, :, 0, :] = np.cos(ang).astype(f32).transpose(1, 0, 2)
    rope[:, :, 1, :] = np.sin(ang).astype(f32).transpose(1, 0, 2)
    g = np.array(GAM, dtype=np.float64)
    j = np.arange(128)
    dif = j[None, :] - j[:, None]
    rmask_p = np.zeros((128, 6, 128))
    for h in range(6):
        rmask_p[:, HSLOT[h], :] = np.where(dif >= 0, 0.125 * g[h] ** np.maximum(dif, 0), 0.0)
    same = (j[:, None] // 4) == (j[None, :] // 4)
    rmask_s = np.zeros((128, 6, 64))
    for h in range(6):
        m = np.where((dif >= 0) & same, 0.125 * g[h] ** np.maximum(dif, 0), 0.0)
        rmask_s[:64, HSLOT[h], :] = m[:64, :64]
    qdec_p = np.zeros((128, 3, 128))
    qdec_s = np.zeros((128, 3, 64))
    sdec = np.zeros((128, 2, 3))
    for h in range(6):
        r0 = (h % 2) * 64
        qdec_p[r0:r0 + 64, h // 2, :] = (g[h] ** (j + 1.0))[None, :]
        qdec_s[r0:r0 + 64, h // 2, :] = (g[h] ** ((j[:64] % 4) + 1.0))[None, :]
        sdec[r0:r0 + 64, 0, h // 2] = g[h] ** 128
        sdec[r0:r0 + 64, 1, h // 2] = g[h] ** 4
    kdec = np.zeros((128, 2, 6))
    for h in range(6):
        kdec[:, 0, h] = 0.125 * g[h] ** (127.0 - j)
        kdec[:64, 1, h] = 0.125 * g[h] ** (3.0 - (j[:64] % 4))
    tri = np.zeros((128, 2, 128))
    upp = np.zeros((128, 2, 128))
    tri[:, 0, :] = (dif >= 0)
    upp[:, 0, :] = (dif < 0)
    tri[:64, 1, :64] = ((dif >= 0) & same)[:64, :64]
    upp[:64, 1, :64] = ((dif < 0) & same)[:64, :64]
    seqind = np.zeros((128, 16))
    seqind[:64, :] = (j[:64, None] // 4) == np.arange(16)[None, :]
    ident = np.eye(128)
    cc = np.concatenate([a.reshape(128, -1) for a in (ident, tri, upp, seqind)], axis=1).astype(f32)
    cr = np.concatenate([a.reshape(128, -1) for a in (rmask_p, rmask_s, qdec_p, qdec_s, kdec, sdec)],
                        axis=1).astype(f32)
    rope_t = np.ascontiguousarray(rope.reshape(128, NT, 64).transpose(1, 0, 2)).astype(f32)
    return np.ascontiguousarray(cc), np.ascontiguousarray(cr), rope_t


CC_OFF = {"ident": (0, 128), "tri": (128, 256), "upp": (384, 256), "seqind": (640, 16)}
CC_N = 656
CR_OFF = {}
_o = 0
for _n, _w in (("rmask_p", 768), ("rmask_s", 384), ("qdec_p", 384), ("qdec_s", 192),
               ("kdec", 12), ("sdec", 6)):
    CR_OFF[_n] = (_o, _w)
    _o += _w
CR_N = _o
PR_OFF = {}
_o = 0
for _n, _w in (("lamr", 8), ("lami", 8), ("ldt", 8), ("br", 128), ("bi", 128), ("ctr", 256), ("cti", 256),
               ("dsk", 2), ("glub", 2), ("gatew", 192), ("cw", 66), ("cb", 22)):
    PR_OFF[_n] = (_o, _w)
    _o += _w
PR_N = _o


def build_nc(stop_after=None, reorder=REORDER, window=WINDOW):
    nc = bass.Bass("TRN2", target_bir_lowering=False)
    S = Sched(nc)
    S.do_reorder = reorder
    S.window = window

    def din(name, shape):
        return nc.dram_tensor(name, list(shape), F32, kind="ExternalInput").ap()

    def dout(name, shape):
        return nc.dram_tensor(name, list(shape), F32, kind="ExternalOutput").ap()

    xp = din("xp", [SEQ, D])
    xs = din("xs", [64, D])
    w_in = din("w_in", [DEPTH, D, INC])
    w_out = din("w_out", [DEPTH, D, D])
    ffn_w_in = din("ffn_w_in", [DEPTH, D, 2 * DFF])
    ffn_w_out = din("ffn_w_out", [DEPTH, DFF, D])
    glu_w = din("glu_w", [DEPTH, 256, 256])
    gvecs = din("gvecs", [5, 128, D])
    retgb = din("retgb", [DEPTH, 128, 768])
    glag = din("glag", [DEPTH, 128, 96])
    prm_d = din("prm", [DEPTH, 128, PR_N])
    cc_d = din("cc", [128, CC_N])
    cr_d = din("cr", [128, CR_N])
    rope_d = din("rope", [NT, 128, 64])
    ret_s0 = din("ret_s0", [DEPTH, 128, NS * 3 * 64])
    gla_s0 = din("gla_s0", [DEPTH, 48, NS * 4 * 96])
    s5_h0 = din("s5_h0", [DEPTH, 2, 128, 8 * NS])
    conv_s0 = din("conv_s0", [DEPTH, 128, NFC * NS * 2])

    y_p = dout("y_p", [SEQ, D])
    y_s = dout("y_s", [64, D])
    o_ret_p = dout("o_ret_p", [DEPTH, 128, 192])
    o_ret_s = dout("o_ret_s", [DEPTH, 128, NS * 192])
    o_s5_p = dout("o_s5_p", [DEPTH, 2, 128, 8])
    o_s5_s = dout("o_s5_s", [DEPTH, 2, 128, 8 * NS])
    o_gla_p = dout("o_gla_p", [DEPTH, 48, 384])
    o_gla_s = dout("o_gla_s", [DEPTH, 48, NS * 384])
    o_conv = dout("o_conv", [DEPTH, 128, NFC * 34])

    A = Arena(nc, 209000)
    PS = nc.alloc_psum_tensor("ps", [128, 8, 512], F32)
    PSb = PS.bitcast(BF16)
    pb = [Buf("psum%d" % i, excl=True) for i in range(8)]

    def psf(b0, nb=1):
        return PS[:, b0:b0 + nb, :].rearrange("p b c -> p (b c)")

    def psbf(b0):
        return PSb[:, b0, :]

    xt = [A.alloc("x%d" % t, [128, D], F32) for t in range(NT)]
    hT = A.alloc("hT", [128, 8, NTOK], BF16)
    hTb = [Buf("hT%d" % t) for t in range(NT)]
    cc = A.alloc("cc", [128, CC_N], F32)
    prm = A.alloc("prm", [128, PR_N], F32)
    gA = A.alloc("gA", [128, D], F32)
    ident_bf = A.alloc("ident_bf", [128, 128], BF16)
    tri_bf = A.alloc("tri_bf", [128, 2, 128], BF16)
    ntri_bf = A.alloc("ntri_bf", [128, 2, 128], BF16)
    hb = A.alloc("hb", [128, D], BF16)
    junk = A.alloc("junk", [128, D], BF16)
    ss = A.alloc("ss", [128, 2], F32)
    rs = A.alloc("rs", [128, 4], F32)
    ssb = [Buf("ss0"), Buf("ss1")]
    rsb = [Buf("rs0"), Buf("rs1")]
    mhalf = A.alloc("mhalf", [128, 8], F32)

    def ccv(name):
        o, w = CC_OFF[name]
        return cc[:, o:o + w]

    ident_f = ccv("ident")
    tri_f = ccv("tri").rearrange("p (a b) -> p a b", a=2)
    upp_f = ccv("upp").rearrange("p (a b) -> p a b", a=2)
    seqind = ccv("seqind")

    def prv(name):
        o, w = PR_OFF[name]
        return prm[:, o:o + w]

    S.dma("sp", cc[:], cc_d[:, :], writes=[cc.b])
    S.op("dve", lambda e: e.tensor_copy(out=ident_bf[:], in_=ident_f), reads=[cc.b], writes=[ident_bf.b])
    S.op("dve", lambda e: e.tensor_copy(out=tri_bf[:], in_=tri_f), reads=[cc.b], writes=[tri_bf.b])
    S.op("dve", lambda e: e.tensor_scalar(out=ntri_bf[:], in0=tri_f, scalar1=-1.0, scalar2=None, op0=ALU.mult),
         reads=[cc.b], writes=[ntri_bf.b])
    S.op("pool", lambda e: e.memset(mhalf[:], -0.5), writes=[mhalf.b])

    def tile_info(t):
        return (t * 128, 128 if t < NPT else 64)

    ORD = list(range(NT))
    PAR = {t: i % 2 for i, t in enumerate(ORD)}

    hbL = [hb, junk]

    def norm_stats(t):
        tok0, n = tile_info(t)
        x = xt[t]
        k = t % 2
        S.op("act", lambda e: e.activation(out=hbL[k][:n, :], in_=x[:n, :], func=AF.Square, accum_out=ss[:n, k:k + 1]),
             reads=[x.b], writes=[hbL[k].b, ssb[k]])
        S.op("dve", lambda e: e.tensor_scalar(out=rs[:n, 2 * k:2 * k + 1], in0=ss[:n, k:k + 1], scalar1=1.0 / D, scalar2=EPS,
                                              op0=ALU.mult, op1=ALU.add), reads=[ssb[k]], writes=[rsb[k]])
        S.op("pool", lambda e: e.tensor_tensor(out=rs[:n, 2 * k + 1:2 * k + 2], in0=rs[:n, 2 * k:2 * k + 1], in1=mhalf[:n, 0:1], op=ALU.pow),
             reads=[rsb[k], mhalf.b], writes=[rsb[k]])

    def norm_apply(t, gt, bank=7):
        tok0, n = tile_info(t)
        x = xt[t]
        k = t % 2
        hbk = hbL[k]
        S.op("dve", lambda e: e.scalar_tensor_tensor(out=hbk[:n, :], in0=x[:n, :], scalar=rs[:n, 2 * k + 1:2 * k + 2], in1=gt[:n, :],
                                                     op0=ALU.mult, op1=ALU.mult),
             reads=[x.b, rsb[k], gt.b], writes=[hbk.b])
        pv = psbf(bank).rearrange("p (k c) -> p k c", k=8)
        for kc in range(8):
            S.op("pe", lambda e, kc=kc: e.transpose(out=pv[:, kc, :n], in_=hbk[:n, kc * 128:(kc + 1) * 128],
                                                    identity=ident_bf[:n, :n]),
                 reads=[hbk.b, ident_bf.b], writes=[pb[bank]])
        S.op("act", lambda e: e.activation(out=hT[:, :, tok0:tok0 + n], in_=pv[:, :, :n], func=AF.Copy),
             reads=[pb[bank]], writes=[hTb[t]])

    def norm_tile(t, gt, bank=7):
        norm_stats(t)
        norm_apply(t, gt, bank)

    def chunked(tl, n):
        tl.b = Buf(tl.b.name, members=[Buf("%s_%d" % (tl.b.name, i)) for i in range(n)])
        return tl

    def load_cast(dst_ap, src_ap, wbuf, idx):
        S.dma("pool", dst_ap, src_ap, writes=[wbuf.members[idx]], max_dma_last_dim=8192)

    lamb = A.alloc("lamb", [128, DEPTH, 24], F32)
    sp5L = [A.alloc("sp5_%d" % i, [128, 20, 8], F32) for i in range(DEPTH)]
    S.dma("sp", lamb[:], prm_d[:, :, 0:24].rearrange("l p c -> p l c"), writes=[lamb.b])

    def s5_scalars(l):
        sp5 = sp5L[l]
        def sv(i):
            return sp5[:, i, :]

        def s_op(eng, fn):
            S.op(eng, fn, reads=[sp5.b, lamb.b], writes=[sp5.b])

        lamr, lami, ldt = lamb[:, l, 0:8], lamb[:, l, 8:16], lamb[:, l, 16:24]
        s_op("dve", lambda e: e.tensor_scalar(out=sv(0), in0=lamr, scalar1=-1e-4, scalar2=None, op0=ALU.min))
        s_op("act", lambda e: e.activation(out=sv(1), in_=ldt, func=AF.Exp))
        s_op("dve", lambda e: e.tensor_tensor(out=sv(6), in0=sv(0), in1=sv(1), op=ALU.mult))
        s_op("act", lambda e: e.activation(out=sv(2), in_=sv(6), func=AF.Exp))
        s_op("dve", lambda e: e.tensor_tensor(out=sv(3), in0=lami, in1=sv(1), op=ALU.mult))
        s_op("dve", lambda e: e.tensor_scalar(out=sv(3), in0=sv(3), scalar1=1.0 / 16, scalar2=None, op0=ALU.mult))
        s_op("dve", lambda e: e.tensor_scalar(out=sv(7), in0=sv(3), scalar1=math.pi / 2, scalar2=None, op0=ALU.add))
        s_op("act", lambda e: e.activation(out=sv(5), in_=sv(3), func=AF.Sin))
        s_op("act", lambda e: e.activation(out=sv(4), in_=sv(7), func=AF.Sin))
        for _ in range(4):
            s_op("dve", lambda e: e.tensor_tensor(out=sv(6), in0=sv(4), in1=sv(4), op=ALU.mult))
            s_op("dve", lambda e: e.tensor_tensor(out=sv(7), in0=sv(5), in1=sv(5), op=ALU.mult))
            s_op("dve", lambda e: e.scalar_tensor_tensor(out=sv(5), in0=sv(4), scalar=2.0, in1=sv(5), op0=ALU.mult, op1=ALU.mult))
            s_op("dve", lambda e: e.tensor_tensor(out=sv(4), in0=sv(6), in1=sv(7), op=ALU.subtract))
        s_op("dve", lambda e: e.tensor_tensor(out=sv(8), in0=sv(2), in1=sv(4), op=ALU.mult))
        s_op("dve", lambda e: e.tensor_tensor(out=sv(9), in0=sv(2), in1=sv(5), op=ALU.mult))
        s_op("dve", lambda e: e.tensor_tensor(out=sv(6), in0=sv(0), in1=sv(0), op=ALU.mult))
        s_op("dve", lambda e: e.tensor_tensor(out=sv(7), in0=lami, in1=lami, op=ALU.mult))
        s_op("dve", lambda e: e.tensor_tensor(out=sv(6), in0=sv(6), in1=sv(7), op=ALU.add))
        s_op("dve", lambda e: e.reciprocal(out=sv(10), in_=sv(6)))
        s_op("dve", lambda e: e.tensor_scalar(out=sv(11), in0=sv(8), scalar1=-1.0, scalar2=None, op0=ALU.add))
        s_op("dve", lambda e: e.tensor_tensor(out=sv(6), in0=sv(11), in1=sv(0), op=ALU.mult))
        s_op("dve", lambda e: e.tensor_tensor(out=sv(7), in0=sv(9), in1=lami, op=ALU.mult))
        s_op("dve", lambda e: e.tensor_tensor(out=sv(6), in0=sv(6), in1=sv(7), op=ALU.add))
        s_op("dve", lambda e: e.tensor_tensor(out=sv(12), in0=sv(6), in1=sv(10), op=ALU.mult))
        s_op("dve", lambda e: e.tensor_tensor(out=sv(6), in0=sv(9), in1=sv(0), op=ALU.mult))
        s_op("dve", lambda e: e.tensor_tensor(out=sv(7), in0=sv(11), in1=lami, op=ALU.mult))
        s_op("dve", lambda e: e.tensor_tensor(out=sv(6), in0=sv(6), in1=sv(7), op=ALU.subtract))
        s_op("dve", lambda e: e.tensor_tensor(out=sv(13), in0=sv(6), in1=sv(10), op=ALU.mult))
        s_op("dve", lambda e: e.tensor_tensor(out=sv(6), in0=sv(2), in1=sv(2), op=ALU.mult))
        s_op("dve", lambda e: e.reciprocal(out=sv(16), in_=sv(6)))
        s_op("dve", lambda e: e.tensor_tensor(out=sv(14), in0=sv(8), in1=sv(16), op=ALU.mult))
        s_op("dve", lambda e: e.scalar_tensor_tensor(out=sv(15), in0=sv(9), scalar=-1.0, in1=sv(16), op0=ALU.mult, op1=ALU.mult))

    for l in range(DEPTH):
        s5_scalars(l)

    def layer(l):
        S.dma("sp", prm[:], prm_d[l, :, :], writes=[prm.b])
        S.dma("sp", gA[:], gvecs[2 * l, :, :], writes=[gA.b])
        if l == 0:
            for t in range(NT):
                tok0, n = tile_info(t)
                srcx = xp[tok0:tok0 + n, :] if t < NPT else xs[:, :]
                S.dma("sp", xt[t][:n, :], srcx, writes=[xt[t].b])
        norm_stats(0)
        norm_stats(1)

        m1 = A.mark()
        cr = A.alloc("cr", [128, CR_N], F32)

        def crv(name):
            o, w = CR_OFF[name]
            return cr[:, o:o + w]

        ropeL = [A.alloc("rope%d" % i, [128, 2, 32], F32) for i in range(2)]
        rmask = [crv("rmask_p").rearrange("p (h i) -> p h i", h=6), crv("rmask_s").rearrange("p (h i) -> p h i", h=6)]
        qdec = [crv("qdec_p").rearrange("p (a i) -> p a i", a=3), crv("qdec_s").rearrange("p (a i) -> p a i", a=3)]
        kdec = crv("kdec").rearrange("p (a h) -> p a h", a=2)
        sdec = crv("sdec").rearrange("p (a h) -> p a h", a=2)
        Wr = chunked(A.alloc("Wr", [128, 8, 1536], BF16), 8)
        WoR = chunked(A.alloc("WoR", [128, 3, D], BF16), 3)
        S0r = A.alloc("S0r", [128, NS, 3, 64], F32)
        rgb = A.alloc("rgb", [128, 768], F32)
        qkrot = A.alloc("qkrot", [128, 12, 2, 32], BF16)
        tmp = [A.alloc("rt%d" % i, [128, 12, 32], F32) for i in range(4)]
        kendL = [A.alloc("kend%d" % i, [128, 6, 64], BF16) for i in range(2)]
        vsbL = [A.alloc("vsb%d" % i, [128, 6, 64], BF16) for i in range(2)]
        gsL = [A.alloc("gs%d" % i, [128, 384], F32) for i in range(2)]
        bsL = [A.alloc("bs%d" % i, [128, 384], F32) for i in range(2)]
        sg = A.alloc("sg", [128, 384], F32)
        qT = A.alloc("qT", [128, 3, 128], BF16)
        kT = A.alloc("kT", [128, 3, 128], BF16)
        qsT = A.alloc("qsT", [128, 3, 128], BF16)
        qkrotL = [qkrot, A.alloc("qkrot1", [128, 12, 2, 32], BF16)]
        qTL = [qT, A.alloc("qT1", [128, 3, 128], BF16)]
        kTL = [kT, A.alloc("kT1", [128, 3, 128], BF16)]
        qsTL = [qsT, A.alloc("qsT1", [128, 3, 128], BF16)]
        scT = A.alloc("scT", [128, 6, 128], BF16)
        on1 = A.alloc("on1", [128, 6, 64], F32)
        on2 = A.alloc("on2", [128, 6, 64], F32)

        class Alias:
            def __init__(self, base, ap):
                self.b = base.b
                self.ap = ap

            def __getitem__(self, idx):
                return self.ap[idx]

        sq = Alias(on2, on2[:].rearrange("p h v -> p (h v)"))
        qsTf = Alias(tmp[2], tmp[2][:].rearrange("p a b -> p (a b)")[:, 0:192].rearrange("p (a i) -> p a i", a=3))
        ocT = Alias(tmp[0], tmp[0][0:64].rearrange("p a b -> p (a b)").rearrange("p (h i) -> p h i", h=6))
        kes = Alias(scT, scT[0:64, 0:3, :].rearrange("p a i -> p (a i)"))
        st = A.alloc("st", [128, 40], F32)
        mixr = A.alloc("mixr", [128, 384], BF16)
        mixT = A.alloc("mixT", [128, 3, 128], BF16)
        Sr = A.alloc("Sr", [128, 3, 64], F32)
        Srb = A.alloc("Srb", [128, 3, 64], BF16)

        S.dma("sp", cr[:], cr_d[:, :], writes=[cr.b])
        for kc in range(8):
            load_cast(Wr[:, kc, :], w_in[l, kc * 128:(kc + 1) * 128, 0:1536], Wr.b, kc)
        for kc in range(3):
            load_cast(WoR[:, kc, :], w_out[l, kc * 128:(kc + 1) * 128, :], WoR.b, kc)
        S.dma("sp", S0r[:].rearrange("p s a v -> p (s a v)"), ret_s0[l, :, :], writes=[S0r.b])
        S.dma("sp", rgb[:], retgb[l, :, :], writes=[rgb.b])
        S.op("pool", lambda e: e.memset(Sr[:], 0.0), writes=[Sr.b])
        S.op("pool", lambda e: e.memset(Srb[:], 0.0), writes=[Srb.b])

        for t in range(NT):
            if 1 <= t + 1 < NT and t + 1 >= 2:
                norm_stats(t + 1)
            norm_apply(t, gA, bank=6 + (t % 2))

        def p1_A(t):
            tok0, n = tile_info(t)
            for bank in range(3):
                for kc in range(8):
                    S.op("pe", lambda e, bank=bank, kc=kc: e.matmul(
                        PS[:n, bank, :], lhsT=hT[:, kc, tok0:tok0 + n], rhs=Wr[:, kc, bank * 512:(bank + 1) * 512],
                        start=(kc == 0), stop=(kc == 7)), reads=[hTb[t], Wr.b], writes=[pb[bank]])

        def p1_B(t):
            tok0, n = tile_info(t)
            sm = 0 if t < NPT else 1
            kend, vsb, gs, bs = kendL[PAR[t]], vsbL[PAR[t]], gsL[PAR[t]], bsL[PAR[t]]
            qkrot = qkrotL[PAR[t]]
            flat = psf(0, 3)
            qk = flat[:n, 0:768].rearrange("p (h c d) -> p h c d", h=12, c=2)
            x1 = qk[:, :, 0, :]
            x2 = qk[:, :, 1, :]
            rp = ropeL[PAR[t]]
            S.dma("sp", rp[:].rearrange("p c h -> p (c h)"), rope_d[t, :, :], writes=[rp.b])
            cosb = rp[:n, 0:1, :].broadcast_to([n, 12, 32])
            sinb = rp[:n, 1:2, :].broadcast_to([n, 12, 32])
            rd = [pb[0], pb[1], rp.b]
            S.op("dve", lambda e: e.tensor_tensor(out=tmp[0][:n], in0=x1, in1=cosb, op=ALU.mult), reads=rd, writes=[tmp[0].b])
            S.op("dve", lambda e: e.tensor_tensor(out=tmp[1][:n], in0=x2, in1=sinb, op=ALU.mult), reads=rd, writes=[tmp[1].b])
            S.op("dve", lambda e: e.tensor_tensor(out=tmp[2][:n], in0=x1, in1=sinb, op=ALU.mult), reads=rd, writes=[tmp[2].b])
            S.op("dve", lambda e: e.tensor_tensor(out=tmp[3][:n], in0=x2, in1=cosb, op=ALU.mult), reads=rd, writes=[tmp[3].b])
            S.op("act", lambda e: e.activation(out=vsb[:n].rearrange("p h d -> p (h d)"), in_=flat[:n, 768:1152], func=AF.Copy),
                 reads=[pb[1], pb[2]], writes=[vsb.b])
            S.op("act", lambda e: e.activation(out=sg[:n, :], in_=flat[:n, 1152:1536], func=AF.Silu),
                 reads=[pb[2]], writes=[sg.b])
            S.op("pool", lambda e: e.tensor_tensor(out=qkrot[:n, :, 0, :], in0=tmp[0][:n], in1=tmp[1][:n], op=ALU.subtract),
                 reads=[tmp[0].b, tmp[1].b], writes=[qkrot.b])
            S.op("pool", lambda e: e.tensor_tensor(out=qkrot[:n, :, 1, :], in0=tmp[2][:n], in1=tmp[3][:n], op=ALU.add),
                 reads=[tmp[2].b, tmp[3].b], writes=[qkrot.b])
            qkf = qkrot[:].rearrange("p h c d -> p (h c d)")
            krot = qkf[:n, 384:768].rearrange("p (h d) -> p h d", h=6)
            S.op("dve", lambda e: e.tensor_tensor(out=kend[:n], in0=krot, in1=kdec[:n, sm, :, None].broadcast_to([n, 6, 64]),
                                                  op=ALU.mult), reads=[qkrot.b, cr.b], writes=[kend.b])
            S.op("pool", lambda e: e.tensor_tensor(out=gs[:n, :], in0=sg[:n, :], in1=rgb[:n, 0:384], op=ALU.mult),
                 reads=[sg.b, rgb.b], writes=[gs.b])
            S.op("pool", lambda e: e.tensor_tensor(out=bs[:n, :], in0=sg[:n, :], in1=rgb[:n, 384:768], op=ALU.mult),
                 reads=[sg.b, rgb.b], writes=[bs.b])

        def p1_C(t):
            tok0, n = tile_info(t)
            sm = 0 if t < NPT else 1
            vsb = vsbL[PAR[t]]
            qkrot, qT, kT, qsT = qkrotL[PAR[t]], qTL[PAR[t]], kTL[PAR[t]], qsTL[PAR[t]]
            qkf = qkrot[:].rearrange("p h c d -> p (h c d)")
            tp = psbf(3)[:, 0:768].rearrange("p (a i) -> p a i", a=6)
            for a in range(6):
                S.op("pe", lambda e, a=a: e.transpose(out=tp[:, a, :n], in_=qkf[:n, a * 128:(a + 1) * 128],
                                                      identity=ident_bf[:n, :n]),
                     reads=[qkrot.b, ident_bf.b], writes=[pb[3]])
            S.op("act", lambda e: e.activation(out=kT[:, :, :n], in_=tp[:, 3:6, :n], func=AF.Copy), reads=[pb[3]], writes=[kT.b])
            S.op("act", lambda e: e.activation(out=qT[:, :, :n], in_=tp[:, 0:3, :n], func=AF.Copy), reads=[pb[3]], writes=[qT.b])
            if sm == 0:
                S.op("dve", lambda e: e.tensor_tensor(out=qsT[:, :, :n], in0=tp[:, 0:3, :n], in1=qdec[0][:, :, :n], op=ALU.mult),
                     reads=[pb[3], cr.b], writes=[qsT.b])
            else:
                S.op("dve", lambda e: e.tensor_tensor(out=qsTf[:, :, :n], in0=tp[:, 0:3, :n], in1=qdec[1][:, :, :n], op=ALU.mult),
                     reads=[pb[3], cr.b], writes=[qsTf.b])
            sc = psf(4, 2).rearrange("p (h i) -> p h i", h=8)
            for h in range(6):
                r0 = (h % 2) * 64
                sl_ = (h % 2) * 4 + h // 2
                S.op("pe", lambda e, h=h, r0=r0, sl_=sl_: e.matmul(sc[:n, sl_, :n], lhsT=kT[r0:r0 + 64, h // 2, :n],
                                                                  rhs=qT[r0:r0 + 64, h // 2, :n], start=True, stop=True),
                     reads=[kT.b, qT.b], writes=[pb[4 + h % 2]])
            for par in range(2):
                S.op("dve", lambda e, par=par: e.tensor_tensor(out=scT[:n, 3 * par:3 * par + 3, :n], in0=sc[:n, 4 * par:4 * par + 3, :n],
                                                              in1=rmask[sm][:n, 3 * par:3 * par + 3, :n], op=ALU.mult),
                     reads=[pb[4 + par], cr.b], writes=[scT.b])
            O = psf(6)[:, 0:384].rearrange("p (h v) -> p h v", h=6)
            if sm == 1:
                OCTs = [psf(3)[:, 0:192].rearrange("p (h i) -> p h i", h=3), psf(2)[:, 0:192].rearrange("p (h i) -> p h i", h=3)]
                obank = [3, 2]
                for s in range(NS):
                    for h in range(6):
                        r0 = (h % 2) * 64
                        S.op("pe", lambda e, s=s, h=h, r0=r0: e.matmul(
                            OCTs[h % 2][0:64, h // 2, 4 * s:4 * s + 4], lhsT=S0r[r0:r0 + 64, s, h // 2, :],
                            rhs=qsTf[r0:r0 + 64, h // 2, 4 * s:4 * s + 4], start=True, stop=True),
                            reads=[S0r.b, qsTf.b], writes=[pb[obank[h % 2]]])
                for par in range(2):
                    S.op("act", lambda e, par=par: e.activation(out=ocT[:, par * 3:par * 3 + 3, :], in_=OCTs[par][0:64, :, :], func=AF.Copy),
                         reads=[pb[obank[par]]], writes=[ocT.b])
            for h in range(6):
                r0 = (h % 2) * 64
                S.op("pe", lambda e, h=h: e.matmul(O[:n, h, :], lhsT=scT[:n, HSLOT[h], :n], rhs=vsb[:n, h, :], start=True, stop=False),
                     reads=[scT.b, vsb.b], writes=[pb[6]])
                if sm == 0:
                    S.op("pe", lambda e, h=h, r0=r0: e.matmul(O[:n, h, :], lhsT=qsT[r0:r0 + 64, h // 2, :n],
                                                             rhs=Srb[r0:r0 + 64, h // 2, :], start=False, stop=True),
                         reads=[qsT.b, Srb.b], writes=[pb[6]])
                else:
                    S.op("pe", lambda e, h=h: e.matmul(O[:n, h, :], lhsT=ocT[0:64, HSLOT[h], 0:64], rhs=ident_f[0:64, 0:64],
                                                      start=False, stop=True),
                         reads=[ocT.b, cc.b], writes=[pb[6]])

        def p1_H(t):
            tok0, n = tile_info(t)
            sm = 0 if t < NPT else 1
            kend, vsb = kendL[PAR[t]], vsbL[PAR[t]]
            if sm == 0:
                KV = psf(7)[:, 192:384].rearrange("p (a v) -> p a v", a=3)
                for h in range(6):
                    r0 = (h % 2) * 64
                    S.op("pe", lambda e, h=h, r0=r0: e.matmul(KV[r0:r0 + 64, h // 2, :], lhsT=kend[:n, h, :], rhs=vsb[:n, h, :],
                                                             start=True, stop=True), reads=[kend.b, vsb.b], writes=[pb[7]])
                S.op("dve", lambda e: e.tensor_tensor(out=Sr[:], in0=Sr[:], in1=sdec[:, 0, :, None].broadcast_to([128, 3, 64]),
                                                      op=ALU.mult), reads=[Sr.b, cr.b], writes=[Sr.b])
                S.op("dve", lambda e: e.tensor_tensor(out=Sr[:], in0=KV, in1=Sr[:], op=ALU.add), reads=[pb[7], Sr.b], writes=[Sr.b])
                S.op("act", lambda e: e.activation(out=Srb[:], in_=Sr[:], func=AF.Copy), reads=[Sr.b], writes=[Srb.b])
                if t == NPT - 1:
                    S.dma("sp", o_ret_p[l, :, :], Sr[:].rearrange("p a v -> p (a v)"), reads=[Sr.b])
            else:
                kendf = kend[:].rearrange("p h d -> p (h d)")
                for g4 in range(4):
                    KV = psf(4, 2)[:, 0:768].rearrange("p (s a v) -> p s a v", s=4, a=3)
                    for sl in range(4):
                        s = g4 * 4 + sl
                        S.op("dve", lambda e, s=s: e.tensor_scalar(out=kes[:, :], in0=kendf[:64, :], scalar1=seqind[:64, s:s + 1],
                                                                   scalar2=None, op0=ALU.mult),
                             reads=[kend.b, cc.b], writes=[kes.b])
                        for h in range(6):
                            r0 = (h % 2) * 64
                            bank = 4 + (sl * 3 + h // 2) // 8
                            S.op("pe", lambda e, sl=sl, h=h, r0=r0, KV=KV: e.matmul(
                                KV[r0:r0 + 64, sl, h // 2, :], lhsT=kes[:64, h * 64:(h + 1) * 64], rhs=vsb[:64, h, :],
                                start=True, stop=True), reads=[kes.b, vsb.b], writes=[pb[bank]])
                    S0g_ = S0r[:, g4 * 4:(g4 + 1) * 4, :, :]
                    S.op("dve", lambda e, S0g_=S0g_: e.tensor_tensor(
                        out=S0g_, in0=S0g_, in1=sdec[:, 1, None, :, None].broadcast_to([128, 4, 3, 64]), op=ALU.mult),
                        reads=[S0r.b, cr.b], writes=[S0r.b])
                    S.op("dve", lambda e, S0g_=S0g_, KV=KV: e.tensor_tensor(out=S0g_, in0=KV, in1=S0g_, op=ALU.add),
                         reads=[pb[4], pb[5], S0r.b], writes=[S0r.b])
                S.dma("sp", o_ret_s[l, :, :], S0r[:].rearrange("p s a v -> p (s a v)"), reads=[S0r.b])

        def p1_L(t):
            tok0, n = tile_info(t)
            gs, bs = gsL[PAR[t]], bsL[PAR[t]]
            O = psf(6)[:, 0:384].rearrange("p (h v) -> p h v", h=6)
            Of = psf(6)[:n, 0:384]
            S.op("act", lambda e: e.activation(out=sq[:n, :], in_=Of, func=AF.Square), reads=[pb[6]], writes=[sq.b])
            S.op("dve", lambda e: e.tensor_reduce(out=st[:n, 0:6], in_=O[:n], axis=AX.X, op=ALU.add), reads=[pb[6]], writes=[st.b])
            S.op("dve", lambda e: e.tensor_reduce(out=st[:n, 6:12], in_=sq[:n, :].rearrange("p (h v) -> p h v", h=6),
                                                  axis=AX.X, op=ALU.add), reads=[sq.b], writes=[st.b])
            S.op("dve", lambda e: e.tensor_scalar(out=st[:n, 12:18], in0=st[:n, 0:6], scalar1=1.0 / 64, scalar2=None, op0=ALU.mult),
                 reads=[st.b], writes=[st.b])
            S.op("dve", lambda e: e.tensor_tensor(out=st[:n, 18:24], in0=st[:n, 12:18], in1=st[:n, 12:18], op=ALU.mult),
                 reads=[st.b], writes=[st.b])
            S.op("dve", lambda e: e.scalar_tensor_tensor(out=st[:n, 24:30], in0=st[:n, 6:12], scalar=1.0 / 64, in1=st[:n, 18:24],
                                                         op0=ALU.mult, op1=ALU.subtract), reads=[st.b], writes=[st.b])
            S.op("dve", lambda e: e.tensor_scalar(out=st[:n, 30:36], in0=st[:n, 24:30], scalar1=EPS, scalar2=None, op0=ALU.add),
                 reads=[st.b], writes=[st.b])
            S.op("pool", lambda e: e.tensor_tensor(out=st[:n, 24:30], in0=st[:n, 30:36], in1=mhalf[:n, 0:6], op=ALU.pow),
                 reads=[st.b, mhalf.b], writes=[st.b])
            S.op("dve", lambda e: e.tensor_tensor(out=on1[:n], in0=O[:n], in1=st[:n, 12:18, None].broadcast_to([n, 6, 64]),
                                                  op=ALU.subtract), reads=[pb[6], st.b], writes=[on1.b])
            S.op("dve", lambda e: e.tensor_tensor(out=on2[:n], in0=on1[:n], in1=st[:n, 24:30, None].broadcast_to([n, 6, 64]),
                                                  op=ALU.mult), reads=[on1.b, st.b], writes=[on2.b])
            on2f = on2[:].rearrange("p h v -> p (h v)")
            on1f = on1[:].rearrange("p h v -> p (h v)")
            S.op("dve", lambda e: e.tensor_tensor(out=on1f[:n], in0=on2f[:n], in1=gs[:n, :], op=ALU.mult),
                 reads=[on2.b, gs.b], writes=[on1.b], safe=True)
            S.op("dve", lambda e: e.tensor_tensor(out=mixr[:n, :], in0=on1f[:n], in1=bs[:n, :], op=ALU.add),
                 reads=[on1.b, bs.b], writes=[mixr.b], safe=True)

        def p1_G(t):
            tok0, n = tile_info(t)
            x = xt[t]
            mp = psbf(7)[:, 0:384].rearrange("p (a i) -> p a i", a=3)
            for a in range(3):
                S.op("pe", lambda e, a=a: e.transpose(out=mp[:, a, :n], in_=mixr[:n, a * 128:(a + 1) * 128],
                                                      identity=ident_bf[:n, :n]), reads=[mixr.b, ident_bf.b], writes=[pb[7]])
            S.op("act", lambda e: e.activation(out=mixT[:, :, :n], in_=mp[:, :, :n], func=AF.Copy), reads=[pb[7]], writes=[mixT.b])
            for bank in range(2):
                for a in range(3):
                    S.op("pe", lambda e, bank=bank, a=a: e.matmul(PS[:n, 4 + bank, :], lhsT=mixT[:, a, :n],
                                                                 rhs=WoR[:, a, bank * 512:(bank + 1) * 512],
                                                                 start=(a == 0), stop=(a == 2)),
                         reads=[mixT.b, WoR.b], writes=[pb[4 + bank]])
            S.op("dve", lambda e: e.tensor_tensor(out=x[:n, :], in0=psf(4, 2)[:n, :], in1=x[:n, :], op=ALU.add),
                 reads=[pb[4], pb[5], x.b], writes=[x.b])

        p1_A(ORD[0])
        p1_B(ORD[0])
        for i, t in enumerate(ORD):
            nx = ORD[i + 1] if i + 1 < NT else None
            p1_C(t)
            p1_H(t)
            if nx is not None:
                p1_A(nx)
            p1_L(t)
            if nx is not None:
                p1_B(nx)
            p1_G(t)
        S.barrier()
        A.release(m1)
        if stop_after is not None and stop_after.startswith("P1"):
            return True

        m2 = A.mark()
        W5 = chunked(A.alloc("W5", [128, 8, 256], BF16), 8)
        Wo5 = chunked(A.alloc("Wo5", [128, 2, D], BF16), 2)
        gluw = chunked(A.alloc("gluw", [128, 2, 256], BF16), 2)
        Tfr = A.alloc("Tfr", [128, 8, 128], F32)
        Tfi = A.alloc("Tfi", [128, 8, 128], F32)
        nTfi = A.alloc("nTfi", [128, 8, 128], F32)
        Tinv = A.alloc("Tinv", [128, 8, 2, 128], F32)
        Tfs = [A.alloc("Tfs%d" % i, [128, 8, 64], F32) for i in range(3)]
        Tinvs = A.alloc("Tinvs", [64, 8, 2, 128], F32)
        Zt = A.alloc("Zt", [128, 8, 2, 32], BF16)
        Bc = A.alloc("Bc", [128, 2, 2, 128], BF16)
        Ctb = A.alloc("Ctb", [128, 2, 8, 32], BF16)
        H0 = A.alloc("H0", [128, 2, 8, NS], F32)
        Hst = A.alloc("Hst", [128, 2, 8], F32)
        Hb = [Buf("H%d" % g) for g in range(8)]
        Hout = A.alloc("Hout", [128, 2, 8, NS], F32)

        for kc in range(8):
            load_cast(W5[:, kc, :], w_in[l, kc * 128:(kc + 1) * 128, 1536:1792], W5.b, kc)
        for kc in range(2):
            load_cast(Wo5[:, kc, :], w_out[l, 384 + kc * 128:384 + (kc + 1) * 128, :], Wo5.b, kc)
            load_cast(gluw[:, kc, :], glu_w[l, kc * 128:(kc + 1) * 128, :], gluw.b, kc)
        S.dma("sp", H0[:].rearrange("p c g s -> p c (g s)"), s5_h0[l].rearrange("c p n -> p c n"), writes=[H0.b])

        m2s = A.mark()
        Pir = A.alloc("Pir", [128, 8, 128], F32)
        Pii = A.alloc("Pii", [128, 8, 128], F32)
        pt = [A.alloc("pt%d" % i, [128, 8, 64], F32) for i in range(4)]
        bt2 = [A.alloc("pu%d" % i, [128, 8, 64], F32) for i in range(4)]
        bt = [A.alloc("bt%d" % i, [128, 8, 16], F32) for i in range(4)]
        Tis = [A.alloc("Tis%d" % i, [128, 8, 64], F32) for i in range(2)]

        sp5 = sp5L[l]

        def sv(i):
            return sp5[:, i, :]


        br = prv("br").rearrange("p (g h) -> p g h", g=8)
        bi = prv("bi").rearrange("p (g h) -> p g h", g=8)

        def bc16(i):
            return sp5[:, i, :, None].broadcast_to([128, 8, 16])

        S.op("pool", lambda e: e.memset(Zt[:], 0.0), writes=[Zt.b])
        rb = [sp5.b, prm.b]
        S.op("dve", lambda e: e.tensor_tensor(out=bt[0][:], in0=br, in1=bc16(12), op=ALU.mult), reads=rb, writes=[bt[0].b])
        S.op("dve", lambda e: e.tensor_tensor(out=bt[1][:], in0=bi, in1=bc16(13), op=ALU.mult), reads=rb, writes=[bt[1].b])
        S.op("dve", lambda e: e.tensor_tensor(out=bt[2][:], in0=bi, in1=bc16(12), op=ALU.mult), reads=rb, writes=[bt[2].b])
        S.op("dve", lambda e: e.tensor_tensor(out=bt[3][:], in0=br, in1=bc16(13), op=ALU.mult), reads=rb, writes=[bt[3].b])
        for (p0, c0) in ((0, 0), (64, 16)):
            S.op("dve", lambda e, p0=p0, c0=c0: e.tensor_tensor(out=Zt[p0:p0 + 64, :, 0, c0:c0 + 16], in0=bt[0][p0:p0 + 64],
                                                               in1=bt[1][p0:p0 + 64], op=ALU.subtract),
                 reads=[bt[0].b, bt[1].b, Zt.b], writes=[Zt.b])
            S.op("dve", lambda e, p0=p0, c0=c0: e.tensor_tensor(out=Zt[p0:p0 + 64, :, 1, c0:c0 + 16], in0=bt[2][p0:p0 + 64],
                                                               in1=bt[3][p0:p0 + 64], op=ALU.add),
                 reads=[bt[2].b, bt[3].b, Zt.b], writes=[Zt.b])
        BCp = psf(0).rearrange("p (a c i) -> p a c i", a=2, c=2)
        for gp in range(8):
            for ri in range(2):
                q0 = 32 * (gp % 4)
                S.op("pe", lambda e, gp=gp, ri=ri, q0=q0: e.matmul(BCp[q0:q0 + 32, gp // 4, ri, :], lhsT=Zt[:, gp, ri, :],
                                                                  rhs=ident_bf[:, :], start=True, stop=True,
                                                                  tile_position=(0, q0)),
                     reads=[Zt.b, ident_bf.b], writes=[pb[0]])
        S.op("act", lambda e: e.activation(out=Bc[:], in_=BCp, func=AF.Copy), reads=[pb[0]], writes=[Bc.b])
        S.op("dve", lambda e: e.tensor_copy(out=Ctb[:, 0, :, :], in_=prv("ctr").rearrange("p (g c) -> p g c", g=8)),
             reads=[prm.b], writes=[Ctb.b])
        S.op("dve", lambda e: e.tensor_copy(out=Ctb[:, 1, :, :], in_=prv("cti").rearrange("p (g c) -> p g c", g=8)),
             reads=[prm.b], writes=[Ctb.b])

        def powers(Pr, Pi, ir, ii, en="dve"):
            ptx = pt if en == "dve" else bt2
            S.op(en, lambda e: e.tensor_copy(out=Pr[:, :, 0], in_=sv(ir)), reads=[sp5.b], writes=[Pr.b])
            S.op(en, lambda e: e.tensor_copy(out=Pi[:, :, 0], in_=sv(ii)), reads=[sp5.b], writes=[Pi.b])
            nn = 1
            while nn < 128:
                a_r = Pr[:, :, 0:nn]
                a_i = Pi[:, :, 0:nn]
                c_r = Pr[:, :, nn - 1:nn].broadcast_to([128, 8, nn])
                c_i = Pi[:, :, nn - 1:nn].broadcast_to([128, 8, nn])
                rw = [Pr.b, Pi.b]
                S.op(en, lambda e, a_r=a_r, c_r=c_r, nn=nn: e.tensor_tensor(out=ptx[0][:, :, 0:nn], in0=a_r, in1=c_r, op=ALU.mult),
                     reads=rw, writes=[ptx[0].b])
                S.op(en, lambda e, a_i=a_i, c_i=c_i, nn=nn: e.tensor_tensor(out=ptx[1][:, :, 0:nn], in0=a_i, in1=c_i, op=ALU.mult),
                     reads=rw, writes=[ptx[1].b])
                S.op(en, lambda e, a_r=a_r, c_i=c_i, nn=nn: e.tensor_tensor(out=ptx[2][:, :, 0:nn], in0=a_r, in1=c_i, op=ALU.mult),
                     reads=rw, writes=[ptx[2].b])
                S.op(en, lambda e, a_i=a_i, c_r=c_r, nn=nn: e.tensor_tensor(out=ptx[3][:, :, 0:nn], in0=a_i, in1=c_r, op=ALU.mult),
                     reads=rw, writes=[ptx[3].b])
                S.op(en, lambda e, nn=nn: e.tensor_tensor(out=Pr[:, :, nn:2 * nn], in0=ptx[0][:, :, 0:nn], in1=ptx[1][:, :, 0:nn],
                                                             op=ALU.subtract), reads=[ptx[0].b, ptx[1].b, Pr.b], writes=[Pr.b])
                S.op(en, lambda e, nn=nn: e.tensor_tensor(out=Pi[:, :, nn:2 * nn], in0=ptx[2][:, :, 0:nn], in1=ptx[3][:, :, 0:nn],
                                                             op=ALU.add), reads=[ptx[2].b, ptx[3].b, Pi.b], writes=[Pi.b])
                nn *= 2

        powers(Tfr, Tfi, 8, 9)
        powers(Pir, Pii, 14, 15, en="pool")
        S.op("dve", lambda e: e.tensor_scalar(out=nTfi[:], in0=Tfi[:], scalar1=-1.0, scalar2=None, op0=ALU.mult),
             reads=[Tfi.b], writes=[nTfi.b])
        for i, src in enumerate((Tfr, Tfi, nTfi)):
            S.op("dve", lambda e, i=i, src=src: e.tensor_copy(
                out=Tfs[i][:].rearrange("p g (s t) -> p g s t", s=NS),
                in_=src[:, :, None, 0:4].broadcast_to([128, 8, NS, 4])), reads=[src.b], writes=[Tfs[i].b])
        for i, src in enumerate((Pir, Pii)):
            S.op("dve", lambda e, i=i, src=src: e.tensor_copy(
                out=Tis[i][:].rearrange("p g (s t) -> p g s t", s=NS),
                in_=src[:, :, None, 0:4].broadcast_to([128, 8, NS, 4])), reads=[src.b], writes=[Tis[i].b])
        for ri, src in enumerate((Pir, Pii)):
            for gq in range(2):
                bank = 1 + ri * 2 + gq
                tpv = psf(bank).rearrange("p (g i) -> p g i", g=4)
                for gl in range(4):
                    gp = gq * 4 + gl
                    S.op("pe", lambda e, tpv=tpv, gl=gl, gp=gp, src=src: e.transpose(out=tpv[:, gl, :], in_=src[:, gp, :],
                                                                                   identity=ident_f[:, :]),
                         reads=[src.b, cc.b], writes=[pb[bank]])
                S.op("act", lambda e, tpv=tpv, gq=gq, ri=ri: e.activation(out=Tinv[:, gq * 4:(gq + 1) * 4, ri, :], in_=tpv,
                                                                         func=AF.Copy), reads=[pb[bank]], writes=[Tinv.b])
        for ri in range(2):
            for gq in range(2):
                bank = 5 + ri
                tpv = psf(bank).rearrange("p (g i) -> p g i", g=4)
                for gl in range(4):
                    gp = gq * 4 + gl
                    S.op("pe", lambda e, tpv=tpv, gl=gl, gp=gp, ri=ri: e.transpose(out=tpv[0:64, gl, :], in_=Tis[ri][:, gp, :],
                                                                                  identity=ident_f[:, :]),
                         reads=[Tis[ri].b, cc.b], writes=[pb[bank]])
                S.op("act", lambda e, tpv=tpv, gq=gq, ri=ri: e.activation(out=Tinvs[:, gq * 4:(gq + 1) * 4, ri, :],
                                                                         in_=tpv[0:64, :, :], func=AF.Copy),
                     reads=[pb[bank]], writes=[Tinvs.b])
        S.barrier()
        A.release(m2s)
        uTb = A.alloc("uTb", [128, 2, 128], BF16)
        uTfL = [A.alloc("uTf%d" % i, [128, 2, 128], F32) for i in range(2)]
        xpL = [[A.alloc("xp%d_%d" % (i, k), [128, 8, 128], BF16) for k in range(4)] for i in range(2)]
        ab = [[A.alloc("ab%d_%d" % (k, i), [128, 128], F32) for i in range(4)] for k in range(2)]
        hrT = A.alloc("hrT", [128, 8, 128], BF16)
        nhiT = A.alloc("nhiT", [128, 8, 128], BF16)
        ysb = A.alloc("ysb", [128, 2, 128], F32)
        y5 = A.alloc("y5", [128, 2, 128], F32)
        y5b = A.alloc("y5b", [128, 2, 128], BF16)
        sig = A.alloc("sig", [128, 2, 128], F32)
        o5T = A.alloc("o5T", [128, 2, 128], BF16)
        BcF = A.alloc("BcF", [128, 2, 4, 256], BF16)

        class Alias2:
            def __init__(self, base, ap):
                self.b = base.b
                self.ap = ap

            def __getitem__(self, idx):
                return self.ap[idx]

        GH = [Alias2(ab[0][i], ab[0][i][:].rearrange("p (g i) -> p g i", g=2)) for i in range(2)]
        ws = [Alias2(ab[1][i], ab[1][i][:].rearrange("p (g i) -> p g i", g=2)) for i in range(4)]
        dsk = prv("dsk")
        glub = prv("glub")

        S.op("pool", lambda e: e.memset(Hst[:], 0.0), writes=Hb)
        S.op("pool", lambda e: e.memset(BcF[:], 0.0), writes=[BcF.b])
        for gq in range(4):
            S.op("act", lambda e, gq=gq: e.activation(out=BcF[32 * gq:32 * gq + 32, :, gq, :],
                                                      in_=Bc[32 * gq:32 * gq + 32, :, :, :].rearrange("p a c i -> p a (c i)"),
                                                      func=AF.Copy), reads=[Bc.b, BcF.b], writes=[BcF.b])

        def p2_Xa(t):
            tok0, n = tile_info(t)
            sm = 0 if t < NPT else 1
            uTf = uTfL[PAR[t]]
            suT = psf(0)[:, 0:256].rearrange("p (a i) -> p a i", a=2)
            for half in range(2):
                for kc in range(8):
                    S.op("pe", lambda e, half=half, kc=kc: e.matmul(suT[:, half, :n], lhsT=W5[:, kc, half * 128:(half + 1) * 128],
                                                                   rhs=hT[:, kc, tok0:tok0 + n], start=(kc == 0), stop=(kc == 7)),
                         reads=[W5.b, hTb[t]], writes=[pb[0]])
            S.op("act", lambda e: e.activation(out=uTb[:, :, :n], in_=suT[:, :, :n], func=AF.Copy), reads=[pb[0]], writes=[uTb.b])
            S.op("act", lambda e: e.activation(out=uTf[:, :, :n], in_=suT[:, :, :n], func=AF.Copy), reads=[pb[0]], writes=[uTf.b])

        def p2_Xb(t):
            tok0, n = tile_info(t)
            sm = 0 if t < NPT else 1
            xa, xb, xc, xd = xpL[PAR[t]]
            TI = Tinv if sm == 0 else Tinvs
            for hh in range(2):
                X = psf(1, 2).rearrange("p (g c i) -> p g c i", g=4, c=2)
                for nb in range(2):
                    S.op("pe", lambda e, hh=hh, nb=nb: e.matmul(
                        PS[:n, 1 + nb, :], lhsT=uTb[:, hh, :n],
                        rhs=BcF[:, hh, 2 * nb:2 * nb + 2, :].rearrange("p g i -> p (g i)"), start=True, stop=True),
                        reads=[uTb.b, BcF.b], writes=[pb[1 + nb]])
                rdx = [pb[1], pb[2], TI.b]
                gs = slice(hh * 4, hh * 4 + 4)
                S.op("dve", lambda e, X=X, gs=gs: e.tensor_tensor(out=xa[:n, gs, :], in0=X[:n, :, 0, :], in1=TI[:n, gs, 0, :], op=ALU.mult),
                     reads=rdx, writes=[xa.b])
                S.op("dve", lambda e, X=X, gs=gs: e.tensor_tensor(out=xb[:n, gs, :], in0=X[:n, :, 1, :], in1=TI[:n, gs, 1, :], op=ALU.mult),
                     reads=rdx, writes=[xb.b])
                S.op("dve", lambda e, X=X, gs=gs: e.tensor_tensor(out=xc[:n, gs, :], in0=X[:n, :, 0, :], in1=TI[:n, gs, 1, :], op=ALU.mult),
                     reads=rdx, writes=[xc.b])
                S.op("dve", lambda e, X=X, gs=gs: e.tensor_tensor(out=xd[:n, gs, :], in0=X[:n, :, 1, :], in1=TI[:n, gs, 0, :], op=ALU.mult),
                     reads=rdx, writes=[xd.b])

        def p2_cum(t, q):
            tok0, n = tile_info(t)
            sm = 0 if t < NPT else 1
            xa, xb, xc, xd = xpL[PAR[t]]
            bank = 5 + (q % 3)
            G = psf(bank).rearrange("p (g c i) -> p g c i", g=2, c=2)
            for gl in range(2):
                gp = 2 * q + gl
                for ri, (u, v, r2) in enumerate(((xa, xb, ntri_bf), (xc, xd, tri_bf))):
                    S.op("pe", lambda e, gl=gl, gp=gp, ri=ri, u=u: e.matmul(G[:, gl, ri, :n], lhsT=u[:n, gp, :],
                                                                          rhs=tri_bf[:n, sm, :n], start=True, stop=False),
                         reads=[u.b, tri_bf.b], writes=[pb[bank]])
                    S.op("pe", lambda e, gl=gl, gp=gp, ri=ri, v=v, r2=r2: e.matmul(G[:, gl, ri, :n], lhsT=v[:n, gp, :],
                                                                                  rhs=r2[:n, sm, :n], start=False, stop=True),
                         reads=[v.b, r2.b], writes=[pb[bank]])

        def p2_H(t, q):
            tok0, n = tile_info(t)
            sm = 0 if t < NPT else 1
            bank = 5 + (q % 3)
            G = psf(bank).rearrange("p (g c i) -> p g c i", g=2, c=2)
            if sm == 0:
                for gl in range(2):
                    gp = 2 * q + gl
                    a_, b_, c_, d_ = ab[gp % 2]
                    hr_s = Hst[:, 0, gp:gp + 1]
                    hi_s = Hst[:, 1, gp:gp + 1]
                    rd = [pb[bank], Hb[gp], Tfr.b, Tfi.b, nTfi.b]
                    S.op("dve", lambda e, gl=gl, gp=gp, a_=a_, hr_s=hr_s: e.scalar_tensor_tensor(
                        out=a_[:, :], in0=G[:, gl, 0, :], scalar=hr_s, in1=Tfr[:, gp, :], op0=ALU.add, op1=ALU.mult),
                        reads=rd, writes=[a_.b])
                    S.op("dve", lambda e, gl=gl, gp=gp, b_=b_, hi_s=hi_s: e.scalar_tensor_tensor(
                        out=b_[:, :], in0=G[:, gl, 1, :], scalar=hi_s, in1=Tfi[:, gp, :], op0=ALU.add, op1=ALU.mult),
                        reads=rd, writes=[b_.b])
                    S.op("dve", lambda e, gl=gl, gp=gp, c_=c_, hr_s=hr_s: e.scalar_tensor_tensor(
                        out=c_[:, :], in0=G[:, gl, 0, :], scalar=hr_s, in1=nTfi[:, gp, :], op0=ALU.add, op1=ALU.mult),
                        reads=rd, writes=[c_.b])
                    S.op("dve", lambda e, gl=gl, gp=gp, d_=d_, hi_s=hi_s: e.scalar_tensor_tensor(
                        out=d_[:, :], in0=G[:, gl, 1, :], scalar=hi_s, in1=Tfr[:, gp, :], op0=ALU.add, op1=ALU.mult),
                        reads=rd, writes=[d_.b])
                    S.op("pool", lambda e, gp=gp, a_=a_, b_=b_: e.tensor_tensor(out=Hst[:, 0, gp:gp + 1], in0=a_[:, 127:128],
                                                                               in1=b_[:, 127:128], op=ALU.subtract),
                         reads=[a_.b, b_.b, Hb[gp]], writes=[Hb[gp]])
                    S.op("pool", lambda e, gp=gp, c_=c_, d_=d_: e.tensor_tensor(out=Hst[:, 1, gp:gp + 1], in0=d_[:, 127:128],
                                                                               in1=c_[:, 127:128], op=ALU.subtract),
                         reads=[c_.b, d_.b, Hb[gp]], writes=[Hb[gp]])
                    S.op("pool", lambda e, gp=gp, a_=a_, b_=b_: e.tensor_tensor(out=hrT[:, gp, :], in0=a_[:, :], in1=b_[:, :],
                                                                               op=ALU.subtract),
                         reads=[a_.b, b_.b], writes=[hrT.b])
                    S.op("pool", lambda e, gp=gp, c_=c_, d_=d_: e.tensor_tensor(out=nhiT[:, gp, :], in0=c_[:, :], in1=d_[:, :],
                                                                               op=ALU.subtract),
                         reads=[c_.b, d_.b], writes=[nhiT.b])
            else:
                gs = slice(2 * q, 2 * q + 2)
                Gr = G[:, :, 0, 0:64].rearrange("p g (s t) -> p g s t", s=NS)
                Gi = G[:, :, 1, 0:64].rearrange("p g (s t) -> p g s t", s=NS)
                h0r = H0[:, 0, gs, :, None].broadcast_to([128, 2, NS, 4])
                h0i = H0[:, 1, gs, :, None].broadcast_to([128, 2, NS, 4])
                S.op("dve", lambda e: e.tensor_tensor(out=GH[0][:].rearrange("p g (s t) -> p g s t", s=NS),
                                                      in0=Gr, in1=h0r, op=ALU.add),
                     reads=[pb[bank], H0.b], writes=[GH[0].b])
                S.op("dve", lambda e: e.tensor_tensor(out=GH[1][:].rearrange("p g (s t) -> p g s t", s=NS),
                                                      in0=Gi, in1=h0i, op=ALU.add),
                     reads=[pb[bank], H0.b], writes=[GH[1].b])
                rdt = [GH[0].b, GH[1].b, Tfs[0].b, Tfs[1].b, Tfs[2].b]
                S.op("dve", lambda e: e.tensor_tensor(out=ws[0][:], in0=GH[0][:], in1=Tfs[0][:, gs, :], op=ALU.mult),
                     reads=rdt, writes=[ws[0].b])
                S.op("dve", lambda e: e.tensor_tensor(out=ws[1][:], in0=GH[1][:], in1=Tfs[1][:, gs, :], op=ALU.mult),
                     reads=rdt, writes=[ws[1].b])
                S.op("dve", lambda e: e.tensor_tensor(out=ws[2][:], in0=GH[0][:], in1=Tfs[2][:, gs, :], op=ALU.mult),
                     reads=rdt, writes=[ws[2].b])
                S.op("dve", lambda e: e.tensor_tensor(out=ws[3][:], in0=GH[1][:], in1=Tfs[0][:, gs, :], op=ALU.mult),
                     reads=rdt, writes=[ws[3].b])
                S.op("pool", lambda e: e.tensor_tensor(out=hrT[:, gs, 0:64], in0=ws[0][:], in1=ws[1][:], op=ALU.subtract),
                     reads=[ws[0].b, ws[1].b], writes=[hrT.b])
                S.op("pool", lambda e: e.tensor_tensor(out=nhiT[:, gs, 0:64], in0=ws[2][:], in1=ws[3][:], op=ALU.subtract),
                     reads=[ws[2].b, ws[3].b], writes=[nhiT.b])

                def last(w):
                    return w[:].rearrange("p g (s t) -> p g s t", s=NS)[:, :, :, 3]
                S.op("pool", lambda e: e.tensor_tensor(out=Hout[:, 0, gs, :], in0=last(ws[0]), in1=last(ws[1]), op=ALU.subtract),
                     reads=[ws[0].b, ws[1].b], writes=[Hout.b])
                S.op("pool", lambda e: e.tensor_tensor(out=Hout[:, 1, gs, :], in0=last(ws[3]), in1=last(ws[2]), op=ALU.subtract),
                     reads=[ws[2].b, ws[3].b], writes=[Hout.b])

        def p2_Y1(t):
            tok0, n = tile_info(t)
            sm = 0 if t < NPT else 1
            if sm == 0 and t == NPT - 1:
                S.dma("sp", o_s5_p[l].rearrange("c p g -> p c g"), Hst[:], reads=Hb)
            if sm == 1:
                S.dma("sp", o_s5_s[l].rearrange("c p n -> p c n"), Hout[:].rearrange("p c g s -> p c (g s)"), reads=[Hout.b])
            yT = psf(0)[:, 256:512].rearrange("p (a i) -> p a i", a=2)
            for gp in range(8):
                q0 = 32 * (gp % 4)
                S.op("pe", lambda e, gp=gp, q0=q0: e.matmul(yT[q0:q0 + 32, gp // 4, :n], lhsT=Ctb[:, 0, gp, :], rhs=hrT[:, gp, :n],
                                                           start=True, stop=False, tile_position=(0, q0)),
                     reads=[Ctb.b, hrT.b], writes=[pb[0]])
                S.op("pe", lambda e, gp=gp, q0=q0: e.matmul(yT[q0:q0 + 32, gp // 4, :n], lhsT=Ctb[:, 1, gp, :], rhs=nhiT[:, gp, :n],
                                                           start=False, stop=True, tile_position=(0, q0)),
                     reads=[Ctb.b, nhiT.b], writes=[pb[0]])

        def p2_Y2a(t):
            tok0, n = tile_info(t)
            uTf = uTfL[PAR[t]]
            yT = psf(0)[:, 256:512].rearrange("p (a i) -> p a i", a=2)
            for half in range(2):
                S.op("dve", lambda e, half=half: e.scalar_tensor_tensor(out=ysb[:, half, :n], in0=uTf[:, half, :n],
                                                                       scalar=dsk[:, half:half + 1], in1=yT[:, half, :n],
                                                                       op0=ALU.mult, op1=ALU.add),
                     reads=[uTf.b, prm.b, pb[0]], writes=[ysb.b])
            S.op("act", lambda e: e.activation(out=y5[:, :, :n], in_=ysb[:, :, :n], func=AF.Gelu_apprx_tanh), reads=[ysb.b], writes=[y5.b])
            S.op("act", lambda e: e.activation(out=y5b[:, :, :n], in_=y5[:, :, :n], func=AF.Copy), reads=[y5.b], writes=[y5b.b])
            zT = psf(0)[:, 256:512].rearrange("p (a i) -> p a i", a=2)
            for ho in range(2):
                for kc in range(2):
                    S.op("pe", lambda e, ho=ho, kc=kc: e.matmul(zT[:, ho, :n], lhsT=gluw[:, kc, ho * 128:(ho + 1) * 128],
                                                               rhs=y5b[:, kc, :n], start=(kc == 0), stop=(kc == 1)),
                         reads=[gluw.b, y5b.b], writes=[pb[0]])
            for ho in range(2):
                S.op("act", lambda e, ho=ho: e.activation(out=sig[:, ho, :n], in_=zT[:, ho, :n], func=AF.Sigmoid,
                                                         bias=glub[:, ho:ho + 1], scale=1.0),
                     reads=[pb[0], prm.b], writes=[sig.b])

        def p2_Y2b(t):
            tok0, n = tile_info(t)
            S.op("dve", lambda e: e.tensor_tensor(out=o5T[:, :, :n], in0=y5[:, :, :n], in1=sig[:, :, :n], op=ALU.mult),
                 reads=[y5.b, sig.b], writes=[o5T.b])
            for bank in range(2):
                for a in range(2):
                    S.op("pe", lambda e, bank=bank, a=a: e.matmul(PS[:n, 3 + bank, :], lhsT=o5T[:, a, :n],
                                                                 rhs=Wo5[:, a, bank * 512:(bank + 1) * 512],
                                                                 start=(a == 0), stop=(a == 1)),
                         reads=[o5T.b, Wo5.b], writes=[pb[3 + bank]])

        def p2_Y2c(t):
            tok0, n = tile_info(t)
            x = xt[t]
            S.op("dve", lambda e: e.tensor_tensor(out=x[:n, :], in0=psf(3, 2)[:n, :], in1=x[:n, :], op=ALU.add),
                 reads=[pb[3], pb[4], x.b], writes=[x.b])

        p2_Xa(ORD[0])
        p2_Xb(ORD[0])
        for q in range(3):
            p2_cum(ORD[0], q)
        for i, t in enumerate(ORD):
            nx = ORD[i + 1] if i + 1 < NT else None
            pv_ = ORD[i - 1] if i > 0 else None
            p2_H(t, 0)
            p2_cum(t, 3)
            if pv_ is not None:
                p2_Y2a(pv_)
            p2_H(t, 1)
            if nx is not None:
                p2_Xa(nx)
            p2_H(t, 2)
            if pv_ is not None:
                p2_Y2b(pv_)
            p2_H(t, 3)
            if pv_ is not None:
                p2_Y2c(pv_)
            p2_Y1(t)
            if nx is not None:
                p2_Xb(nx)
                for q in range(3):
                    p2_cum(nx, q)
        p2_Y2a(ORD[-1])
        p2_Y2b(ORD[-1])
        p2_Y2c(ORD[-1])
        S.barrier()
        A.release(m2)
        if stop_after == "P2":
            return True

        m3 = A.mark()
        gB = A.alloc("gB", [128, D], F32)
        S.dma("sp", gB[:], gvecs[2 * l + 1, :, :], writes=[gB.b])
        Wg = chunked(A.alloc("Wg", [128, 8, 1168], BF16), 8)
        WoG = chunked(A.alloc("WoG", [128, 3, D], BF16), 3)
        S0g = A.alloc("S0g", [48, NS, 4, 96], F32)
        glg = A.alloc("glg", [128, 96], F32)
        glrT = A.alloc("glrT", [32, 128], F32)
        lg = A.alloc("lg", [128, 192], F32)
        ex = [A.alloc("ex%d" % i, [128, 192], F32) for i in range(3)]
        qin = A.alloc("qin", [128, 192], BF16)
        kin = A.alloc("kin", [128, 192], BF16)
        ken = A.alloc("ken", [128, 192], BF16)
        vg = A.alloc("vg", [128, 4, 96], BF16)
        sgg = A.alloc("sgg", [128, 384], F32)
        qkT = A.alloc("qkT", [48, 8, 128], BF16)
        qinL = [qin, A.alloc("qin1", [128, 192], BF16)]
        kinL = [kin, A.alloc("kin1", [128, 192], BF16)]
        kenL = [ken, A.alloc("ken1", [128, 192], BF16)]
        vgL = [vg, A.alloc("vg1", [128, 4, 96], BF16)]
        qTf = A.alloc("qTf", [48, 4, 64], F32)
        scg = A.alloc("scg", [128, 4, 128], BF16)
        Sg = A.alloc("Sg", [48, 4, 96], F32)
        Sgb = A.alloc("Sgb", [48, 4, 96], BF16)
        dec = A.alloc("dec", [48, 4, NS], F32)
        ocg = A.alloc("ocg", [96, 4, 64], F32)
        og1 = A.alloc("og1", [128, 4, 96], F32)
        og2 = A.alloc("og2", [128, 4, 96], F32)
        sgt = A.alloc("sgt", [128, 16], F32)
        mixg = A.alloc("mixg", [128, 384], BF16)
        mixTg = A.alloc("mixTg", [128, 3, 128], BF16)
        kesg = A.alloc("kesg", [64, 192], BF16)
        qkTL = [qkT, A.alloc("qkT1", [48, 8, 128], BF16)]
        scgL = [scg, A.alloc("scg1", [128, 4, 128], BF16)]
        mixgL = [mixg, A.alloc("mixg1", [128, 384], BF16)]
        mixTgL = [mixTg, A.alloc("mixTg1", [128, 3, 128], BF16)]
        gatew = prv("gatew")

        for kc in range(8):
            load_cast(Wg[:, kc, :], w_in[l, kc * 128:(kc + 1) * 128, 1792:2960], Wg.b, kc)
        for kc in range(3):
            load_cast(WoG[:, kc, :], w_out[l, 640 + kc * 128:640 + (kc + 1) * 128, :], WoG.b, kc)
        S.dma("sp", S0g[:].rearrange("p s h v -> p (s h v)"), gla_s0[l, :, :], writes=[S0g.b])
        S.dma("sp", glg[:], glag[l, :, :], writes=[glg.b])
        S.op("pool", lambda e: e.memset(Sg[:], 0.0), writes=[Sg.b])
        S.op("pool", lambda e: e.memset(Sgb[:], 0.0), writes=[Sgb.b])
        S.op("pool", lambda e: e.memset(glrT[:], 1.0), writes=[glrT.b])

        sggL = [sgg, A.alloc("sgg1", [128, 384], F32)]
        decL = [dec, A.alloc("dec1", [48, 4, NS], F32)]
        widths = [512, 512, 144]

        def p3_A(t):
            tok0, n = tile_info(t)
            for bank in range(3):
                for kc in range(8):
                    S.op("pe", lambda e, bank=bank, kc=kc: e.matmul(
                        PS[:n, bank, 0:widths[bank]], lhsT=hT[:, kc, tok0:tok0 + n],
                        rhs=Wg[:, kc, bank * 512:bank * 512 + widths[bank]], start=(kc == 0), stop=(kc == 7)),
                        reads=[hTb[t], Wg.b], writes=[pb[bank]])

        def p3_B1(t):
            tok0, n = tile_info(t)
            sm = 0 if t < NPT else 1
            dec = decL[PAR[t]]
            gT = psf(2)[0:16, 256:384]
            for kc in range(8):
                S.op("pe", lambda e, kc=kc: e.matmul(gT[:, :n], lhsT=Wg[:, kc, 1152:1168], rhs=hT[:, kc, tok0:tok0 + n],
                                                    start=(kc == 0), stop=(kc == 7)), reads=[hTb[t], Wg.b], writes=[pb[2]])
            S.op("dve", lambda e: e.tensor_copy(out=glrT[0:16, :n], in_=gT[:, :n]), reads=[pb[2]], writes=[glrT.b])
            xg = psf(3)[:, 0:192]
            S.op("pe", lambda e: e.matmul(xg[:n, :], lhsT=glrT[0:17, :n], rhs=gatew[0:17, :], start=True, stop=True),
                 reads=[glrT.b, prm.b], writes=[pb[3]])
            S.op("act", lambda e: e.activation(out=ex[0][:n, :], in_=xg[:n, :], func=AF.Exp, scale=-1.0), reads=[pb[3]], writes=[ex[0].b])
            S.op("act", lambda e: e.activation(out=ex[1][:n, :], in_=ex[0][:n, :], func=AF.Ln, bias=1.0, scale=1.0),
                 reads=[ex[0].b], writes=[ex[1].b])
            S.op("dve", lambda e: e.tensor_scalar(out=lg[:n, :], in0=ex[1][:n, :], scalar1=-1.0 / 16, scalar2=None, op0=ALU.mult),
                 reads=[ex[1].b], writes=[lg.b])
            bcu = psf(4)[:, 0:384].rearrange("p (a d) -> p a d", a=2)
            S.op("pe", lambda e: e.matmul(bcu[:n, 0, :], lhsT=tri_f[:n, sm, :n], rhs=lg[:n, :], start=True, stop=True),
                 reads=[cc.b, lg.b], writes=[pb[4]])
            S.op("pe", lambda e: e.matmul(bcu[:n, 1, :], lhsT=upp_f[:n, sm, :n], rhs=lg[:n, :], start=True, stop=True),
                 reads=[cc.b, lg.b], writes=[pb[4]])
            bE = psf(3)[0:48, 192:192 + 64].rearrange("p (h s) -> p h s", h=4)
            ncol = 2 if sm == 0 else NS
            for h in range(4):
                if sm == 0:
                    rhs_ap = cc[:n, CC_OFF["tri"][0] + 126:CC_OFF["tri"][0] + 128]
                else:
                    rhs_ap = seqind[:n, :]
                S.op("pe", lambda e, h=h, rhs_ap=rhs_ap: e.matmul(bE[:, h, 0:ncol], lhsT=lg[:n, h * 48:(h + 1) * 48], rhs=rhs_ap,
                                                                 start=True, stop=True), reads=[lg.b, cc.b], writes=[pb[3]])
            S.op("act", lambda e: e.activation(out=ex[0][:n, :], in_=bcu[:n, 0, :], func=AF.Exp), reads=[pb[4]], writes=[ex[0].b])
            S.op("act", lambda e: e.activation(out=ex[1][:n, :], in_=bcu[:n, 0, :], func=AF.Exp, scale=-1.0), reads=[pb[4]], writes=[ex[1].b])
            S.op("act", lambda e: e.activation(out=ex[2][:n, :], in_=bcu[:n, 1, :], func=AF.Exp), reads=[pb[4]], writes=[ex[2].b])
            S.op("act", lambda e: e.activation(out=dec[:, :, 0:ncol], in_=bE[:, :, 0:ncol], func=AF.Exp), reads=[pb[3]], writes=[dec.b])

        def p3_B2(t):
            tok0, n = tile_info(t)
            sgg_ = sggL[PAR[t]]
            qin, kin, ken, vg = qinL[PAR[t]], kinL[PAR[t]], kenL[PAR[t]], vgL[PAR[t]]
            flat = psf(0, 3)
            S.op("dve", lambda e: e.scalar_tensor_tensor(out=qin[:n, :], in0=flat[:n, 0:192], scalar=48.0 ** -0.5, in1=ex[0][:n, :],
                                                         op0=ALU.mult, op1=ALU.mult), reads=[pb[0], ex[0].b], writes=[qin.b])
            S.op("dve", lambda e: e.tensor_tensor(out=kin[:n, :], in0=flat[:n, 192:384], in1=ex[1][:n, :], op=ALU.mult),
                 reads=[pb[0], ex[1].b], writes=[kin.b])
            S.op("dve", lambda e: e.tensor_tensor(out=ken[:n, :], in0=flat[:n, 192:384], in1=ex[2][:n, :], op=ALU.mult),
                 reads=[pb[0], ex[2].b], writes=[ken.b])
            S.op("act", lambda e: e.activation(out=vg[:n].rearrange("p h v -> p (h v)"), in_=flat[:n, 384:768], func=AF.Copy),
                 reads=[pb[0], pb[1]], writes=[vg.b])
            S.op("act", lambda e: e.activation(out=sgg_[:n, :], in_=flat[:n, 768:1152], func=AF.Exp, scale=-1.0),
                 reads=[pb[1], pb[2]], writes=[sgg_.b])
            S.op("act", lambda e: e.activation(out=sgg_[:n, :], in_=sgg_[:n, :], func=AF.Ln, bias=1.0, scale=1.0),
                 reads=[sgg_.b], writes=[sgg_.b])
            S.op("act", lambda e: e.activation(out=sgg_[:n, :], in_=sgg_[:n, :], func=AF.Exp, scale=-1.0),
                 reads=[sgg_.b], writes=[sgg_.b])
            S.op("dve", lambda e: e.tensor_tensor(out=sgg_[:n, :], in0=flat[:n, 768:1152], in1=sgg_[:n, :], op=ALU.mult),
                 reads=[pb[1], pb[2], sgg_.b], writes=[sgg_.b])
            S.op("pool", lambda e: e.tensor_tensor(out=sgg_[:n, :].rearrange("p (h v) -> p h v", h=4),
                                                   in0=sgg_[:n, :].rearrange("p (h v) -> p h v", h=4),
                                                   in1=glg[:n, None, :].broadcast_to([n, 4, 96]), op=ALU.mult),
                 reads=[sgg_.b, glg.b], writes=[sgg_.b])

        def p3_C(t):
            tok0, n = tile_info(t)
            sm = 0 if t < NPT else 1
            qin, kin, ken, vg = qinL[PAR[t]], kinL[PAR[t]], kenL[PAR[t]], vgL[PAR[t]]
            qkT, scg = qkTL[PAR[t]], scgL[PAR[t]]
            tpg = psbf(5).rearrange("p (a i) -> p a i", a=8)
            for h in range(4):
                S.op("pe", lambda e, h=h: e.transpose(out=tpg[0:48, h, :n], in_=qin[:n, h * 48:(h + 1) * 48], identity=ident_bf[:n, :n]),
                     reads=[qin.b, ident_bf.b], writes=[pb[5]])
                S.op("pe", lambda e, h=h: e.transpose(out=tpg[0:48, 4 + h, :n], in_=kin[:n, h * 48:(h + 1) * 48], identity=ident_bf[:n, :n]),
                     reads=[kin.b, ident_bf.b], writes=[pb[5]])
            S.op("act", lambda e: e.activation(out=qkT[:, :, :n], in_=tpg[0:48, :, :n], func=AF.Copy), reads=[pb[5]], writes=[qkT.b])
            if sm == 1:
                S.op("dve", lambda e: e.tensor_copy(out=qTf[:, :, :n], in_=tpg[0:48, 0:4, :n]), reads=[pb[5]], writes=[qTf.b])
            scp = psf(6).rearrange("p (h i) -> p h i", h=4)
            for h in range(4):
                S.op("pe", lambda e, h=h: e.matmul(scp[:n, h, :n], lhsT=qkT[0:48, 4 + h, :n], rhs=qkT[0:48, h, :n], start=True, stop=True),
                     reads=[qkT.b], writes=[pb[6]])
            S.op("dve", lambda e: e.tensor_tensor(out=scg[:n, :, :n], in0=scp[:n, :, :n],
                                                  in1=tri_f[:n, sm:sm + 1, :n].broadcast_to([n, 4, n]), op=ALU.mult),
                 reads=[pb[6], cc.b], writes=[scg.b])
            Og = psf(7)[:, 0:384].rearrange("p (h v) -> p h v", h=4)
            if sm == 1:
                OCT = psf(4)[:, 0:256].rearrange("p (h i) -> p h i", h=4)
                for s in range(NS):
                    for h in range(4):
                        S.op("pe", lambda e, s=s, h=h: e.matmul(OCT[0:96, h, 4 * s:4 * s + 4], lhsT=S0g[0:48, s, h, :],
                                                               rhs=qTf[0:48, h, 4 * s:4 * s + 4], start=True, stop=True),
                             reads=[S0g.b, qTf.b], writes=[pb[4]])
                S.op("act", lambda e: e.activation(out=ocg[:], in_=OCT[0:96, :, :], func=AF.Copy), reads=[pb[4]], writes=[ocg.b])
            for h in range(4):
                S.op("pe", lambda e, h=h: e.matmul(Og[:n, h, :], lhsT=scg[:n, h, :n], rhs=vg[:n, h, :], start=True, stop=False),
                     reads=[scg.b, vg.b], writes=[pb[7]])
                if sm == 0:
                    S.op("pe", lambda e, h=h: e.matmul(Og[:n, h, :], lhsT=qkT[0:48, h, :n], rhs=Sgb[0:48, h, :], start=False, stop=True),
                         reads=[qkT.b, Sgb.b], writes=[pb[7]])
                else:
                    S.op("pe", lambda e, h=h: e.matmul(Og[:n, h, :], lhsT=ocg[0:96, h, 0:64], rhs=ident_f[0:96, 0:96], start=False, stop=True),
                         reads=[ocg.b, cc.b], writes=[pb[7]])

        def p3_L(t):
            tok0, n = tile_info(t)
            sgg_ = sggL[PAR[t]]
            mixg = mixgL[PAR[t]]
            Og = psf(7)[:, 0:384].rearrange("p (h v) -> p h v", h=4)
            Ogf = psf(7)[:n, 0:384]
            S.op("act", lambda e: e.activation(out=og1[:n].rearrange("p h v -> p (h v)"), in_=Ogf, func=AF.Square), reads=[pb[7]], writes=[og1.b])
            S.op("dve", lambda e: e.tensor_tensor(out=og2[:n].rearrange("p h v -> p (h v)"), in0=Ogf, in1=sgg_[:n, :], op=ALU.mult),
                 reads=[pb[7], sgg_.b], writes=[og2.b])
            S.op("dve", lambda e: e.tensor_reduce(out=sgt[:n, 0:4], in_=og1[:n], axis=AX.X, op=ALU.add), reads=[og1.b], writes=[sgt.b])
            S.op("dve", lambda e: e.tensor_scalar(out=sgt[:n, 4:8], in0=sgt[:n, 0:4], scalar1=1.0 / 96, scalar2=EPS, op0=ALU.mult, op1=ALU.add),
                 reads=[sgt.b], writes=[sgt.b])
            S.op("pool", lambda e: e.tensor_tensor(out=sgt[:n, 8:12], in0=sgt[:n, 4:8], in1=mhalf[:n, 0:4], op=ALU.pow),
                 reads=[sgt.b, mhalf.b], writes=[sgt.b])
            S.op("dve", lambda e: e.tensor_tensor(out=mixg[:n, :].rearrange("p (h v) -> p h v", h=4), in0=og2[:n],
                                                  in1=sgt[:n, 8:12, None].broadcast_to([n, 4, 96]), op=ALU.mult),
                 reads=[og2.b, sgt.b], writes=[mixg.b])

        def p3_G(t):
            tok0, n = tile_info(t)
            x = xt[t]
            mixg, mixTg = mixgL[PAR[t]], mixTgL[PAR[t]]
            mp = psbf(3)[:, 512:896].rearrange("p (a i) -> p a i", a=3)
            for a in range(3):
                S.op("pe", lambda e, a=a: e.transpose(out=mp[:, a, :n], in_=mixg[:n, a * 128:(a + 1) * 128], identity=ident_bf[:n, :n]),
                     reads=[mixg.b, ident_bf.b], writes=[pb[3]])
            S.op("act", lambda e: e.activation(out=mixTg[:, :, :n], in_=mp[:, :, :n], func=AF.Copy), reads=[pb[3]], writes=[mixTg.b])
            for bank in range(2):
                for a in range(3):
                    S.op("pe", lambda e, bank=bank, a=a: e.matmul(PS[:n, 5 + bank, :], lhsT=mixTg[:, a, :n],
                                                                 rhs=WoG[:, a, bank * 512:(bank + 1) * 512], start=(a == 0), stop=(a == 2)),
                         reads=[mixTg.b, WoG.b], writes=[pb[5 + bank]])
            S.op("dve", lambda e: e.tensor_tensor(out=x[:n, :], in0=psf(5, 2)[:n, :], in1=x[:n, :], op=ALU.add),
                 reads=[pb[5], pb[6], x.b], writes=[x.b])

        def p3_H(t):
            tok0, n = tile_info(t)
            sm = 0 if t < NPT else 1
            dec = decL[PAR[t]]
            qin, kin, ken, vg = qinL[PAR[t]], kinL[PAR[t]], kenL[PAR[t]], vgL[PAR[t]]
            if sm == 0:
                KVg = psf(4)[0:48, 0:384].rearrange("p (h v) -> p h v", h=4)
                for h in range(4):
                    S.op("pe", lambda e, h=h: e.matmul(KVg[:, h, :], lhsT=ken[:n, h * 48:(h + 1) * 48], rhs=vg[:n, h, :], start=True, stop=True),
                         reads=[ken.b, vg.b], writes=[pb[4]])
                for h in range(4):
                    S.op("dve", lambda e, h=h: e.scalar_tensor_tensor(out=Sg[:, h, :], in0=Sg[:, h, :], scalar=dec[:, h, 1:2],
                                                                     in1=KVg[:, h, :], op0=ALU.mult, op1=ALU.add),
                         reads=[Sg.b, dec.b, pb[4]], writes=[Sg.b])
                S.op("act", lambda e: e.activation(out=Sgb[:], in_=Sg[:], func=AF.Copy), reads=[Sg.b], writes=[Sgb.b])
                if t == NPT - 1:
                    S.dma("sp", o_gla_p[l, :, :], Sg[:].rearrange("p h v -> p (h v)"), reads=[Sg.b])
            else:
                for s in range(NS):
                    bank = 4 + (s % 2)
                    KVg = psf(bank)[0:48, 0:384].rearrange("p (h v) -> p h v", h=4)
                    S.op("dve", lambda e, s=s: e.tensor_scalar(out=kesg[:, :], in0=ken[:64, :], scalar1=seqind[:64, s:s + 1], scalar2=None,
                                                               op0=ALU.mult), reads=[ken.b, cc.b], writes=[kesg.b])
                    for h in range(4):
                        S.op("pe", lambda e, h=h, KVg=KVg: e.matmul(KVg[:, h, :], lhsT=kesg[:64, h * 48:(h + 1) * 48], rhs=vg[:64, h, :],
                                                                   start=True, stop=True), reads=[kesg.b, vg.b], writes=[pb[bank]])
                    for h in range(4):
                        S.op("dve", lambda e, s=s, h=h, KVg=KVg: e.scalar_tensor_tensor(
                            out=S0g[:, s, h, :], in0=S0g[:, s, h, :], scalar=dec[:, h, s:s + 1], in1=KVg[:, h, :],
                            op0=ALU.mult, op1=ALU.add), reads=[S0g.b, dec.b, pb[bank]], writes=[S0g.b])
                S.dma("sp", o_gla_s[l, :, :], S0g[:].rearrange("p s h v -> p (s h v)"), reads=[S0g.b])

        p3_B1(ORD[0])
        p3_A(ORD[0])
        p3_B2(ORD[0])
        p3_B1(ORD[1])
        for i, t in enumerate(ORD):
            nx = ORD[i + 1] if i + 1 < NT else None
            nx2 = ORD[i + 2] if i + 2 < NT else None
            p3_C(t)
            if i > 0:
                norm_tile(ORD[i - 1], gB, bank=5)
            p3_H(t)
            if nx is not None:
                p3_A(nx)
            p3_L(t)
            if nx is not None:
                p3_B2(nx)
            if nx2 is not None:
                p3_B1(nx2)
            p3_G(t)
        norm_tile(ORD[-1], gB, bank=5)
        S.barrier()
        A.release(m3)
        if stop_after == "P3":
            return True

        m5 = A.mark()
        ring = [(chunked(A.alloc("Wa%d" % i, [128, 8, 512], BF16), 8), chunked(A.alloc("Wgt%d" % i, [128, 8, 512], BF16), 8),
                 chunked(A.alloc("Wo%d" % i, [128, 4, D], BF16), 4)) for i in range(2)]
        asb = [A.alloc("asb%d" % i, [128, 514], F32) for i in range(2)]
        cv = [A.alloc("cv%d" % i, [128, 512], F32) for i in range(2)]
        ge = A.alloc("ge", [128, 512], F32)
        actT = [A.alloc("actT%d" % i, [128, 4, 512], BF16) for i in range(2)]
        carry = A.alloc("carry", [128, NFC, 2], F32)
        a_s = A.alloc("a_s", [128, NS, 6], F32)
        cvs = [A.alloc("cvs%d" % i, [128, NS, 4], F32) for i in range(2)]
        cn = A.alloc("cn", [128, NFC, 34], F32)
        cs0 = A.alloc("cs0", [128, NFC, NS, 2], F32)
        cw = prv("cw").rearrange("p (c k) -> p c k", c=NFC)
        cb = prv("cb")
        S.dma("sp", cs0[:].rearrange("p c s j -> p (c s j)"), conv_s0[l, :, :], writes=[cs0.b])
        S.op("pool", lambda e: e.memset(carry[:], 0.0), writes=[carry.b])
        it_box = [0]

        def ffn_chunk(cl, c0, G, tg, tk0, nt_, tiles, AT, Wa, Wgt):
            c = c0 + cl
            ba = (cl % 2) * 2
            aps = PS[:, ba, :]
            gps = PS[:, ba + 1, :]
            hrd = [hTb[tt] for tt in tiles]
            for kc in range(8):
                S.op("pe", lambda e, kc=kc: e.matmul(aps[:, :nt_], lhsT=Wa[:, kc, cl * 128:(cl + 1) * 128],
                                                    rhs=hT[:, kc, tk0:tk0 + nt_], start=(kc == 0), stop=(kc == 7)),
                     reads=[Wa.b] + hrd, writes=[pb[ba]])
            for kc in range(8):
                S.op("pe", lambda e, kc=kc: e.matmul(gps[:, :nt_], lhsT=Wgt[:, kc, cl * 128:(cl + 1) * 128],
                                                    rhs=hT[:, kc, tk0:tk0 + nt_], start=(kc == 0), stop=(kc == 7)),
                     reads=[Wgt.b] + hrd, writes=[pb[ba + 1]])
            w0 = cw[:, c, 0:1]
            w1 = cw[:, c, 1:2]
            w2 = cw[:, c, 2:3]
            bb = cb[:, c:c + 1]
            if tg < 4:
                a_t = asb[cl % 2]
                c_t = cv[cl % 2]
                S.op("act", lambda e: e.activation(out=a_t[:, 0:2], in_=carry[:, c, :], func=AF.Copy), reads=[carry.b], writes=[a_t.b])
                S.op("act", lambda e: e.activation(out=a_t[:, 2:514], in_=aps[:, :], func=AF.Copy), reads=[pb[ba]], writes=[a_t.b])
                S.op("act", lambda e: e.activation(out=carry[:, c, :], in_=a_t[:, 512:514], func=AF.Copy), reads=[a_t.b], writes=[carry.b])
                if tg == 3:
                    S.op("act", lambda e: e.activation(out=cn[:, c, 32:34], in_=a_t[:, 512:514], func=AF.Copy), reads=[a_t.b], writes=[cn.b])
                S.op("dve", lambda e: e.tensor_scalar(out=c_t[:, :], in0=a_t[:, 2:514], scalar1=w2, scalar2=bb, op0=ALU.mult, op1=ALU.add),
                     reads=[a_t.b, prm.b], writes=[c_t.b])
                S.op("dve", lambda e: e.scalar_tensor_tensor(out=c_t[:, :], in0=a_t[:, 1:513], scalar=w1, in1=c_t[:, :],
                                                             op0=ALU.mult, op1=ALU.add), reads=[a_t.b, prm.b, c_t.b], writes=[c_t.b], safe=True)
                S.op("dve", lambda e: e.scalar_tensor_tensor(out=c_t[:, :], in0=a_t[:, 0:512], scalar=w0, in1=c_t[:, :],
                                                             op0=ALU.mult, op1=ALU.add), reads=[a_t.b, prm.b, c_t.b], writes=[c_t.b], safe=True)
                S.op("act", lambda e: e.activation(out=ge[:, :], in_=c_t[:, :], func=AF.Gelu_apprx_tanh), reads=[c_t.b], writes=[ge.b])
                S.op("dve", lambda e: e.tensor_tensor(out=AT[:, cl, :], in0=gps[:, :], in1=ge[:, :], op=ALU.mult),
                     reads=[pb[ba + 1], ge.b], writes=[AT.b])
            else:
                c_t = cvs[cl % 2]
                S.op("act", lambda e: e.activation(out=a_s[:, :, 0:2], in_=cs0[:, c, :, :], func=AF.Copy), reads=[cs0.b], writes=[a_s.b])
                S.op("act", lambda e: e.activation(out=a_s[:, :, 2:6], in_=aps[:, 0:64].rearrange("p (s t) -> p s t", s=NS),
                                                   func=AF.Copy), reads=[pb[ba]], writes=[a_s.b])
                S.op("act", lambda e: e.activation(out=cn[:, c, 0:32].rearrange("p (s j) -> p s j", s=NS), in_=a_s[:, :, 4:6], func=AF.Copy),
                     reads=[a_s.b], writes=[cn.b])
                S.op("dve", lambda e: e.tensor_scalar(out=c_t[:], in0=a_s[:, :, 2:6], scalar1=w2, scalar2=bb, op0=ALU.mult, op1=ALU.add),
                     reads=[a_s.b, prm.b], writes=[c_t.b])
                S.op("dve", lambda e: e.scalar_tensor_tensor(out=c_t[:], in0=a_s[:, :, 1:5], scalar=w1, in1=c_t[:],
                                                             op0=ALU.mult, op1=ALU.add), reads=[a_s.b, prm.b, c_t.b], writes=[c_t.b])
                S.op("dve", lambda e: e.scalar_tensor_tensor(out=c_t[:], in0=a_s[:, :, 0:4], scalar=w0, in1=c_t[:],
                                                             op0=ALU.mult, op1=ALU.add), reads=[a_s.b, prm.b, c_t.b], writes=[c_t.b])
                S.op("act", lambda e: e.activation(out=ge[:, 0:64], in_=c_t[:].rearrange("p s t -> p (s t)"),
                                                   func=AF.Gelu_apprx_tanh), reads=[c_t.b], writes=[ge.b])
                S.op("dve", lambda e: e.tensor_tensor(out=AT[:, cl, 0:64], in0=gps[:, 0:64], in1=ge[:, 0:64], op=ALU.mult),
                     reads=[pb[ba + 1], ge.b], writes=[AT.b])

        def ffn_down(ti, tt, G, AT, Wo):
            n = 128 if tt < NPT else 64
            bd = 4 + (ti % 2) * 2
            for bank in range(2):
                for cl in range(G):
                    S.op("pe", lambda e, bank=bank, cl=cl: e.matmul(
                        PS[:n, bd + bank, :], lhsT=AT[:, cl, ti * 128:ti * 128 + n], rhs=Wo[:, cl, bank * 512:(bank + 1) * 512],
                        start=(cl == 0), stop=(cl == G - 1)), reads=[AT.b, Wo.b], writes=[pb[bd + bank]])
            xx = xt[tt]
            S.op("dve", lambda e: e.tensor_tensor(out=xx[:n, :], in0=psf(bd, 2)[:n, :], in1=xx[:n, :], op=ALU.add),
                 reads=[pb[bd], pb[bd + 1], xx.b], writes=[xx.b])

        def ffn_load(gi):
            c0, G = FGROUPS[gi]
            Wa, Wgt, Wo = ring[gi % 2]
            f0 = c0 * 128
            for kc in range(8):
                load_cast(Wa[:, kc, 0:G * 128], ffn_w_in[l, kc * 128:(kc + 1) * 128, f0:f0 + G * 128], Wa.b, kc)
                load_cast(Wgt[:, kc, 0:G * 128], ffn_w_in[l, kc * 128:(kc + 1) * 128, DFF + f0:DFF + f0 + G * 128], Wgt.b, kc)
            for cl in range(G):
                load_cast(Wo[:, cl, :], ffn_w_out[l, f0 + cl * 128:f0 + (cl + 1) * 128, :], Wo.b, cl)

        def ffn_group(gi, c0, G):
            Wa, Wgt, Wo = ring[gi % 2]
            if gi + 1 < len(FGROUPS):
                ffn_load(gi + 1)
            pend = []
            for tg in range(5):
                tk0 = tg * 512
                nt_ = 512 if tg < 4 else 64
                tiles = list(range(tg * 4, tg * 4 + 4)) if tg < 4 else [NPT]
                AT = actT[it_box[0] % 2]
                it_box[0] += 1
                for cl in range(G):
                    ffn_chunk(cl, c0, G, tg, tk0, nt_, tiles, AT, Wa, Wgt)
                    if pend:
                        ffn_down(*pend.pop(0))
                while pend:
                    ffn_down(*pend.pop(0))
                pend = [(ti, tt, G, AT, Wo) for ti, tt in enumerate(tiles)]
            while pend:
                ffn_down(*pend.pop(0))

        ffn_load(0)
        for gi, (c0, G) in enumerate(FGROUPS):
            ffn_group(gi, c0, G)
        S.dma("sp", o_conv[l, :, :], cn[:].rearrange("p c j -> p (c j)"), reads=[cn.b])
        S.barrier()
        A.release(m5)

    for l in range(DEPTH):
        if layer(l):
            break

    if stop_after is None:
        S.dma("sp", gA[:], gvecs[4, :, :], writes=[gA.b])
        m6 = A.mark()
        yo = [A.alloc("yo%d" % i, [128, D], F32) for i in range(2)]
        for t in range(NT):
            tok0, n = tile_info(t)
            x = xt[t]
            yy = yo[t % 2]
            norm_stats(t)
            k = t % 2
            S.op("dve", lambda e, x=x, n=n, yy=yy, k=k: e.scalar_tensor_tensor(out=yy[:n, :], in0=x[:n, :], scalar=rs[:n, 2 * k + 1:2 * k + 2],
                                                                              in1=gA[:n, :], op0=ALU.mult, op1=ALU.mult),
                 reads=[x.b, rsb[k], gA.b], writes=[yy.b])
            dst = y_p[tok0:tok0 + n, :] if t < NPT else y_s[:, :]
            S.dma("sp", dst, yy[:n, :], reads=[yy.b])
        A.release(m6)
    S.finalize()
    return nc, S


_CACHE = {}


def kernel(x_prompt, x_sample, state_ret, state_s5_re, state_s5_im, state_gla, state_ffn_conv,
           norm_mix_g, w_in, ret_norm_g, ret_norm_b,
           s5_lambda_re, s5_lambda_im, s5_log_dt, s5_b_re, s5_b_im, s5_c_re, s5_c_im, s5_d,
           s5_glu_w, s5_glu_b, gla_gate_w, gla_gate_b, gla_norm_g, w_out,
           norm_ffn_g, ffn_w_in, ffn_conv_w, ffn_conv_b, ffn_w_out, norm_final_g, _stop_after=None, _ncores=8, _trace=False):
    f32 = np.float32
    a = lambda v: np.ascontiguousarray(np.asarray(v, dtype=f32))
    x_prompt, x_sample = a(x_prompt), a(x_sample)
    state_ret, state_s5_re, state_s5_im = a(state_ret), a(state_s5_re), a(state_s5_im)
    state_gla, state_ffn_conv = a(state_gla), a(state_ffn_conv)
    w_in, w_out, ffn_w_in, ffn_w_out, s5_glu_w = a(w_in), a(w_out), a(ffn_w_in), a(ffn_w_out), a(s5_glu_w)
    n_cores = 8
    key = _stop_after
    if key not in _CACHE:
        _CACHE[key] = build_nc(_stop_after)
    nc, _ = _CACHE[key]
    cc, cr, rope_t = _const_tables()

    gvecs = np.stack([np.broadcast_to(a(v)[None, :], (128, D)) for v in
                      (norm_mix_g[0], norm_ffn_g[0], norm_mix_g[1], norm_ffn_g[1], norm_final_g)]).astype(f32)
    retgb = np.stack([np.concatenate([np.broadcast_to(a(ret_norm_g[l])[None, :], (128, 384)),
                                      np.broadcast_to(a(ret_norm_b[l])[None, :], (128, 384))], axis=1)
                      for l in range(DEPTH)]).astype(f32)
    glag = np.stack([np.broadcast_to(a(gla_norm_g[l])[None, :], (128, 96)) for l in range(DEPTH)]).astype(f32)
    prm = np.zeros((DEPTH, 128, PR_N), dtype=f32)

    def put(l, name, arr):
        o, w = PR_OFF[name]
        arr = np.asarray(arr, dtype=f32).reshape(arr.shape[0], -1)
        prm[l, :arr.shape[0], o:o + arr.shape[1]] = arr

    for l in range(DEPTH):
        def gp_layout(v):
            return a(v).reshape(8, 2, 64).transpose(1, 2, 0).reshape(128, 8)
        put(l, "lamr", gp_layout(s5_lambda_re[l]))
        put(l, "lami", gp_layout(s5_lambda_im[l]))
        put(l, "ldt", gp_layout(np.broadcast_to(a(s5_log_dt[l])[:, None], (16, 64))))
        for nm, src in (("br", s5_b_re), ("bi", s5_b_im)):
            v = a(src[l]).reshape(8, 2, 64, 16).transpose(1, 2, 0, 3).reshape(128, 8 * 16)
            put(l, nm, v)
        for nm, src in (("ctr", s5_c_re), ("cti", s5_c_im)):
            v = a(src[l]).reshape(8, 2, 16, 64)
            blk = np.zeros((2, 64, 8, 2, 16), dtype=f32)
            for g2 in range(2):
                blk[g2, :, :, g2, :] = v[:, g2, :, :].transpose(2, 0, 1)
            put(l, nm, blk.reshape(128, 8 * 32))
        put(l, "dsk", a(s5_d[l]).reshape(2, 128).T)
        put(l, "glub", a(s5_glu_b[l]).reshape(2, 128).T)
        put(l, "gatew", np.concatenate([a(gla_gate_w[l]), a(gla_gate_b[l])[None, :]], axis=0))
        put(l, "cw", a(ffn_conv_w[l]).reshape(3, NFC, 128).transpose(2, 1, 0).reshape(128, NFC * 3))
        put(l, "cb", a(ffn_conv_b[l]).reshape(NFC, 128).T)

    in_maps = []
    for c in range(n_cores):
        sl = slice(c * NS, (c + 1) * NS)
        rs0 = state_ret[:, sl].reshape(DEPTH, NS, 3, 2, 64, 64).transpose(0, 3, 4, 1, 2, 5).reshape(DEPTH, 128, NS * 192)
        gs0 = state_gla[:, sl].transpose(0, 3, 1, 2, 4).reshape(DEPTH, 48, NS * 384)
        h0 = np.stack([state_s5_re[:, sl], state_s5_im[:, sl]], axis=1)
        h0 = h0.reshape(DEPTH, 2, NS, 8, 2, 64).transpose(0, 1, 4, 5, 3, 2).reshape(DEPTH, 2, 128, 8 * NS)
        cs0 = state_ffn_conv[:, sl].reshape(DEPTH, NS, 2, NFC, 128).transpose(0, 4, 3, 1, 2).reshape(DEPTH, 128, NFC * NS * 2)
        in_maps.append({
            "xp": x_prompt[c], "xs": x_sample[sl].reshape(64, D),
            "w_in": w_in, "w_out": w_out, "ffn_w_in": ffn_w_in, "ffn_w_out": ffn_w_out, "glu_w": s5_glu_w,
            "gvecs": gvecs, "retgb": retgb, "glag": glag, "prm": prm, "cc": cc, "cr": cr, "rope": rope_t,
            "ret_s0": np.ascontiguousarray(rs0), "gla_s0": np.ascontiguousarray(gs0),
            "s5_h0": np.ascontiguousarray(h0), "conv_s0": np.ascontiguousarray(cs0),
        })
    if _trace:
        res = run_bass_kernel_spmd(nc, in_maps[:_ncores], core_ids=list(range(_ncores)), trace=True)
        print("EXEC_TIME_NS", res.exec_time_ns)
    else:
        res = run_bass_kernel_spmd(nc, in_maps[:_ncores], core_ids=list(range(_ncores)))
    R = list(res.results)
    while len(R) < n_cores:
        R.append(R[0])
    y_prompt = np.stack([R[c]["y_p"] for c in range(n_cores)]).astype(f32)
    y_sample = np.concatenate([R[c]["y_s"].reshape(NS, LS, D) for c in range(n_cores)], axis=0).astype(f32)
    ret_p = np.stack([R[c]["o_ret_p"].reshape(DEPTH, 2, 64, 3, 64).transpose(0, 3, 1, 2, 4).reshape(DEPTH, 6, 64, 64)
                      for c in range(n_cores)], axis=1)
    ret_s = np.concatenate([R[c]["o_ret_s"].reshape(DEPTH, 2, 64, NS, 3, 64).transpose(0, 3, 4, 1, 2, 5).reshape(DEPTH, NS, 6, 64, 64)
                            for c in range(n_cores)], axis=1)
    s5p = np.stack([R[c]["o_s5_p"].reshape(DEPTH, 2, 2, 64, 8).transpose(0, 1, 4, 2, 3).reshape(DEPTH, 2, 16, 64)
                    for c in range(n_cores)], axis=2)
    s5s = np.concatenate([R[c]["o_s5_s"].reshape(DEPTH, 2, 2, 64, 8, NS).transpose(0, 1, 5, 4, 2, 3).reshape(DEPTH, 2, NS, 16, 64)
                          for c in range(n_cores)], axis=2)
    gla_p = np.stack([R[c]["o_gla_p"].reshape(DEPTH, 48, 4, 96).transpose(0, 2, 1, 3) for c in range(n_cores)], axis=1)
    gla_s = np.concatenate([R[c]["o_gla_s"].reshape(DEPTH, 48, NS, 4, 96).transpose(0, 2, 3, 1, 4) for c in range(n_cores)], axis=1)
    cv = [R[c]["o_conv"].reshape(DEPTH, 128, NFC, 34) for c in range(n_cores)]
    conv_p = np.stack([v[:, :, :, 32:34].transpose(0, 3, 2, 1).reshape(DEPTH, 2, DFF) for v in cv], axis=1)
    conv_s = np.concatenate([v[:, :, :, 0:32].reshape(DEPTH, 128, NFC, NS, 2).transpose(0, 3, 4, 2, 1).reshape(DEPTH, NS, 2, DFF)
                             for v in cv], axis=1)
    c_ = np.ascontiguousarray
    return (c_(y_prompt), c_(y_sample), c_(ret_p.astype(f32)), c_(ret_s.astype(f32)),
            c_(s5p[:, 0].astype(f32)), c_(s5s[:, 0].astype(f32)), c_(s5p[:, 1].astype(f32)), c_(s5s[:, 1].astype(f32)),
            c_(gla_p.astype(f32)), c_(gla_s.astype(f32)), c_(conv_p.astype(f32)), c_(conv_s.astype(f32)))
```

```python
import math
import numpy as np
import concourse.bass as bass
import concourse.mybir as mybir
from concourse.bass_utils import run_bass_kernel_spmd

F32 = mybir.dt.float32
BF16 = mybir.dt.bfloat16
AF = mybir.ActivationFunctionType
ALU = mybir.AluOpType
AX = mybir.AxisListType

D = 1024
SEQ = 2048
NPT = 16
NT = 17
NTOK = 2112
NS = 16
LS = 4
DEPTH = 2
PAST = 16384
INC = 2960
DFF = 2816
NFC = 22
EPS = 1e-6
GAM = [1.0 - 2.0 ** (-5.0 - h) for h in range(6)]
FGROUPS = [(0, 4), (4, 4), (8, 4), (12, 4), (16, 3), (19, 3)]
HSLOT = [0, 3, 1, 4, 2, 5]

ENGS = ("pe", "dve", "act", "pool", "sp")
import os
REORDER = os.environ.get('K_REORDER', '1') == '1'
WINDOW = int(os.environ.get('K_WINDOW', '1500'))
LAT = float(os.environ.get('K_LAT', '0.8'))
CPW = float(os.environ.get('K_CPW', '0.05'))
STRICT = os.environ.get('K_STRICT', '1') == '1'


class Buf:
    __slots__ = ("name", "excl", "members")

    def __init__(self, name, excl=False, members=None):
        self.name = name
        self.excl = excl
        self.members = members


def _expand(bufs):
    out = []
    for b in bufs:
        if b.members:
            out.extend(b.members)
        else:
            out.append(b)
    return out


class _Probe:
    def __getattr__(self, name):
        def f(*a, **k):
            self.__dict__["call"] = (name, a, k)
            return self
        return f


def _free_elems(ap):
    n = 1
    for s in list(ap.shape)[1:]:
        n *= int(s)
    return n


def _estimate_cost(eng, fn):
    try:
        p = _Probe()
        fn(p)
        name, a, k = p.__dict__["call"]
        out = k.get("out", a[0] if a else None)
        if eng == "pe":
            if name == "transpose":
                return 0.11
            rhs = k.get("rhs")
            n = _free_elems(rhs) if rhs is not None else 128
            f32 = rhs is not None and rhs.dtype == F32
            return n / 2400.0 * (4.0 if f32 else 1.0) + 0.02
        sz = _free_elems(out) if out is not None else 128
        if eng == "dve":
            return 1.1e-3 * sz + 0.07
        if eng == "act":
            return 1.0e-3 * sz + 0.08
        if eng == "pool":
            if k.get("op", None) == ALU.pow:
                return 0.7
            return 2.4e-3 * sz + 0.06
    except Exception:
        pass
    return {"pe": 0.15, "dve": 0.4, "act": 0.45, "pool": 0.7}.get(eng, 0.5)


class Op:
    __slots__ = ("eng", "fn", "reads", "writes", "is_dma", "deps", "raw", "needs_inc", "cnt", "dsem", "dval", "safe", "cost",
                 "tw", "ar")

    def __init__(self, eng, fn, reads, writes, is_dma, safe=False):
        self.safe = safe
        self.cost = 0.5
        self.tw = frozenset(writes)
        self.ar = frozenset(reads)
        self.eng = eng
        self.fn = fn
        self.reads = reads
        self.writes = writes
        self.is_dma = is_dma
        self.deps = None
        self.raw = ()
        self.needs_inc = False
        self.cnt = 0
        self.dsem = None
        self.dval = 0


class Sched:
    def __init__(self, nc, n_dma_sems=28, same_engine_sync=False):
        self.nc = nc
        self.ops = []
        self.n_dma_sems = n_dma_sems
        self.same_engine_sync = same_engine_sync
        self.barriers = []
        self.do_reorder = False
        self.window = 600

    def op(self, eng, fn, reads=(), writes=(), safe=False):
        reads = _expand(reads)
        writes = _expand(writes)
        rd = tuple(b for b in reads if not b.excl)
        wr = tuple(writes) + tuple(b for b in reads if b.excl)
        o = Op(eng, fn, rd, wr, False, safe)
        o.tw = frozenset(writes)
        o.ar = frozenset(reads)
        o.cost = _estimate_cost(eng, fn)
        self.ops.append(o)

    def dma(self, eng, out_ap, in_ap, reads=(), writes=(), **kw):
        def fn(e, out_ap=out_ap, in_ap=in_ap, kw=kw):
            return e.dma_start(out=out_ap, in_=in_ap, **kw)
        o = Op(eng, fn, tuple(_expand(reads)), tuple(_expand(writes)), True)
        try:
            nb = _free_elems(out_ap) * (4 if out_ap.dtype == F32 else 2) * int(out_ap.shape[0])
        except Exception:
            nb = 1 << 20
        o.cost = 2.0 + nb / 1.5e5
        self.ops.append(o)

    def barrier(self):
        self.barriers.append(len(self.ops))

    def _conflict_deps(self, ops):
        last_writer = {}
        readers = {}
        deps = []
        for i, op in enumerate(ops):
            d = set()
            for b in op.reads:
                j = last_writer.get(b)
                if j is not None:
                    d.add(j)
            for b in op.writes:
                j = last_writer.get(b)
                if j is not None:
                    d.add(j)
                d.update(readers.get(b, ()))
            for b in op.reads:
                readers.setdefault(b, []).append(i)
            for b in op.writes:
                last_writer[b] = i
                readers[b] = []
            d.discard(i)
            deps.append(d)
        return deps

    def _list_schedule(self, ops, window=600, lat=LAT):
        n = len(ops)
        deps = self._conflict_deps(ops)
        users = [[] for _ in range(n)]
        ndep = [0] * n
        for i, d in enumerate(deps):
            ndep[i] = len(d)
            for j in d:
                users[j].append(i)
        cp = [0.0] * n
        for i in range(n - 1, -1, -1):
            m = 0.0
            for u in users[i]:
                v = cp[u] + lat
                if v > m:
                    m = v
            cp[i] = ops[i].cost + m
        finish = [0.0] * n
        eng_free = {}
        done = [False] * n
        order = []
        head = 0
        ready = set(i for i in range(min(n, window)) if ndep[i] == 0)
        hi = min(n, window)
        while len(order) < n:
            best = None
            bt = None
            for i in ready:
                op = ops[i]
                t0 = eng_free.get(op.eng, 0.0)
                for j in deps[i]:
                    fj = finish[j] + (lat if ops[j].eng != op.eng or ops[j].is_dma else 0.0)
                    if fj > t0:
                        t0 = fj
                key = (t0 - CPW * cp[i], i)
                if bt is None or key < bt:
                    bt = key
                    best = i
                    bt0 = t0
            if best is None:
                raise RuntimeError("list scheduler stuck")
            i = best
            ready.discard(i)
            op = ops[i]
            t0 = bt0
            if op.is_dma:
                eng_free[op.eng] = t0 + 0.6
                finish[i] = t0 + op.cost
            else:
                finish[i] = t0 + op.cost
                eng_free[op.eng] = finish[i]
            done[i] = True
            order.append(i)
            for u in users[i]:
                ndep[u] -= 1
                if ndep[u] == 0 and u < hi:
                    ready.add(u)
            while head < n and done[head]:
                head += 1
            nhi = min(n, head + window)
            for u in range(hi, nhi):
                if ndep[u] == 0:
                    ready.add(u)
            hi = max(hi, nhi)
        return [ops[i] for i in order]

    def reorder(self, window=600):
        bounds = sorted(set(self.barriers))
        segs = []
        prev = 0
        for b in bounds + [len(self.ops)]:
            segs.append(self.ops[prev:b])
            prev = b
        new_ops = []
        new_barriers = []
        for k, seg in enumerate(segs):
            if k > 0:
                new_barriers.append(len(new_ops))
            new_ops.extend(self._list_schedule(seg, window) if seg else [])
        self.ops = new_ops
        self.barriers = new_barriers

    def finalize(self):
        nc = self.nc
        if self.do_reorder:
            self.reorder(self.window)
        ops = self.ops
        engobj = {"pe": nc.tensor, "dve": nc.vector, "act": nc.scalar, "pool": nc.gpsimd, "sp": nc.sync}
        barrier_set = set(self.barriers)
        last_writer = {}
        readers = {}
        last_on_eng = {}
        open_dmas = []
        pending_barrier = {}
        dma_sem_last = [None] * self.n_dma_sems
        n_sw = self.n_dma_sems // 2
        pools = {"pool": list(range(0, n_sw)), "sp": list(range(n_sw, self.n_dma_sems)),
                 "act": list(range(n_sw, self.n_dma_sems))}
        dma_rr = {"pool": 0, "sp": 0, "act": 0}
        for i, op in enumerate(ops):
            deps = set()
            if i in barrier_set:
                bd = set(last_on_eng.values())
                for j in dma_sem_last:
                    if j is not None:
                        bd.add(j)
                last_writer = {}
                readers = {}
                for e in ENGS:
                    pending_barrier[e] = bd
            pb = pending_barrier.pop(op.eng, None)
            if pb is not None:
                deps.update(pb)
            raw = set()
            for b in op.reads:
                j = last_writer.get(b)
                if j is not None:
                    deps.add(j)
                    raw.add(j)
            op.raw = raw
            for b in op.writes:
                j = last_writer.get(b)
                if j is not None:
                    deps.add(j)
                r = readers.get(b)
                if r:
                    deps.update(r)
            if op.is_dma:
                pl = pools[op.eng]
                s = pl[dma_rr[op.eng] % len(pl)]
                dma_rr[op.eng] += 1
                op.dsem = s
                prev = dma_sem_last[s]
                if prev is not None:
                    deps.add(prev)
                    op.dval = ops[prev].dval + 16
                else:
                    op.dval = 16
                dma_sem_last[s] = i
                open_dmas.append(i)
            for b in op.reads:
                r = readers.setdefault(b, [])
                if not op.is_dma:
                    r[:] = [j for j in r if ops[j].is_dma or ops[j].eng != op.eng]
                r.append(i)
            for b in op.writes:
                last_writer[b] = i
                readers[b] = []
            deps.discard(i)
            op.deps = deps
            if not op.is_dma:
                last_on_eng[op.eng] = i

        def skip_same(d, op, j):
            if d.is_dma or op.is_dma or d.eng != op.eng:
                return False
            if d.eng == "pe":
                return True
            if STRICT:
                return not ((d.tw & op.ar) or (d.tw & op.tw) or (d.ar & op.tw))
            return j not in op.raw

        for op in ops:
            for j in op.deps:
                d = ops[j]
                if d.is_dma or skip_same(d, op, j):
                    continue
                d.needs_inc = True
        cnt = {e: 0 for e in ENGS}
        for op in ops:
            if op.is_dma:
                continue
            if op.needs_inc:
                cnt[op.eng] += 1
            op.cnt = cnt[op.eng]
        esem = {e: nc.alloc_semaphore("s_" + e) for e in ENGS}
        dsem = [nc.alloc_semaphore("d%d" % k) for k in range(self.n_dma_sems)]
        know = {e: {} for e in ENGS}
        know_dma = {e: {} for e in ENGS}
        op_know = [None] * len(ops)
        n_wait = 0
        for i, op in enumerate(ops):
            e = op.eng
            eo = engobj[e]
            k = know[e]
            kd = know_dma[e]
            for j in sorted(op.deps):
                d = ops[j]
                if d.is_dma:
                    if kd.get(d.dsem, 0) >= d.dval:
                        continue
                    eo.wait_ge(dsem[d.dsem], d.dval)
                    n_wait += 1
                    kd[d.dsem] = d.dval
                else:
                    if skip_same(d, op, j):
                        continue
                    if k.get(d.eng, 0) >= d.cnt:
                        continue
                    eo.wait_ge(esem[d.eng], d.cnt)
                    n_wait += 1
                    k[d.eng] = d.cnt
                    ok = op_know[j]
                    if ok is not None:
                        for e2, c2 in ok.items():
                            if k.get(e2, 0) < c2:
                                k[e2] = c2
            ins = op.fn(eo)
            if op.is_dma:
                ins.then_inc(dsem[op.dsem], 16)
            elif op.needs_inc:
                ins.then_inc(esem[e], 1)
                op_know[i] = dict(k)
        sp = nc.sync
        for s in range(self.n_dma_sems):
            j = dma_sem_last[s]
            if j is not None:
                sp.wait_ge(dsem[s], ops[j].dval)
        for e in ENGS:
            if e != "sp" and cnt[e] > 0:
                sp.wait_ge(esem[e], cnt[e])
        self.n_wait = n_wait
        self.counts = cnt
        return self


class Tl:
    def __init__(self, h, name):
        self.h = h
        self.b = Buf(name)

    def __getitem__(self, idx):
        return self.h[idx]


class Arena:
    def __init__(self, nc, nbytes):
        self.nc = nc
        lo, hi = nc.bump_sbuf(nbytes)
        self.lo = lo
        self.hi = hi
        self.cur = lo
        self.n = 0

    def alloc(self, name, shape, dt):
        per = 1
        for s in shape[1:]:
            per *= s
        per *= 4 if dt == F32 else 2
        per = (per + 31) // 32 * 32
        off = self.cur
        if off + per > self.hi:
            raise RuntimeError("arena overflow at %s: need %d, have %d" % (name, per, self.hi - off))
        self.cur += per
        self.n += 1
        h = self.nc.alloc_sbuf_tensor_at("%s_%d" % (name, self.n), list(shape), dt, offset=off)
        return Tl(h, name)

    def mark(self):
        return self.cur

    def release(self, m):
        self.cur = m


def _const_tables():
    f32 = np.float32
    half = 32
    inv = np.power(f32(10000.0), -(np.arange(half, dtype=f32) / f32(half))).astype(f32)
    pos = np.zeros((NT, 128), dtype=f32)
    for t in range(NPT):
        pos[t] = np.arange(t * 128, (t + 1) * 128)
    pos[NPT, :64] = PAST + (np.arange(64) % 4)
    ang = (pos[:, :, None] * inv[None, None, :]).astype(f32)
    rope = np.zeros((128, NT, 2, half), dtype=f32)
    rope[:, :, 0, :] = np.cos(ang).astype(f32).transpose(1, 0, 2)
    rope[:, :, 1, :] = np.sin(ang).astype(f32).transpose(1, 0, 2)
    g = np.array(GAM, dtype=np.float64)
    j = np.arange(128)
    dif = j[None, :] - j[:, None]
    rmask_p = np.zeros((128, 6, 128))
    for h in range(6):
        rmask_p[:, HSLOT[h], :] = np.where(dif >= 0, 0.125 * g[h] ** np.maximum(dif, 0), 0.0)
    same = (j[:, None] // 4) == (j[None, :] // 4)
    rmask_s = np.zeros((128, 6, 64))
    for h in range(6):
        m = np.where((dif >= 0) & same, 0.125 * g[h] ** np.maximum(dif, 0), 0.0)
        rmask_s[:64, HSLOT[h], :] = m[:64, :64]
    qdec_p = np.zeros((128, 3, 128))
    qdec_s = np.zeros((128, 3, 64))
    sdec = np.zeros((128, 2, 3))
    for h in range(6):
        r0 = (h % 2) * 64
        qdec_p[r0:r0 + 64, h // 2, :] = (g[h] ** (j + 1.0))[None, :]
        qdec_s[r0:r0 + 64, h // 2, :] = (g[h] ** ((j[:64] % 4) + 1.0))[None, :]
        sdec[r0:r0 + 64, 0, h // 2] = g[h] ** 128
        sdec[r0:r0 + 64, 1, h // 2] = g[h] ** 4
    kdec = np.zeros((128, 2, 6))
    for h in range(6):
        kdec[:, 0, h] = 0.125 * g[h] ** (127.0 - j)
        kdec[:64, 1, h] = 0.125 * g[h] ** (3.0 - (j[:64] % 4))
    tri = np.zeros((128, 2, 128))
    upp = np.zeros((128, 2, 128))
    tri[:, 0, :] = (dif >= 0)
    upp[:, 0, :] = (dif < 0)
    tri[:64, 1, :64] = ((dif >= 0) & same)[:64, :64]
    upp[:64, 1, :64] = ((dif < 0) & same)[:64, :64]
    seqind = np.zeros((128, 16))
    seqind[:64, :] = (j[:64, None] // 4) == np.arange(16)[None, :]
    ident = np.eye(128)
    cc = np.concatenate([a.reshape(128, -1) for a in (ident, tri, upp, seqind)], axis=1).astype(f32)
    cr = np.concatenate([a.reshape(128, -1) for a in (rmask_p, rmask_s, qdec_p, qdec_s, kdec, sdec)],
                        axis=1).astype(f32)
    rope_t = np.ascontiguousarray(rope.reshape(128, NT, 64).transpose(1, 0, 2)).astype(f32)
    return np.ascontiguousarray(cc), np.ascontiguousarray(cr), rope_t


CC_OFF = {"ident": (0, 128), "tri": (128, 256), "upp": (384, 256), "seqind": (640, 16)}
CC_N = 656
CR_OFF = {}
_o = 0
for _n, _w in (("rmask_p", 768), ("rmask_s", 384), ("qdec_p", 384), ("qdec_s", 192),
               ("kdec", 12), ("sdec", 6)):
    CR_OFF[_n] = (_o, _w)
    _o += _w
CR_N = _o
PR_OFF = {}
_o = 0
for _n, _w in (("lamr", 8), ("lami", 8), ("ldt", 8), ("br", 128), ("bi", 128), ("ctr", 256), ("cti", 256),
               ("dsk", 2), ("glub", 2), ("gatew", 192), ("cw", 66), ("cb", 22)):
    PR_OFF[_n] = (_o, _w)
    _o += _w
PR_N = _o


def build_nc(stop_after=None, reorder=REORDER, window=WINDOW):
    nc = bass.Bass("TRN2", target_bir_lowering=False)
    S = Sched(nc)
    S.do_reorder = reorder
    S.window = window

    def din(name, shape):
        return nc.dram_tensor(name, list(shape), F32, kind="ExternalInput").ap()

    def dout(name, shape):
        return nc.dram_tensor(name, list(shape), F32, kind="ExternalOutput").ap()

    xp = din("xp", [SEQ, D])
    xs = din("xs", [64, D])
    w_in = din("w_in", [DEPTH, D, INC])
    w_out = din("w_out", [DEPTH, D, D])
    ffn_w_in = din("ffn_w_in", [DEPTH, D, 2 * DFF])
    ffn_w_out = din("ffn_w_out", [DEPTH, DFF, D])
    glu_w = din("glu_w", [DEPTH, 256, 256])
    gvecs = din("gvecs", [5, 128, D])
    retgb = din("retgb", [DEPTH, 128, 768])
    glag = din("glag", [DEPTH, 128, 96])
    prm_d = din("prm", [DEPTH, 128, PR_N])
    cc_d = din("cc", [128, CC_N])
    cr_d = din("cr", [128, CR_N])
    rope_d = din("rope", [NT, 128, 64])
    ret_s0 = din("ret_s0", [DEPTH, 128, NS * 3 * 64])
    gla_s0 = din("gla_s0", [DEPTH, 48, NS * 4 * 96])
    s5_h0 = din("s5_h0", [DEPTH, 2, 128, 8 * NS])
    conv_s0 = din("conv_s0", [DEPTH, 128, NFC * NS * 2])

    y_p = dout("y_p", [SEQ, D])
    y_s = dout("y_s", [64, D])
    o_ret_p = dout("o_ret_p", [DEPTH, 128, 192])
    o_ret_s = dout("o_ret_s", [DEPTH, 128, NS * 192])
    o_s5_p = dout("o_s5_p", [DEPTH, 2, 128, 8])
    o_s5_s = dout("o_s5_s", [DEPTH, 2, 128, 8 * NS])
    o_gla_p = dout("o_gla_p", [DEPTH, 48, 384])
    o_gla_s = dout("o_gla_s", [DEPTH, 48, NS * 384])
    o_conv = dout("o_conv", [DEPTH, 128, NFC * 34])

    A = Arena(nc, 209000)
    PS = nc.alloc_psum_tensor("ps", [128, 8, 512], F32)
    PSb = PS.bitcast(BF16)
    pb = [Buf("psum%d" % i, excl=True) for i in range(8)]

    def psf(b0, nb=1):
        return PS[:, b0:b0 + nb, :].rearrange("p b c -> p (b c)")

    def psbf(b0):
        return PSb[:, b0, :]

    xt = [A.alloc("x%d" % t, [128, D], F32) for t in range(NT)]
    hT = A.alloc("hT", [128, 8, NTOK], BF16)
    hTb = [Buf("hT%d" % t) for t in range(NT)]
    cc = A.alloc("cc", [128, CC_N], F32)
    prm = A.alloc("prm", [128, PR_N], F32)
    gA = A.alloc("gA", [128, D], F32)
    ident_bf = A.alloc("ident_bf", [128, 128], BF16)
    tri_bf = A.alloc("tri_bf", [128, 2, 128], BF16)
    ntri_bf = A.alloc("ntri_bf", [128, 2, 128], BF16)
    hb = A.alloc("hb", [128, D], BF16)
    junk = A.alloc("junk", [128, D], BF16)
    ss = A.alloc("ss", [128, 2], F32)
    rs = A.alloc("rs", [128, 4], F32)
    ssb = [Buf("ss0"), Buf("ss1")]
    rsb = [Buf("rs0"), Buf("rs1")]
    mhalf = A.alloc("mhalf", [128, 8], F32)

    def ccv(name):
        o, w = CC_OFF[name]
        return cc[:, o:o + w]

    ident_f = ccv("ident")
    tri_f = ccv("tri").rearrange("p (a b) -> p a b", a=2)
    upp_f = ccv("upp").rearrange("p (a b) -> p a b", a=2)
    seqind = ccv("seqind")

    def prv(name):
        o, w = PR_OFF[name]
        return prm[:, o:o + w]

    S.dma("sp", cc[:], cc_d[:, :], writes=[cc.b])
    S.op("dve", lambda e: e.tensor_copy(out=ident_bf[:], in_=ident_f), reads=[cc.b], writes=[ident_bf.b])
    S.op("dve", lambda e: e.tensor_copy(out=tri_bf[:], in_=tri_f), reads=[cc.b], writes=[tri_bf.b])
    S.op("dve", lambda e: e.tensor_scalar(out=ntri_bf[:], in0=tri_f, scalar1=-1.0, scalar2=None, op0=ALU.mult),
         reads=[cc.b], writes=[ntri_bf.b])
    S.op("pool", lambda e: e.memset(mhalf[:], -0.5), writes=[mhalf.b])

    def tile_info(t):
        return (t * 128, 128 if t < NPT else 64)

    ORD = list(range(NT))
    PAR = {t: i % 2 for i, t in enumerate(ORD)}

    hbL = [hb, junk]

    def norm_stats(t):
        tok0, n = tile_info(t)
        x = xt[t]
        k = t % 2
        S.op("act", lambda e: e.activation(out=hbL[k][:n, :], in_=x[:n, :], func=AF.Square, accum_out=ss[:n, k:k + 1]),
             reads=[x.b], writes=[hbL[k].b, ssb[k]])
        S.op("dve", lambda e: e.tensor_scalar(out=rs[:n, 2 * k:2 * k + 1], in0=ss[:n, k:k + 1], scalar1=1.0 / D, scalar2=EPS,
                                              op0=ALU.mult, op1=ALU.add), reads=[ssb[k]], writes=[rsb[k]])
        S.op("pool", lambda e: e.tensor_tensor(out=rs[:n, 2 * k + 1:2 * k + 2], in0=rs[:n, 2 * k:2 * k + 1], in1=mhalf[:n, 0:1], op=ALU.pow),
             reads=[rsb[k], mhalf.b], writes=[rsb[k]])

    def norm_apply(t, gt, bank=7):
        tok0, n = tile_info(t)
        x = xt[t]
        k = t % 2
        hbk = hbL[k]
        S.op("dve", lambda e: e.scalar_tensor_tensor(out=hbk[:n, :], in0=x[:n, :], scalar=rs[:n, 2 * k + 1:2 * k + 2], in1=gt[:n, :],
                                                     op0=ALU.mult, op1=ALU.mult),
             reads=[x.b, rsb[k], gt.b], writes=[hbk.b])
        pv = psbf(bank).rearrange("p (k c) -> p k c", k=8)
        for kc in range(8):
            S.op("pe", lambda e, kc=kc: e.transpose(out=pv[:, kc, :n], in_=hbk[:n, kc * 128:(kc + 1) * 128],
                                                    identity=ident_bf[:n, :n]),
                 reads=[hbk.b, ident_bf.b], writes=[pb[bank]])
        S.op("act", lambda e: e.activation(out=hT[:, :, tok0:tok0 + n], in_=pv[:, :, :n], func=AF.Copy),
             reads=[pb[bank]], writes=[hTb[t]])

    def norm_tile(t, gt, bank=7):
        norm_stats(t)
        norm_apply(t, gt, bank)

    def chunked(tl, n):
        tl.b = Buf(tl.b.name, members=[Buf("%s_%d" % (tl.b.name, i)) for i in range(n)])
        return tl

    def load_cast(dst_ap, src_ap, wbuf, idx):
        S.dma("pool", dst_ap, src_ap, writes=[wbuf.members[idx]], max_dma_last_dim=8192)

    lamb = A.alloc("lamb", [128, DEPTH, 24], F32)
    sp5L = [A.alloc("sp5_%d" % i, [128, 20, 8], F32) for i in range(DEPTH)]
    S.dma("sp", lamb[:], prm_d[:, :, 0:24].rearrange("l p c -> p l c"), writes=[lamb.b])

    def s5_scalars(l):
        sp5 = sp5L[l]
        def sv(i):
            return sp5[:, i, :]

        def s_op(eng, fn):
            S.op(eng, fn, reads=[sp5.b, lamb.b], writes=[sp5.b])

        lamr, lami, ldt = lamb[:, l, 0:8], lamb[:, l, 8:16], lamb[:, l, 16:24]
        s_op("dve", lambda e: e.tensor_scalar(out=sv(0), in0=lamr, scalar1=-1e-4, scalar2=None, op0=ALU.min))
        s_op("act", lambda e: e.activation(out=sv(1), in_=ldt, func=AF.Exp))
        s_op("dve", lambda e: e.tensor_tensor(out=sv(6), in0=sv(0), in1=sv(1), op=ALU.mult))
        s_op("act", lambda e: e.activation(out=sv(2), in_=sv(6), func=AF.Exp))
        s_op("dve", lambda e: e.tensor_tensor(out=sv(3), in0=lami, in1=sv(1), op=ALU.mult))
        s_op("dve", lambda e: e.tensor_scalar(out=sv(3), in0=sv(3), scalar1=1.0 / 16, scalar2=None, op0=ALU.mult))
        s_op("dve", lambda e: e.tensor_scalar(out=sv(7), in0=sv(3), scalar1=math.pi / 2, scalar2=None, op0=ALU.add))
        s_op("act", lambda e: e.activation(out=sv(5), in_=sv(3), func=AF.Sin))
        s_op("act", lambda e: e.activation(out=sv(4), in_=sv(7), func=AF.Sin))
        for _ in range(4):
            s_op("dve", lambda e: e.tensor_tensor(out=sv(6), in0=sv(4), in1=sv(4), op=ALU.mult))
            s_op("dve", lambda e: e.tensor_tensor(out=sv(7), in0=sv(5), in1=sv(5), op=ALU.mult))
            s_op("dve", lambda e: e.scalar_tensor_tensor(out=sv(5), in0=sv(4), scalar=2.0, in1=sv(5), op0=ALU.mult, op1=ALU.mult))
            s_op("dve", lambda e: e.tensor_tensor(out=sv(4), in0=sv(6), in1=sv(7), op=ALU.subtract))
        s_op("dve", lambda e: e.tensor_tensor(out=sv(8), in0=sv(2), in1=sv(4), op=ALU.mult))
        s_op("dve", lambda e: e.tensor_tensor(out=sv(9), in0=sv(2), in1=sv(5), op=ALU.mult))
        s_op("dve", lambda e: e.tensor_tensor(out=sv(6), in0=sv(0), in1=sv(0), op=ALU.mult))
        s_op("dve", lambda e: e.tensor_tensor(out=sv(7), in0=lami, in1=lami, op=ALU.mult))
        s_op("dve", lambda e: e.tensor_tensor(out=sv(6), in0=sv(6), in1=sv(7), op=ALU.add))
        s_op("dve", lambda e: e.reciprocal(out=sv(10), in_=sv(6)))
        s_op("dve", lambda e: e.tensor_scalar(out=sv(11), in0=sv(8), scalar1=-1.0, scalar2=None, op0=ALU.add))
        s_op("dve", lambda e: e.tensor_tensor(out=sv(6), in0=sv(11), in1=sv(0), op=ALU.mult))
        s_op("dve", lambda e: e.tensor_tensor(out=sv(7), in0=sv(9), in1=lami, op=ALU.mult))
        s_op("dve", lambda e: e.tensor_tensor(out=sv(6), in0=sv(6), in1=sv(7), op=ALU.add))
        s_op("dve", lambda e: e.tensor_tensor(out=sv(12), in0=sv(6), in1=sv(10), op=ALU.mult))
        s_op("dve", lambda e: e.tensor_tensor(out=sv(6), in0=sv(9), in1=sv(0), op=ALU.mult))
        s_op("dve", lambda e: e.tensor_tensor(out=sv(7), in0=sv(11), in1=lami, op=ALU.mult))
        s_op("dve", lambda e: e.tensor_tensor(out=sv(6), in0=sv(6), in1=sv(7), op=ALU.subtract))
        s_op("dve", lambda e: e.tensor_tensor(out=sv(13), in0=sv(6), in1=sv(10), op=ALU.mult))
        s_op("dve", lambda e: e.tensor_tensor(out=sv(6), in0=sv(2), in1=sv(2), op=ALU.mult))
        s_op("dve", lambda e: e.reciprocal(out=sv(16), in_=sv(6)))
        s_op("dve", lambda e: e.tensor_tensor(out=sv(14), in0=sv(8), in1=sv(16), op=ALU.mult))
        s_op("dve", lambda e: e.scalar_tensor_tensor(out=sv(15), in0=sv(9), scalar=-1.0, in1=sv(16), op0=ALU.mult, op1=ALU.mult))

    for l in range(DEPTH):
        s5_scalars(l)

    def layer(l):
        S.dma("sp", prm[:], prm_d[l, :, :], writes=[prm.b])
        S.dma("sp", gA[:], gvecs[2 * l, :, :], writes=[gA.b])
        if l == 0:
            for t in range(NT):
                tok0, n = tile_info(t)
                srcx = xp[tok0:tok0 + n, :] if t < NPT else xs[:, :]
                S.dma("sp", xt[t][:n, :], srcx, writes=[xt[t].b])
        norm_stats(0)
        norm_stats(1)

        m1 = A.mark()
        cr = A.alloc("cr", [128, CR_N], F32)

        def crv(name):
            o, w = CR_OFF[name]
            return cr[:, o:o + w]

        ropeL = [A.alloc("rope%d" % i, [128, 2, 32], F32) for i in range(2)]
        rmask = [crv("rmask_p").rearrange("p (h i) -> p h i", h=6), crv("rmask_s").rearrange("p (h i) -> p h i", h=6)]
        qdec = [crv("qdec_p").rearrange("p (a i) -> p a i", a=3), crv("qdec_s").rearrange("p (a i) -> p a i", a=3)]
        kdec = crv("kdec").rearrange("p (a h) -> p a h", a=2)
        sdec = crv("sdec").rearrange("p (a h) -> p a h", a=2)
        Wr = chunked(A.alloc("Wr", [128, 8, 1536], BF16), 8)
        WoR = chunked(A.alloc("WoR", [128, 3, D], BF16), 3)
        S0r = A.alloc("S0r", [128, NS, 3, 64], F32)
        rgb = A.alloc("rgb", [128, 768], F32)
        qkrot = A.alloc("qkrot", [128, 12, 2, 32], BF16)
        tmp = [A.alloc("rt%d" % i, [128, 12, 32], F32) for i in range(4)]
        kendL = [A.alloc("kend%d" % i, [128, 6, 64], BF16) for i in range(2)]
        vsbL = [A.alloc("vsb%d" % i, [128, 6, 64], BF16) for i in range(2)]
        gsL = [A.alloc("gs%d" % i, [128, 384], F32) for i in range(2)]
        bsL = [A.alloc("bs%d" % i, [128, 384], F32) for i in range(2)]
        sg = A.alloc("sg", [128, 384], F32)
        qT = A.alloc("qT", [128, 3, 128], BF16)
        kT = A.alloc("kT", [128, 3, 128], BF16)
        qsT = A.alloc("qsT", [128, 3, 128], BF16)
        qkrotL = [qkrot, A.alloc("qkrot1", [128, 12, 2, 32], BF16)]
        qTL = [qT, A.alloc("qT1", [128, 3, 128], BF16)]
        kTL = [kT, A.alloc("kT1", [128, 3, 128], BF16)]
        qsTL = [qsT, A.alloc("qsT1", [128, 3, 128], BF16)]
        scT = A.alloc("scT", [128, 6, 128], BF16)
        on1 = A.alloc("on1", [128, 6, 64], F32)
        on2 = A.alloc("on2", [128, 6, 64], F32)

        class Alias:
            def __init__(self, base, ap):
                self.b = base.b
                self.ap = ap

            def __getitem__(self, idx):
                return self.ap[idx]

        sq = Alias(on2, on2[:].rearrange("p h v -> p (h v)"))
        qsTf = Alias(tmp[2], tmp[2][:].rearrange("p a b -> p (a b)")[:, 0:192].rearrange("p (a i) -> p a i", a=3))
        ocT = Alias(tmp[0], tmp[0][0:64].rearrange("p a b -> p (a b)").rearrange("p (h i) -> p h i", h=6))
        kes = Alias(scT, scT[0:64, 0:3, :].rearrange("p a i -> p (a i)"))
        st = A.alloc("st", [128, 40], F32)
        mixr = A.alloc("mixr", [128, 384], BF16)
        mixT = A.alloc("mixT", [128, 3, 128], BF16)
        Sr = A.alloc("Sr", [128, 3, 64], F32)
        Srb = A.alloc("Srb", [128, 3, 64], BF16)

        S.dma("sp", cr[:], cr_d[:, :], writes=[cr.b])
        for kc in range(8):
            load_cast(Wr[:, kc, :], w_in[l, kc * 128:(kc + 1) * 128, 0:1536], Wr.b, kc)
        for kc in range(3):
            load_cast(WoR[:, kc, :], w_out[l, kc * 128:(kc + 1) * 128, :], WoR.b, kc)
        S.dma("sp", S0r[:].rearrange("p s a v -> p (s a v)"), ret_s0[l, :, :], writes=[S0r.b])
        S.dma("sp", rgb[:], retgb[l, :, :], writes=[rgb.b])
        S.op("pool", lambda e: e.memset(Sr[:], 0.0), writes=[Sr.b])
        S.op("pool", lambda e: e.memset(Srb[:], 0.0), writes=[Srb.b])

        for t in range(NT):
            if 1 <= t + 1 < NT and t + 1 >= 2:
                norm_stats(t + 1)
            norm_apply(t, gA, bank=6 + (t % 2))

        def p1_A(t):
            tok0, n = tile_info(t)
            for bank in range(3):
                for kc in range(8):
                    S.op("pe", lambda e, bank=bank, kc=kc: e.matmul(
                        PS[:n, bank, :], lhsT=hT[:, kc, tok0:tok0 + n], rhs=Wr[:, kc, bank * 512:(bank + 1) * 512],
                        start=(kc == 0), stop=(kc == 7)), reads=[hTb[t], Wr.b], writes=[pb[bank]])

        def p1_B(t):
            tok0, n = tile_info(t)
            sm = 0 if t < NPT else 1
            kend, vsb, gs, bs = kendL[PAR[t]], vsbL[PAR[t]], gsL[PAR[t]], bsL[PAR[t]]
            qkrot = qkrotL[PAR[t]]
            flat = psf(0, 3)
            qk = flat[:n, 0:768].rearrange("p (h c d) -> p h c d", h=12, c=2)
            x1 = qk[:, :, 0, :]
            x2 = qk[:, :, 1, :]
            rp = ropeL[PAR[t]]
            S.dma("sp", rp[:].rearrange("p c h -> p (c h)"), rope_d[t, :, :], writes=[rp.b])
            cosb = rp[:n, 0:1, :].broadcast_to([n, 12, 32])
            sinb = rp[:n, 1:2, :].broadcast_to([n, 12, 32])
            rd = [pb[0], pb[1], rp.b]
            S.op("dve", lambda e: e.tensor_tensor(out=tmp[0][:n], in0=x1, in1=cosb, op=ALU.mult), reads=rd, writes=[tmp[0].b])
            S.op("dve", lambda e: e.tensor_tensor(out=tmp[1][:n], in0=x2, in1=sinb, op=ALU.mult), reads=rd, writes=[tmp[1].b])
            S.op("dve", lambda e: e.tensor_tensor(out=tmp[2][:n], in0=x1, in1=sinb, op=ALU.mult), reads=rd, writes=[tmp[2].b])
            S.op("dve", lambda e: e.tensor_tensor(out=tmp[3][:n], in0=x2, in1=cosb, op=ALU.mult), reads=rd, writes=[tmp[3].b])
            S.op("act", lambda e: e.activation(out=vsb[:n].rearrange("p h d -> p (h d)"), in_=flat[:n, 768:1152], func=AF.Copy),
                 reads=[pb[1], pb[2]], writes=[vsb.b])
            S.op("act", lambda e: e.activation(out=sg[:n, :], in_=flat[:n, 1152:1536], func=AF.Silu),
                 reads=[pb[2]], writes=[sg.b])
            S.op("pool", lambda e: e.tensor_tensor(out=qkrot[:n, :, 0, :], in0=tmp[0][:n], in1=tmp[1][:n], op=ALU.subtract),
                 reads=[tmp[0].b, tmp[1].b], writes=[qkrot.b])
            S.op("pool", lambda e: e.tensor_tensor(out=qkrot[:n, :, 1, :], in0=tmp[2][:n], in1=tmp[3][:n], op=ALU.add),
                 reads=[tmp[2].b, tmp[3].b], writes=[qkrot.b])
            qkf = qkrot[:].rearrange("p h c d -> p (h c d)")
            krot = qkf[:n, 384:768].rearrange("p (h d) -> p h d", h=6)
            S.op("dve", lambda e: e.tensor_tensor(out=kend[:n], in0=krot, in1=kdec[:n, sm, :, None].broadcast_to([n, 6, 64]),
                                                  op=ALU.mult), reads=[qkrot.b, cr.b], writes=[kend.b])
            S.op("pool", lambda e: e.tensor_tensor(out=gs[:n, :], in0=sg[:n, :], in1=rgb[:n, 0:384], op=ALU.mult),
                 reads=[sg.b, rgb.b], writes=[gs.b])
            S.op("pool", lambda e: e.tensor_tensor(out=bs[:n, :], in0=sg[:n, :], in1=rgb[:n, 384:768], op=ALU.mult),
                 reads=[sg.b, rgb.b], writes=[bs.b])

        def p1_C(t):
            tok0, n = tile_info(t)
            sm = 0 if t < NPT else 1
            vsb = vsbL[PAR[t]]
            qkrot, qT, kT, qsT = qkrotL[PAR[t]], qTL[PAR[t]], kTL[PAR[t]], qsTL[PAR[t]]
            qkf = qkrot[:].rearrange("p h c d -> p (h c d)")
            tp = psbf(3)[:, 0:768].rearrange("p (a i) -> p a i", a=6)
            for a in range(6):
                S.op("pe", lambda e, a=a: e.transpose(out=tp[:, a, :n], in_=qkf[:n, a * 128:(a + 1) * 128],
                                                      identity=ident_bf[:n, :n]),
                     reads=[qkrot.b, ident_bf.b], writes=[pb[3]])
            S.op("act", lambda e: e.activation(out=kT[:, :, :n], in_=tp[:, 3:6, :n], func=AF.Copy), reads=[pb[3]], writes=[kT.b])
            S.op("act", lambda e: e.activation(out=qT[:, :, :n], in_=tp[:, 0:3, :n], func=AF.Copy), reads=[pb[3]], writes=[qT.b])
            if sm == 0:
                S.op("dve", lambda e: e.tensor_tensor(out=qsT[:, :, :n], in0=tp[:, 0:3, :n], in1=qdec[0][:, :, :n], op=ALU.mult),
                     reads=[pb[3], cr.b], writes=[qsT.b])
            else:
                S.op("dve", lambda e: e.tensor_tensor(out=qsTf[:, :, :n], in0=tp[:, 0:3, :n], in1=qdec[1][:, :, :n], op=ALU.mult),
                     reads=[pb[3], cr.b], writes=[qsTf.b])
            sc = psf(4, 2).rearrange("p (h i) -> p h i", h=8)
            for h in range(6):
                r0 = (h % 2) * 64
                sl_ = (h % 2) * 4 + h // 2
                S.op("pe", lambda e, h=h, r0=r0, sl_=sl_: e.matmul(sc[:n, sl_, :n], lhsT=kT[r0:r0 + 64, h // 2, :n],
                                                                  rhs=qT[r0:r0 + 64, h // 2, :n], start=True, stop=True),
                     reads=[kT.b, qT.b], writes=[pb[4 + h % 2]])
            for par in range(2):
                S.op("dve", lambda e, par=par: e.tensor_tensor(out=scT[:n, 3 * par:3 * par + 3, :n], in0=sc[:n, 4 * par:4 * par + 3, :n],
                                                              in1=rmask[sm][:n, 3 * par:3 * par + 3, :n], op=ALU.mult),
                     reads=[pb[4 + par], cr.b], writes=[scT.b])
            O = psf(6)[:, 0:384].rearrange("p (h v) -> p h v", h=6)
            if sm == 1:
                OCTs = [psf(3)[:, 0:192].rearrange("p (h i) -> p h i", h=3), psf(2)[:, 0:192].rearrange("p (h i) -> p h i", h=3)]
                obank = [3, 2]
                for s in range(NS):
                    for h in range(6):
                        r0 = (h % 2) * 64
                        S.op("pe", lambda e, s=s, h=h, r0=r0: e.matmul(
                            OCTs[h % 2][0:64, h // 2, 4 * s:4 * s + 4], lhsT=S0r[r0:r0 + 64, s, h // 2, :],
                            rhs=qsTf[r0:r0 + 64, h // 2, 4 * s:4 * s + 4], start=True, stop=True),
                            reads=[S0r.b, qsTf.b], writes=[pb[obank[h % 2]]])
                for par in range(2):
                    S.op("act", lambda e, par=par: e.activation(out=ocT[:, par * 3:par * 3 + 3, :], in_=OCTs[par][0:64, :, :], func=AF.Copy),
                         reads=[pb[obank[par]]], writes=[ocT.b])
            for h in range(6):
                r0 = (h % 2) * 64
                S.op("pe", lambda e, h=h: e.matmul(O[:n, h, :], lhsT=scT[:n, HSLOT[h], :n], rhs=vsb[:n, h, :], start=True, stop=False),
                     reads=[scT.b, vsb.b], writes=[pb[6]])
                if sm == 0:
                    S.op("pe", lambda e, h=h, r0=r0: e.matmul(O[:n, h, :], lhsT=qsT[r0:r0 + 64, h // 2, :n],
                                                             rhs=Srb[r0:r0 + 64, h // 2, :], start=False, stop=True),
                         reads=[qsT.b, Srb.b], writes=[pb[6]])
                else:
                    S.op("pe", lambda e, h=h: e.matmul(O[:n, h, :], lhsT=ocT[0:64, HSLOT[h], 0:64], rhs=ident_f[0:64, 0:64],
                                                      start=False, stop=True),
                         reads=[ocT.b, cc.b], writes=[pb[6]])

        def p1_H(t):
            tok0, n = tile_info(t)
            sm = 0 if t < NPT else 1
            kend, vsb = kendL[PAR[t]], vsbL[PAR[t]]
            if sm == 0:
                KV = psf(7)[:, 192:384].rearrange("p (a v) -> p a v", a=3)
                for h in range(6):
                    r0 = (h % 2) * 64
                    S.op("pe", lambda e, h=h, r0=r0: e.matmul(KV[r0:r0 + 64, h // 2, :], lhsT=kend[:n, h, :], rhs=vsb[:n, h, :],
                                                             start=True, stop=True), reads=[kend.b, vsb.b], writes=[pb[7]])
                S.op("dve", lambda e: e.tensor_tensor(out=Sr[:], in0=Sr[:], in1=sdec[:, 0, :, None].broadcast_to([128, 3, 64]),
                                                      op=ALU.mult), reads=[Sr.b, cr.b], writes=[Sr.b])
                S.op("dve", lambda e: e.tensor_tensor(out=Sr[:], in0=KV, in1=Sr[:], op=ALU.add), reads=[pb[7], Sr.b], writes=[Sr.b])
                S.op("act", lambda e: e.activation(out=Srb[:], in_=Sr[:], func=AF.Copy), reads=[Sr.b], writes=[Srb.b])
                if t == NPT - 1:
                    S.dma("sp", o_ret_p[l, :, :], Sr[:].rearrange("p a v -> p (a v)"), reads=[Sr.b])
            else:
                kendf = kend[:].rearrange("p h d -> p (h d)")
                for g4 in range(4):
                    KV = psf(4, 2)[:, 0:768].rearrange("p (s a v) -> p s a v", s=4, a=3)
                    for sl in range(4):
                        s = g4 * 4 + sl
                        S.op("dve", lambda e, s=s: e.tensor_scalar(out=kes[:, :], in0=kendf[:64, :], scalar1=seqind[:64, s:s + 1],
                                                                   scalar2=None, op0=ALU.mult),
                             reads=[kend.b, cc.b], writes=[kes.b])
                        for h in range(6):
                            r0 = (h % 2) * 64
                            bank = 4 + (sl * 3 + h // 2) // 8
                            S.op("pe", lambda e, sl=sl, h=h, r0=r0, KV=KV: e.matmul(
                                KV[r0:r0 + 64, sl, h // 2, :], lhsT=kes[:64, h * 64:(h + 1) * 64], rhs=vsb[:64, h, :],
                                start=True, stop=True), reads=[kes.b, vsb.b], writes=[pb[bank]])
                    S0g_ = S0r[:, g4 * 4:(g4 + 1) * 4, :, :]
                    S.op("dve", lambda e, S0g_=S0g_: e.tensor_tensor(
                        out=S0g_, in0=S0g_, in1=sdec[:, 1, None, :, None].broadcast_to([128, 4, 3, 64]), op=ALU.mult),
                        reads=[S0r.b, cr.b], writes=[S0r.b])
                    S.op("dve", lambda e, S0g_=S0g_, KV=KV: e.tensor_tensor(out=S0g_, in0=KV, in1=S0g_, op=ALU.add),
                         reads=[pb[4], pb[5], S0r.b], writes=[S0r.b])
                S.dma("sp", o_ret_s[l, :, :], S0r[:].rearrange("p s a v -> p (s a v)"), reads=[S0r.b])

        def p1_L(t):
            tok0, n = tile_info(t)
            gs, bs = gsL[PAR[t]], bsL[PAR[t]]
            O = psf(6)[:, 0:384].rearrange("p (h v) -> p h v", h=6)
            Of = psf(6)[:n, 0:384]
            S.op("act", lambda e: e.activation(out=sq[:n, :], in_=Of, func=AF.Square), reads=[pb[6]], writes=[sq.b])
            S.op("dve", lambda e: e.tensor_reduce(out=st[:n, 0:6], in_=O[:n], axis=AX.X, op=ALU.add), reads=[pb[6]], writes=[st.b])
            S.op("dve", lambda e: e.tensor_reduce(out=st[:n, 6:12], in_=sq[:n, :].rearrange("p (h v) -> p h v", h=6),
                                                  axis=AX.X, op=ALU.add), reads=[sq.b], writes=[st.b])
            S.op("dve", lambda e: e.tensor_scalar(out=st[:n, 12:18], in0=st[:n, 0:6], scalar1=1.0 / 64, scalar2=None, op0=ALU.mult),
                 reads=[st.b], writes=[st.b])
            S.op("dve", lambda e: e.tensor_tensor(out=st[:n, 18:24], in0=st[:n, 12:18], in1=st[:n, 12:18], op=ALU.mult),
                 reads=[st.b], writes=[st.b])
            S.op("dve", lambda e: e.scalar_tensor_tensor(out=st[:n, 24:30], in0=st[:n, 6:12], scalar=1.0 / 64, in1=st[:n, 18:24],
                                                         op0=ALU.mult, op1=ALU.subtract), reads=[st.b], writes=[st.b])
            S.op("dve", lambda e: e.tensor_scalar(out=st[:n, 30:36], in0=st[:n, 24:30], scalar1=EPS, scalar2=None, op0=ALU.add),
                 reads=[st.b], writes=[st.b])
            S.op("pool", lambda e: e.tensor_tensor(out=st[:n, 24:30], in0=st[:n, 30:36], in1=mhalf[:n, 0:6], op=ALU.pow),
                 reads=[st.b, mhalf.b], writes=[st.b])
            S.op("dve", lambda e: e.tensor_tensor(out=on1[:n], in0=O[:n], in1=st[:n, 12:18, None].broadcast_to([n, 6, 64]),
                                                  op=ALU.subtract), reads=[pb[6], st.b], writes=[on1.b])
            S.op("dve", lambda e: e.tensor_tensor(out=on2[:n], in0=on1[:n], in1=st[:n, 24:30, None].broadcast_to([n, 6, 64]),
                                                  op=ALU.mult), reads=[on1.b, st.b], writes=[on2.b])
            on2f = on2[:].rearrange("p h v -> p (h v)")
            on1f = on1[:].rearrange("p h v -> p (h v)")
            S.op("dve", lambda e: e.tensor_tensor(out=on1f[:n], in0=on2f[:n], in1=gs[:n, :], op=ALU.mult),
                 reads=[on2.b, gs.b], writes=[on1.b], safe=True)
            S.op("dve", lambda e: e.tensor_tensor(out=mixr[:n, :], in0=on1f[:n], in1=bs[:n, :], op=ALU.add),
                 reads=[on1.b, bs.b], writes=[mixr.b], safe=True)

        def p1_G(t):
            tok0, n = tile_info(t)
            x = xt[t]
            mp = psbf(7)[:, 0:384].rearrange("p (a i) -> p a i", a=3)
            for a in range(3):
                S.op("pe", lambda e, a=a: e.transpose(out=mp[:, a, :n], in_=mixr[:n, a * 128:(a + 1) * 128],
                                                      identity=ident_bf[:n, :n]), reads=[mixr.b, ident_bf.b], writes=[pb[7]])
            S.op("act", lambda e: e.activation(out=mixT[:, :, :n], in_=mp[:, :, :n], func=AF.Copy), reads=[pb[7]], writes=[mixT.b])
            for bank in range(2):
                for a in range(3):
                    S.op("pe", lambda e, bank=bank, a=a: e.matmul(PS[:n, 4 + bank, :], lhsT=mixT[:, a, :n],
                                                                 rhs=WoR[:, a, bank * 512:(bank + 1) * 512],
                                                                 start=(a == 0), stop=(a == 2)),
                         reads=[mixT.b, WoR.b], writes=[pb[4 + bank]])
            S.op("dve", lambda e: e.tensor_tensor(out=x[:n, :], in0=psf(4, 2)[:n, :], in1=x[:n, :], op=ALU.add),
                 reads=[pb[4], pb[5], x.b], writes=[x.b])

        p1_A(ORD[0])
        p1_B(ORD[0])
        for i, t in enumerate(ORD):
            nx = ORD[i + 1] if i + 1 < NT else None
            p1_C(t)
            p1_H(t)
            if nx is not None:
                p1_A(nx)
            p1_L(t)
            if nx is not None:
                p1_B(nx)
            p1_G(t)
        S.barrier()
        A.release(m1)
        if stop_after is not None and stop_after.startswith("P1"):
            return True

        m2 = A.mark()
        W5 = chunked(A.alloc("W5", [128, 8, 256], BF16), 8)
        Wo5 = chunked(A.alloc("Wo5", [128, 2, D], BF16), 2)
        gluw = chunked(A.alloc("gluw", [128, 2, 256], BF16), 2)
        Tfr = A.alloc("Tfr", [128, 8, 128], F32)
        Tfi = A.alloc("Tfi", [128, 8, 128], F32)
        nTfi = A.alloc("nTfi", [128, 8, 128], F32)
        Tinv = A.alloc("Tinv", [128, 8, 2, 128], F32)
        Tfs = [A.alloc("Tfs%d" % i, [128, 8, 64], F32) for i in range(3)]
        Tinvs = A.alloc("Tinvs", [64, 8, 2, 128], F32)
        Zt = A.alloc("Zt", [128, 8, 2, 32], BF16)
        Bc = A.alloc("Bc", [128, 2, 2, 128], BF16)
        Ctb = A.alloc("Ctb", [128, 2, 8, 32], BF16)
        H0 = A.alloc("H0", [128, 2, 8, NS], F32)
        Hst = A.alloc("Hst", [128, 2, 8], F32)
        Hb = [Buf("H%d" % g) for g in range(8)]
        Hout = A.alloc("Hout", [128, 2, 8, NS], F32)

        for kc in range(8):
            load_cast(W5[:, kc, :], w_in[l, kc * 128:(kc + 1) * 128, 1536:1792], W5.b, kc)
        for kc in range(2):
            load_cast(Wo5[:, kc, :], w_out[l, 384 + kc * 128:384 + (kc + 1) * 128, :], Wo5.b, kc)
            load_cast(gluw[:, kc, :], glu_w[l, kc * 128:(kc + 1) * 128, :], gluw.b, kc)
        S.dma("sp", H0[:].rearrange("p c g s -> p c (g s)"), s5_h0[l].rearrange("c p n -> p c n"), writes=[H0.b])

        m2s = A.mark()
        Pir = A.alloc("Pir", [128, 8, 128], F32)
        Pii = A.alloc("Pii", [128, 8, 128], F32)
        pt = [A.alloc("pt%d" % i, [128, 8, 64], F32) for i in range(4)]
        bt2 = [A.alloc("pu%d" % i, [128, 8, 64], F32) for i in range(4)]
        bt = [A.alloc("bt%d" % i, [128, 8, 16], F32) for i in range(4)]
        Tis = [A.alloc("Tis%d" % i, [128, 8, 64], F32) for i in range(2)]

        sp5 = sp5L[l]

        def sv(i):
            return sp5[:, i, :]


        br = prv("br").rearrange("p (g h) -> p g h", g=8)
        bi = prv("bi").rearrange("p (g h) -> p g h", g=8)

        def bc16(i):
            return sp5[:, i, :, None].broadcast_to([128, 8, 16])

        S.op("pool", lambda e: e.memset(Zt[:], 0.0), writes=[Zt.b])
        rb = [sp5.b, prm.b]
        S.op("dve", lambda e: e.tensor_tensor(out=bt[0][:], in0=br, in1=bc16(12), op=ALU.mult), reads=rb, writes=[bt[0].b])
        S.op("dve", lambda e: e.tensor_tensor(out=bt[1][:], in0=bi, in1=bc16(13), op=ALU.mult), reads=rb, writes=[bt[1].b])
        S.op("dve", lambda e: e.tensor_tensor(out=bt[2][:], in0=bi, in1=bc16(12), op=ALU.mult), reads=rb, writes=[bt[2].b])
        S.op("dve", lambda e: e.tensor_tensor(out=bt[3][:], in0=br, in1=bc16(13), op=ALU.mult), reads=rb, writes=[bt[3].b])
        for (p0, c0) in ((0, 0), (64, 16)):
            S.op("dve", lambda e, p0=p0, c0=c0: e.tensor_tensor(out=Zt[p0:p0 + 64, :, 0, c0:c0 + 16], in0=bt[0][p0:p0 + 64],
                                                               in1=bt[1][p0:p0 + 64], op=ALU.subtract),
                 reads=[bt[0].b, bt[1].b, Zt.b], writes=[Zt.b])
            S.op("dve", lambda e, p0=p0, c0=c0: e.tensor_tensor(out=Zt[p0:p0 + 64, :, 1, c0:c0 + 16], in0=bt[2][p0:p0 + 64],
                                                               in1=bt[3][p0:p0 + 64], op=ALU.add),
                 reads=[bt[2].b, bt[3].b, Zt.b], writes=[Zt.b])
        BCp = psf(0).rearrange("p (a c i) -> p a c i", a=2, c=2)
        for gp in range(8):
            for ri in range(2):
                q0 = 32 * (gp % 4)
                S.op("pe", lambda e, gp=gp, ri=ri, q0=q0: e.matmul(BCp[q0:q0 + 32, gp // 4, ri, :], lhsT=Zt[:, gp, ri, :],
                                                                  rhs=ident_bf[:, :], start=True, stop=True,
                                                                  tile_position=(0, q0)),
                     reads=[Zt.b, ident_bf.b], writes=[pb[0]])
        S.op("act", lambda e: e.activation(out=Bc[:], in_=BCp, func=AF.Copy), reads=[pb[0]], writes=[Bc.b])
        S.op("dve", lambda e: e.tensor_copy(out=Ctb[:, 0, :, :], in_=prv("ctr").rearrange("p (g c) -> p g c", g=8)),
             reads=[prm.b], writes=[Ctb.b])
        S.op("dve", lambda e: e.tensor_copy(out=Ctb[:, 1, :, :], in_=prv("cti").rearrange("p (g c) -> p g c", g=8)),
             reads=[prm.b], writes=[Ctb.b])

        def powers(Pr, Pi, ir, ii, en="dve"):
            ptx = pt if en == "dve" else bt2
            S.op(en, lambda e: e.tensor_copy(out=Pr[:, :, 0], in_=sv(ir)), reads=[sp5.b], writes=[Pr.b])
            S.op(en, lambda e: e.tensor_copy(out=Pi[:, :, 0], in_=sv(ii)), reads=[sp5.b], writes=[Pi.b])
            nn = 1
            while nn < 128:
                a_r = Pr[:, :, 0:nn]
                a_i = Pi[:, :, 0:nn]
                c_r = Pr[:, :, nn - 1:nn].broadcast_to([128, 8, nn])
                c_i = Pi[:, :, nn - 1:nn].broadcast_to([128, 8, nn])
                rw = [Pr.b, Pi.b]
                S.op(en, lambda e, a_r=a_r, c_r=c_r, nn=nn: e.tensor_tensor(out=ptx[0][:, :, 0:nn], in0=a_r, in1=c_r, op=ALU.mult),
                     reads=rw, writes=[ptx[0].b])
                S.op(en, lambda e, a_i=a_i, c_i=c_i, nn=nn: e.tensor_tensor(out=ptx[1][:, :, 0:nn], in0=a_i, in1=c_i, op=ALU.mult),
                     reads=rw, writes=[ptx[1].b])
                S.op(en, lambda e, a_r=a_r, c_i=c_i, nn=nn: e.tensor_tensor(out=ptx[2][:, :, 0:nn], in0=a_r, in1=c_i, op=ALU.mult),
                     reads=rw, writes=[ptx[2].b])
                S.op(en, lambda e, a_i=a_i, c_r=c_r, nn=nn: e.tensor_tensor(out=ptx[3][:, :, 0:nn], in0=a_i, in1=c_r, op=ALU.mult),
                     reads=rw, writes=[ptx[3].b])
                S.op(en, lambda e, nn=nn: e.tensor_tensor(out=Pr[:, :, nn:2 * nn], in0=ptx[0][:, :, 0:nn], in1=ptx[1][:, :, 0:nn],
                                                             op=ALU.subtract), reads=[ptx[0].b, ptx[1].b, Pr.b], writes=[Pr.b])
                S.op(en, lambda e, nn=nn: e.tensor_tensor(out=Pi[:, :, nn:2 * nn], in0=ptx[2][:, :, 0:nn], in1=ptx[3][:, :, 0:nn],
                                                             op=ALU.add), reads=[ptx[2].b, ptx[3].b, Pi.b], writes=[Pi.b])
                nn *= 2

        powers(Tfr, Tfi, 8, 9)
        powers(Pir, Pii, 14, 15, en="pool")
        S.op("dve", lambda e: e.tensor_scalar(out=nTfi[:], in0=Tfi[:], scalar1=-1.0, scalar2=None, op0=ALU.mult),
             reads=[Tfi.b], writes=[nTfi.b])
        for i, src in enumerate((Tfr, Tfi, nTfi)):
            S.op("dve", lambda e, i=i, src=src: e.tensor_copy(
                out=Tfs[i][:].rearrange("p g (s t) -> p g s t", s=NS),
                in_=src[:, :, None, 0:4].broadcast_to([128, 8, NS, 4])), reads=[src.b], writes=[Tfs[i].b])
        for i, src in enumerate((Pir, Pii)):
            S.op("dve", lambda e, i=i, src=src: e.tensor_copy(
                out=Tis[i][:].rearrange("p g (s t) -> p g s t", s=NS),
                in_=src[:, :, None, 0:4].broadcast_to([128, 8, NS, 4])), reads=[src.b], writes=[Tis[i].b])
        for ri, src in enumerate((Pir, Pii)):
            for gq in range(2):
                bank = 1 + ri * 2 + gq
                tpv = psf(bank).rearrange("p (g i) -> p g i", g=4)
                for gl in range(4):
                    gp = gq * 4 + gl
                    S.op("pe", lambda e, tpv=tpv, gl=gl, gp=gp, src=src: e.transpose(out=tpv[:, gl, :], in_=src[:, gp, :],
                                                                                   identity=ident_f[:, :]),
                         reads=[src.b, cc.b], writes=[pb[bank]])
                S.op("act", lambda e, tpv=tpv, gq=gq, ri=ri: e.activation(out=Tinv[:, gq * 4:(gq + 1) * 4, ri, :], in_=tpv,
                                                                         func=AF.Copy), reads=[pb[bank]], writes=[Tinv.b])
        for ri in range(2):
            for gq in range(2):
                bank = 5 + ri
                tpv = psf(bank).rearrange("p (g i) -> p g i", g=4)
                for gl in range(4):
                    gp = gq * 4 + gl
                    S.op("pe", lambda e, tpv=tpv, gl=gl, gp=gp, ri=ri: e.transpose(out=tpv[0:64, gl, :], in_=Tis[ri][:, gp, :],
                                                                                  identity=ident_f[:, :]),
                         reads=[Tis[ri].b, cc.b], writes=[pb[bank]])
                S.op("act", lambda e, tpv=tpv, gq=gq, ri=ri: e.activation(out=Tinvs[:, gq * 4:(gq + 1) * 4, ri, :],
                                                                         in_=tpv[0:64, :, :], func=AF.Copy),
                     reads=[pb[bank]], writes=[Tinvs.b])
        S.barrier()
        A.release(m2s)
        uTb = A.alloc("uTb", [128, 2, 128], BF16)
        uTfL = [A.alloc("uTf%d" % i, [128, 2, 128], F32) for i in range(2)]
        xpL = [[A.alloc("xp%d_%d" % (i, k), [128, 8, 128], BF16) for k in range(4)] for i in range(2)]
        ab = [[A.alloc("ab%d_%d" % (k, i), [128, 128], F32) for i in range(4)] for k in range(2)]
        hrT = A.alloc("hrT", [128, 8, 128], BF16)
        nhiT = A.alloc("nhiT", [128, 8, 128], BF16)
        ysb = A.alloc("ysb", [128, 2, 128], F32)
        y5 = A.alloc("y5", [128, 2, 128], F32)
        y5b = A.alloc("y5b", [128, 2, 128], BF16)
        sig = A.alloc("sig", [128, 2, 128], F32)
        o5T = A.alloc("o5T", [128, 2, 128], BF16)
        BcF = A.alloc("BcF", [128, 2, 4, 256], BF16)

        class Alias2:
            def __init__(self, base, ap):
                self.b = base.b
                self.ap = ap

            def __getitem__(self, idx):
                return self.ap[idx]

        GH = [Alias2(ab[0][i], ab[0][i][:].rearrange("p (g i) -> p g i", g=2)) for i in range(2)]
        ws = [Alias2(ab[1][i], ab[1][i][:].rearrange("p (g i) -> p g i", g=2)) for i in range(4)]
        dsk = prv("dsk")
        glub = prv("glub")

        S.op("pool", lambda e: e.memset(Hst[:], 0.0), writes=Hb)
        S.op("pool", lambda e: e.memset(BcF[:], 0.0), writes=[BcF.b])
        for gq in range(4):
            S.op("act", lambda e, gq=gq: e.activation(out=BcF[32 * gq:32 * gq + 32, :, gq, :],
                                                      in_=Bc[32 * gq:32 * gq + 32, :, :, :].rearrange("p a c i -> p a (c i)"),
                                                      func=AF.Copy), reads=[Bc.b, BcF.b], writes=[BcF.b])

        def p2_Xa(t):
            tok0, n = tile_info(t)
            sm = 0 if t < NPT else 1
            uTf = uTfL[PAR[t]]
            suT = psf(0)[:, 0:256].rearrange("p (a i) -> p a i", a=2)
            for half in range(2):
                for kc in range(8):
                    S.op("pe", lambda e, half=half, kc=kc: e.matmul(suT[:, half, :n], lhsT=W5[:, kc, half * 128:(half + 1) * 128],
                                                                   rhs=hT[:, kc, tok0:tok0 + n], start=(kc == 0), stop=(kc == 7)),
                         reads=[W5.b, hTb[t]], writes=[pb[0]])
            S.op("act", lambda e: e.activation(out=uTb[:, :, :n], in_=suT[:, :, :n], func=AF.Copy), reads=[pb[0]], writes=[uTb.b])
            S.op("act", lambda e: e.activation(out=uTf[:, :, :n], in_=suT[:, :, :n], func=AF.Copy), reads=[pb[0]], writes=[uTf.b])

        def p2_Xb(t):
            tok0, n = tile_info(t)
            sm = 0 if t < NPT else 1
            xa, xb, xc, xd = xpL[PAR[t]]
            TI = Tinv if sm == 0 else Tinvs
            for hh in range(2):
                X = psf(1, 2).rearrange("p (g c i) -> p g c i", g=4, c=2)
                for nb in range(2):
                    S.op("pe", lambda e, hh=hh, nb=nb: e.matmul(
                        PS[:n, 1 + nb, :], lhsT=uTb[:, hh, :n],
                        rhs=BcF[:, hh, 2 * nb:2 * nb + 2, :].rearrange("p g i -> p (g i)"), start=True, stop=True),
                        reads=[uTb.b, BcF.b], writes=[pb[1 + nb]])
                rdx = [pb[1], pb[2], TI.b]
                gs = slice(hh * 4, hh * 4 + 4)
                S.op("dve", lambda e, X=X, gs=gs: e.tensor_tensor(out=xa[:n, gs, :], in0=X[:n, :, 0, :], in1=TI[:n, gs, 0, :], op=ALU.mult),
                     reads=rdx, writes=[xa.b])
                S.op("dve", lambda e, X=X, gs=gs: e.tensor_tensor(out=xb[:n, gs, :], in0=X[:n, :, 1, :], in1=TI[:n, gs, 1, :], op=ALU.mult),
                     reads=rdx, writes=[xb.b])
                S.op("dve", lambda e, X=X, gs=gs: e.tensor_tensor(out=xc[:n, gs, :], in0=X[:n, :, 0, :], in1=TI[:n, gs, 1, :], op=ALU.mult),
                     reads=rdx, writes=[xc.b])
                S.op("dve", lambda e, X=X, gs=gs: e.tensor_tensor(out=xd[:n, gs, :], in0=X[:n, :, 1, :], in1=TI[:n, gs, 0, :], op=ALU.mult),
                     reads=rdx, writes=[xd.b])

        def p2_cum(t, q):
            tok0, n = tile_info(t)
            sm = 0 if t < NPT else 1
            xa, xb, xc, xd = xpL[PAR[t]]
            bank = 5 + (q % 3)
            G = psf(bank).rearrange("p (g c i) -> p g c i", g=2, c=2)
            for gl in range(2):
                gp = 2 * q + gl
                for ri, (u, v, r2) in enumerate(((xa, xb, ntri_bf), (xc, xd, tri_bf))):
                    S.op("pe", lambda e, gl=gl, gp=gp, ri=ri, u=u: e.matmul(G[:, gl, ri, :n], lhsT=u[:n, gp, :],
                                                                          rhs=tri_bf[:n, sm, :n], start=True, stop=False),
                         reads=[u.b, tri_bf.b], writes=[pb[bank]])
                    S.op("pe", lambda e, gl=gl, gp=gp, ri=ri, v=v, r2=r2: e.matmul(G[:, gl, ri, :n], lhsT=v[:n, gp, :],
                                                                                  rhs=r2[:n, sm, :n], start=False, stop=True),
                         reads=[v.b, r2.b], writes=[pb[bank]])

        def p2_H(t, q):
            tok0, n = tile_info(t)
            sm = 0 if t < NPT else 1
            bank = 5 + (q % 3)
            G = psf(bank).rearrange("p (g c i) -> p g c i", g=2, c=2)
            if sm == 0:
                for gl in range(2):
                    gp = 2 * q + gl
                    a_, b_, c_, d_ = ab[gp % 2]
                    hr_s = Hst[:, 0, gp:gp + 1]
                    hi_s = Hst[:, 1, gp:gp + 1]
                    rd = [pb[bank], Hb[gp], Tfr.b, Tfi.b, nTfi.b]
                    S.op("dve", lambda e, gl=gl, gp=gp, a_=a_, hr_s=hr_s: e.scalar_tensor_tensor(
                        out=a_[:, :], in0=G[:, gl, 0, :], scalar=hr_s, in1=Tfr[:, gp, :], op0=ALU.add, op1=ALU.mult),
                        reads=rd, writes=[a_.b])
                    S.op("dve", lambda e, gl=gl, gp=gp, b_=b_, hi_s=hi_s: e.scalar_tensor_tensor(
                        out=b_[:, :], in0=G[:, gl, 1, :], scalar=hi_s, in1=Tfi[:, gp, :], op0=ALU.add, op1=ALU.mult),
                        reads=rd, writes=[b_.b])
                    S.op("dve", lambda e, gl=gl, gp=gp, c_=c_, hr_s=hr_s: e.scalar_tensor_tensor(
                        out=c_[:, :], in0=G[:, gl, 0, :], scalar=hr_s, in1=nTfi[:, gp, :], op0=ALU.add, op1=ALU.mult),
                        reads=rd, writes=[c_.b])
                    S.op("dve", lambda e, gl=gl, gp=gp, d_=d_, hi_s=hi_s: e.scalar_tensor_tensor(
                        out=d_[:, :], in0=G[:, gl, 1, :], scalar=hi_s, in1=Tfr[:, gp, :], op0=ALU.add, op1=ALU.mult),
                        reads=rd, writes=[d_.b])
                    S.op("pool", lambda e, gp=gp, a_=a_, b_=b_: e.tensor_tensor(out=Hst[:, 0, gp:gp + 1], in0=a_[:, 127:128],
                                                                               in1=b_[:, 127:128], op=ALU.subtract),
                         reads=[a_.b, b_.b, Hb[gp]], writes=[Hb[gp]])
                    S.op("pool", lambda e, gp=gp, c_=c_, d_=d_: e.tensor_tensor(out=Hst[:, 1, gp:gp + 1], in0=d_[:, 127:128],
                                                                               in1=c_[:, 127:128], op=ALU.subtract),
                         reads=[c_.b, d_.b, Hb[gp]], writes=[Hb[gp]])
                    S.op("pool", lambda e, gp=gp, a_=a_, b_=b_: e.tensor_tensor(out=hrT[:, gp, :], in0=a_[:, :], in1=b_[:, :],
                                                                               op=ALU.subtract),
                         reads=[a_.b, b_.b], writes=[hrT.b])
                    S.op("pool", lambda e, gp=gp, c_=c_, d_=d_: e.tensor_tensor(out=nhiT[:, gp, :], in0=c_[:, :], in1=d_[:, :],
                                                                               op=ALU.subtract),
                         reads=[c_.b, d_.b], writes=[nhiT.b])
            else:
                gs = slice(2 * q, 2 * q + 2)
                Gr = G[:, :, 0, 0:64].rearrange("p g (s t) -> p g s t", s=NS)
                Gi = G[:, :, 1, 0:64].rearrange("p g (s t) -> p g s t", s=NS)
                h0r = H0[:, 0, gs, :, None].broadcast_to([128, 2, NS, 4])
                h0i = H0[:, 1, gs, :, None].broadcast_to([128, 2, NS, 4])
                S.op("dve", lambda e: e.tensor_tensor(out=GH[0][:].rearrange("p g (s t) -> p g s t", s=NS),
                                                      in0=Gr, in1=h0r, op=ALU.add),
                     reads=[pb[bank], H0.b], writes=[GH[0].b])
                S.op("dve", lambda e: e.tensor_tensor(out=GH[1][:].rearrange("p g (s t) -> p g s t", s=NS),
                                                      in0=Gi, in1=h0i, op=ALU.add),
                     reads=[pb[bank], H0.b], writes=[GH[1].b])
                rdt = [GH[0].b, GH[1].b, Tfs[0].b, Tfs[1].b, Tfs[2].b]
                S.op("dve", lambda e: e.tensor_tensor(out=ws[0][:], in0=GH[0][:], in1=Tfs[0][:, gs, :], op=ALU.mult),
                     reads=rdt, writes=[ws[0].b])
                S.op("dve", lambda e: e.tensor_tensor(out=ws[1][:], in0=GH[1][:], in1=Tfs[1][:, gs, :], op=ALU.mult),
                     reads=rdt, writes=[ws[1].b])
                S.op("dve", lambda e: e.tensor_tensor(out=ws[2][:], in0=GH[0][:], in1=Tfs[2][:, gs, :], op=ALU.mult),
                     reads=rdt, writes=[ws[2].b])
                S.op("dve", lambda e: e.tensor_tensor(out=ws[3][:], in0=GH[1][:], in1=Tfs[0][:, gs, :], op=ALU.mult),
                     reads=rdt, writes=[ws[3].b])
                S.op("pool", lambda e: e.tensor_tensor(out=hrT[:, gs, 0:64], in0=ws[0][:], in1=ws[1][:], op=ALU.subtract),
                     reads=[ws[0].b, ws[1].b], writes=[hrT.b])
                S.op("pool", lambda e: e.tensor_tensor(out=nhiT[:, gs, 0:64], in0=ws[2][:], in1=ws[3][:], op=ALU.subtract),
                     reads=[ws[2].b, ws[3].b], writes=[nhiT.b])

                def last(w):
                    return w[:].rearrange("p g (s t) -> p g s t", s=NS)[:, :, :, 3]
                S.op("pool", lambda e: e.tensor_tensor(out=Hout[:, 0, gs, :], in0=last(ws[0]), in1=last(ws[1]), op=ALU.subtract),
                     reads=[ws[0].b, ws[1].b], writes=[Hout.b])
                S.op("pool", lambda e: e.tensor_tensor(out=Hout[:, 1, gs, :], in0=last(ws[3]), in1=last(ws[2]), op=ALU.subtract),
                     reads=[ws[2].b, ws[3].b], writes=[Hout.b])

        def p2_Y1(t):
            tok0, n = tile_info(t)
            sm = 0 if t < NPT else 1
            if sm == 0 and t == NPT - 1:
                S.dma("sp", o_s5_p[l].rearrange("c p g -> p c g"), Hst[:], reads=Hb)
            if sm == 1:
                S.dma("sp", o_s5_s[l].rearrange("c p n -> p c n"), Hout[:].rearrange("p c g s -> p c (g s)"), reads=[Hout.b])
            yT = psf(0)[:, 256:512].rearrange("p (a i) -> p a i", a=2)
            for gp in range(8):
                q0 = 32 * (gp % 4)
                S.op("pe", lambda e, gp=gp, q0=q0: e.matmul(yT[q0:q0 + 32, gp // 4, :n], lhsT=Ctb[:, 0, gp, :], rhs=hrT[:, gp, :n],
                                                           start=True, stop=False, tile_position=(0, q0)),
                     reads=[Ctb.b, hrT.b], writes=[pb[0]])
                S.op("pe", lambda e, gp=gp, q0=q0: e.matmul(yT[q0:q0 + 32, gp // 4, :n], lhsT=Ctb[:, 1, gp, :], rhs=nhiT[:, gp, :n],
                                                           start=False, stop=True, tile_position=(0, q0)),
                     reads=[Ctb.b, nhiT.b], writes=[pb[0]])

        def p2_Y2a(t):
            tok0, n = tile_info(t)
            uTf = uTfL[PAR[t]]
            yT = psf(0)[:, 256:512].rearrange("p (a i) -> p a i", a=2)
            for half in range(2):
                S.op("dve", lambda e, half=half: e.scalar_tensor_tensor(out=ysb[:, half, :n], in0=uTf[:, half, :n],
                                                                       scalar=dsk[:, half:half + 1], in1=yT[:, half, :n],
                                                                       op0=ALU.mult, op1=ALU.add),
                     reads=[uTf.b, prm.b, pb[0]], writes=[ysb.b])
            S.op("act", lambda e: e.activation(out=y5[:, :, :n], in_=ysb[:, :, :n], func=AF.Gelu_apprx_tanh), reads=[ysb.b], writes=[y5.b])
            S.op("act", lambda e: e.activation(out=y5b[:, :, :n], in_=y5[:, :, :n], func=AF.Copy), reads=[y5.b], writes=[y5b.b])
            zT = psf(0)[:, 256:512].rearrange("p (a i) -> p a i", a=2)
            for ho in range(2):
                for kc in range(2):
                    S.op("pe", lambda e, ho=ho, kc=kc: e.matmul(zT[:, ho, :n], lhsT=gluw[:, kc, ho * 128:(ho + 1) * 128],
                                                               rhs=y5b[:, kc, :n], start=(kc == 0), stop=(kc == 1)),
                         reads=[gluw.b, y5b.b], writes=[pb[0]])
            for ho in range(2):
                S.op("act", lambda e, ho=ho: e.activation(out=sig[:, ho, :n], in_=zT[:, ho, :n], func=AF.Sigmoid,
                                                         bias=glub[:, ho:ho + 1], scale=1.0),
                     reads=[pb[0], prm.b], writes=[sig.b])

        def p2_Y2b(t):
            tok0, n = tile_info(t)
            S.op("dve", lambda e: e.tensor_tensor(out=o5T[:, :, :n], in0=y5[:, :, :n], in1=sig[:, :, :n], op=ALU.mult),
                 reads=[y5.b, sig.b], writes=[o5T.b])
            for bank in range(2):
                for a in range(2):
                    S.op("pe", lambda e, bank=bank, a=a: e.matmul(PS[:n, 3 + bank, :], lhsT=o5T[:, a, :n],
                                                                 rhs=Wo5[:, a, bank * 512:(bank + 1) * 512],
                                                                 start=(a == 0), stop=(a == 1)),
                         reads=[o5T.b, Wo5.b], writes=[pb[3 + bank]])

        def p2_Y2c(t):
            tok0, n = tile_info(t)
            x = xt[t]
            S.op("dve", lambda e: e.tensor_tensor(out=x[:n, :], in0=psf(3, 2)[:n, :], in1=x[:n, :], op=ALU.add),
                 reads=[pb[3], pb[4], x.b], writes=[x.b])

        p2_Xa(ORD[0])
        p2_Xb(ORD[0])
        for q in range(3):
            p2_cum(ORD[0], q)
        for i, t in enumerate(ORD):
            nx = ORD[i + 1] if i + 1 < NT else None
            pv_ = ORD[i - 1] if i > 0 else None
            p2_H(t, 0)
            p2_cum(t, 3)
            if pv_ is not None:
                p2_Y2a(pv_)
            p2_H(t, 1)
            if nx is not None:
                p2_Xa(nx)
            p2_H(t, 2)
            if pv_ is not None:
                p2_Y2b(pv_)
            p2_H(t, 3)
            if pv_ is not None:
                p2_Y2c(pv_)
            p2_Y1(t)
            if nx is not None:
                p2_Xb(nx)
                for q in range(3):
                    p2_cum(nx, q)
        p2_Y2a(ORD[-1])
        p2_Y2b(ORD[-1])
        p2_Y2c(ORD[-1])
        S.barrier()
        A.release(m2)
        if stop_after == "P2":
            return True

        m3 = A.mark()
        gB = A.alloc("gB", [128, D], F32)
        S.dma("sp", gB[:], gvecs[2 * l + 1, :, :], writes=[gB.b])
        Wg = chunked(A.alloc("Wg", [128, 8, 1168], BF16), 8)
        WoG = chunked(A.alloc("WoG", [128, 3, D], BF16), 3)
        S0g = A.alloc("S0g", [48, NS, 4, 96], F32)
        glg = A.alloc("glg", [128, 96], F32)
        glrT = A.alloc("glrT", [32, 128], F32)
        lg = A.alloc("lg", [128, 192], F32)
        ex = [A.alloc("ex%d" % i, [128, 192], F32) for i in range(3)]
        qin = A.alloc("qin", [128, 192], BF16)
        kin = A.alloc("kin", [128, 192], BF16)
        ken = A.alloc("ken", [128, 192], BF16)
        vg = A.alloc("vg", [128, 4, 96], BF16)
        sgg = A.alloc("sgg", [128, 384], F32)
        qkT = A.alloc("qkT", [48, 8, 128], BF16)
        qinL = [qin, A.alloc("qin1", [128, 192], BF16)]
        kinL = [kin, A.alloc("kin1", [128, 192], BF16)]
        kenL = [ken, A.alloc("ken1", [128, 192], BF16)]
        vgL = [vg, A.alloc("vg1", [128, 4, 96], BF16)]
        qTf = A.alloc("qTf", [48, 4, 64], F32)
        scg = A.alloc("scg", [128, 4, 128], BF16)
        Sg = A.alloc("Sg", [48, 4, 96], F32)
        Sgb = A.alloc("Sgb", [48, 4, 96], BF16)
        dec = A.alloc("dec", [48, 4, NS], F32)
        ocg = A.alloc("ocg", [96, 4, 64], F32)
        og1 = A.alloc("og1", [128, 4, 96], F32)
        og2 = A.alloc("og2", [128, 4, 96], F32)
        sgt = A.alloc("sgt", [128, 16], F32)
        mixg = A.alloc("mixg", [128, 384], BF16)
        mixTg = A.alloc("mixTg", [128, 3, 128], BF16)
        kesg = A.alloc("kesg", [64, 192], BF16)
        qkTL = [qkT, A.alloc("qkT1", [48, 8, 128], BF16)]
        scgL = [scg, A.alloc("scg1", [128, 4, 128], BF16)]
        mixgL = [mixg, A.alloc("mixg1", [128, 384], BF16)]
        mixTgL = [mixTg, A.alloc("mixTg1", [128, 3, 128], BF16)]
        gatew = prv("gatew")

        for kc in range(8):
            load_cast(Wg[:, kc, :], w_in[l, kc * 128:(kc + 1) * 128, 1792:2960], Wg.b, kc)
        for kc in range(3):
            load_cast(WoG[:, kc, :], w_out[l, 640 + kc * 128:640 + (kc + 1) * 128, :], WoG.b, kc)
        S.dma("sp", S0g[:].rearrange("p s h v -> p (s h v)"), gla_s0[l, :, :], writes=[S0g.b])
        S.dma("sp", glg[:], glag[l, :, :], writes=[glg.b])
        S.op("pool", lambda e: e.memset(Sg[:], 0.0), writes=[Sg.b])
        S.op("pool", lambda e: e.memset(Sgb[:], 0.0), writes=[Sgb.b])
        S.op("pool", lambda e: e.memset(glrT[:], 1.0), writes=[glrT.b])

        sggL = [sgg, A.alloc("sgg1", [128, 384], F32)]
        decL = [dec, A.alloc("dec1", [48, 4, NS], F32)]
        widths = [512, 512, 144]

        def p3_A(t):
            tok0, n = tile_info(t)
            for bank in range(3):
                for kc in range(8):
                    S.op("pe", lambda e, bank=bank, kc=kc: e.matmul(
                        PS[:n, bank, 0:widths[bank]], lhsT=hT[:, kc, tok0:tok0 + n],
                        rhs=Wg[:, kc, bank * 512:bank * 512 + widths[bank]], start=(kc == 0), stop=(kc == 7)),
                        reads=[hTb[t], Wg.b], writes=[pb[bank]])

        def p3_B1(t):
            tok0, n = tile_info(t)
            sm = 0 if t < NPT else 1
            dec = decL[PAR[t]]
            gT = psf(2)[0:16, 256:384]
            for kc in range(8):
                S.op("pe", lambda e, kc=kc: e.matmul(gT[:, :n], lhsT=Wg[:, kc, 1152:1168], rhs=hT[:, kc, tok0:tok0 + n],
                                                    start=(kc == 0), stop=(kc == 7)), reads=[hTb[t], Wg.b], writes=[pb[2]])
            S.op("dve", lambda e: e.tensor_copy(out=glrT[0:16, :n], in_=gT[:, :n]), reads=[pb[2]], writes=[glrT.b])
            xg = psf(3)[:, 0:192]
            S.op("pe", lambda e: e.matmul(xg[:n, :], lhsT=glrT[0:17, :n], rhs=gatew[0:17, :], start=True, stop=True),
                 reads=[glrT.b, prm.b], writes=[pb[3]])
            S.op("act", lambda e: e.activation(out=ex[0][:n, :], in_=xg[:n, :], func=AF.Exp, scale=-1.0), reads=[pb[3]], writes=[ex[0].b])
            S.op("act", lambda e: e.activation(out=ex[1][:n, :], in_=ex[0][:n, :], func=AF.Ln, bias=1.0, scale=1.0),
                 reads=[ex[0].b], writes=[ex[1].b])
            S.op("dve", lambda e: e.tensor_scalar(out=lg[:n, :], in0=ex[1][:n, :], scalar1=-1.0 / 16, scalar2=None, op0=ALU.mult),
                 reads=[ex[1].b], writes=[lg.b])
            bcu = psf(4)[:, 0:384].rearrange("p (a d) -> p a d", a=2)
            S.op("pe", lambda e: e.matmul(bcu[:n, 0, :], lhsT=tri_f[:n, sm, :n], rhs=lg[:n, :], start=True, stop=True),
                 reads=[cc.b, lg.b], writes=[pb[4]])
            S.op("pe", lambda e: e.matmul(bcu[:n, 1, :], lhsT=upp_f[:n, sm, :n], rhs=lg[:n, :], start=True, stop=True),
                 reads=[cc.b, lg.b], writes=[pb[4]])
            bE = psf(3)[0:48, 192:192 + 64].rearrange("p (h s) -> p h s", h=4)
            ncol = 2 if sm == 0 else NS
            for h in range(4):
                if sm == 0:
                    rhs_ap = cc[:n, CC_OFF["tri"][0] + 126:CC_OFF["tri"][0] + 128]
                else:
                    rhs_ap = seqind[:n, :]
                S.op("pe", lambda e, h=h, rhs_ap=rhs_ap: e.matmul(bE[:, h, 0:ncol], lhsT=lg[:n, h * 48:(h + 1) * 48], rhs=rhs_ap,
                                                                 start=True, stop=True), reads=[lg.b, cc.b], writes=[pb[3]])
            S.op("act", lambda e: e.activation(out=ex[0][:n, :], in_=bcu[:n, 0, :], func=AF.Exp), reads=[pb[4]], writes=[ex[0].b])
            S.op("act", lambda e: e.activation(out=ex[1][:n, :], in_=bcu[:n, 0, :], func=AF.Exp, scale=-1.0), reads=[pb[4]], writes=[ex[1].b])
            S.op("act", lambda e: e.activation(out=ex[2][:n, :], in_=bcu[:n, 1, :], func=AF.Exp), reads=[pb[4]], writes=[ex[2].b])
            S.op("act", lambda e: e.activation(out=dec[:, :, 0:ncol], in_=bE[:, :, 0:ncol], func=AF.Exp), reads=[pb[3]], writes=[dec.b])

        def p3_B2(t):
            tok0, n = tile_info(t)
            sgg_ = sggL[PAR[t]]
            qin, kin, ken, vg = qinL[PAR[t]], kinL[PAR[t]], kenL[PAR[t]], vgL[PAR[t]]
            flat = psf(0, 3)
            S.op("dve", lambda e: e.scalar_tensor_tensor(out=qin[:n, :], in0=flat[:n, 0:192], scalar=48.0 ** -0.5, in1=ex[0][:n, :],
                                                         op0=ALU.mult, op1=ALU.mult), reads=[pb[0], ex[0].b], writes=[qin.b])
            S.op("dve", lambda e: e.tensor_tensor(out=kin[:n, :], in0=flat[:n, 192:384], in1=ex[1][:n, :], op=ALU.mult),
                 reads=[pb[0], ex[1].b], writes=[kin.b])
            S.op("dve", lambda e: e.tensor_tensor(out=ken[:n, :], in0=flat[:n, 192:384], in1=ex[2][:n, :], op=ALU.mult),
                 reads=[pb[0], ex[2].b], writes=[ken.b])
            S.op("act", lambda e: e.activation(out=vg[:n].rearrange("p h v -> p (h v)"), in_=flat[:n, 384:768], func=AF.Copy),
                 reads=[pb[0], pb[1]], writes=[vg.b])
            S.op("act", lambda e: e.activation(out=sgg_[:n, :], in_=flat[:n, 768:1152], func=AF.Exp, scale=-1.0),
                 reads=[pb[1], pb[2]], writes=[sgg_.b])
            S.op("act", lambda e: e.activation(out=sgg_[:n, :], in_=sgg_[:n, :], func=AF.Ln, bias=1.0, scale=1.0),
                 reads=[sgg_.b], writes=[sgg_.b])
            S.op("act", lambda e: e.activation(out=sgg_[:n, :], in_=sgg_[:n, :], func=AF.Exp, scale=-1.0),
                 reads=[sgg_.b], writes=[sgg_.b])
            S.op("dve", lambda e: e.tensor_tensor(out=sgg_[:n, :], in0=flat[:n, 768:1152], in1=sgg_[:n, :], op=ALU.mult),
                 reads=[pb[1], pb[2], sgg_.b], writes=[sgg_.b])
            S.op("pool", lambda e: e.tensor_tensor(out=sgg_[:n, :].rearrange("p (h v) -> p h v", h=4),
                                                   in0=sgg_[:n, :].rearrange("p (h v) -> p h v", h=4),
                                                   in1=glg[:n, None, :].broadcast_to([n, 4, 96]), op=ALU.mult),
                 reads=[sgg_.b, glg.b], writes=[sgg_.b])

        def p3_C(t):
            tok0, n = tile_info(t)
            sm = 0 if t < NPT else 1
            qin, kin, ken, vg = qinL[PAR[t]], kinL[PAR[t]], kenL[PAR[t]], vgL[PAR[t]]
            qkT, scg = qkTL[PAR[t]], scgL[PAR[t]]
            tpg = psbf(5).rearrange("p (a i) -> p a i", a=8)
            for h in range(4):
                S.op("pe", lambda e, h=h: e.transpose(out=tpg[0:48, h, :n], in_=qin[:n, h * 48:(h + 1) * 48], identity=ident_bf[:n, :n]),
                     reads=[qin.b, ident_bf.b], writes=[pb[5]])
                S.op("pe", lambda e, h=h: e.transpose(out=tpg[0:48, 4 + h, :n], in_=kin[:n, h * 48:(h + 1) * 48], identity=ident_bf[:n, :n]),
                     reads=[kin.b, ident_bf.b], writes=[pb[5]])
            S.op("act", lambda e: e.activation(out=qkT[:, :, :n], in_=tpg[0:48, :, :n], func=AF.Copy), reads=[pb[5]], writes=[qkT.b])
            if sm == 1:
                S.op("dve", lambda e: e.tensor_copy(out=qTf[:, :, :n], in_=tpg[0:48, 0:4, :n]), reads=[pb[5]], writes=[qTf.b])
            scp = psf(6).rearrange("p (h i) -> p h i", h=4)
            for h in range(4):
                S.op("pe", lambda e, h=h: e.matmul(scp[:n, h, :n], lhsT=qkT[0:48, 4 + h, :n], rhs=qkT[0:48, h, :n], start=True, stop=True),
                     reads=[qkT.b], writes=[pb[6]])
            S.op("dve", lambda e: e.tensor_tensor(out=scg[:n, :, :n], in0=scp[:n, :, :n],
                                                  in1=tri_f[:n, sm:sm + 1, :n].broadcast_to([n, 4, n]), op=ALU.mult),
                 reads=[pb[6], cc.b], writes=[scg.b])
            Og = psf(7)[:, 0:384].rearrange("p (h v) -> p h v", h=4)
            if sm == 1:
                OCT = psf(4)[:, 0:256].rearrange("p (h i) -> p h i", h=4)
                for s in range(NS):
                    for h in range(4):
                        S.op("pe", lambda e, s=s, h=h: e.matmul(OCT[0:96, h, 4 * s:4 * s + 4], lhsT=S0g[0:48, s, h, :],
                                                               rhs=qTf[0:48, h, 4 * s:4 * s + 4], start=True, stop=True),
                             reads=[S0g.b, qTf.b], writes=[pb[4]])
                S.op("act", lambda e: e.activation(out=ocg[:], in_=OCT[0:96, :, :], func=AF.Copy), reads=[pb[4]], writes=[ocg.b])
            for h in range(4):
                S.op("pe", lambda e, h=h: e.matmul(Og[:n, h, :], lhsT=scg[:n, h, :n], rhs=vg[:n, h, :], start=True, stop=False),
                     reads=[scg.b, vg.b], writes=[pb[7]])
                if sm == 0:
                    S.op("pe", lambda e, h=h: e.matmul(Og[:n, h, :], lhsT=qkT[0:48, h, :n], rhs=Sgb[0:48, h, :], start=False, stop=True),
                         reads=[qkT.b, Sgb.b], writes=[pb[7]])
                else:
                    S.op("pe", lambda e, h=h: e.matmul(Og[:n, h, :], lhsT=ocg[0:96, h, 0:64], rhs=ident_f[0:96, 0:96], start=False, stop=True),
                         reads=[ocg.b, cc.b], writes=[pb[7]])

        def p3_L(t):
            tok0, n = tile_info(t)
            sgg_ = sggL[PAR[t]]
            mixg = mixgL[PAR[t]]
            Og = psf(7)[:, 0:384].rearrange("p (h v) -> p h v", h=4)
            Ogf = psf(7)[:n, 0:384]
            S.op("act", lambda e: e.activation(out=og1[:n].rearrange("p h v -> p (h v)"), in_=Ogf, func=AF.Square), reads=[pb[7]], writes=[og1.b])
            S.op("dve", lambda e: e.tensor_reduce(out=sgt[:n, 0:4], in_=og1[:n], axis=AX.X, op=ALU.add), reads=[og1.b], writes=[sgt.b])
            S.op("dve", lambda e: e.tensor_scalar(out=sgt[:n, 4:8], in0=sgt[:n, 0:4], scalar1=1.0 / 96, scalar2=EPS, op0=ALU.mult, op1=ALU.add),
                 reads=[sgt.b], writes=[sgt.b])
            S.op("pool", lambda e: e.tensor_tensor(out=sgt[:n, 8:12], in0=sgt[:n, 4:8], in1=mhalf[:n, 0:4], op=ALU.pow),
                 reads=[sgt.b, mhalf.b], writes=[sgt.b])
            S.op("dve", lambda e: e.tensor_tensor(out=og2[:n], in0=Og[:n], in1=sgt[:n, 8:12, None].broadcast_to([n, 4, 96]), op=ALU.mult),
                 reads=[pb[7], sgt.b], writes=[og2.b])
            S.op("dve", lambda e: e.tensor_tensor(out=mixg[:n, :], in0=og2[:n].rearrange("p h v -> p (h v)"), in1=sgg_[:n, :], op=ALU.mult),
                 reads=[og2.b, sgg_.b], writes=[mixg.b], safe=True)

        def p3_G(t):
            tok0, n = tile_info(t)
            x = xt[t]
            mixg, mixTg = mixgL[PAR[t]], mixTgL[PAR[t]]
            mp = psbf(3)[:, 512:896].rearrange("p (a i) -> p a i", a=3)
            for a in range(3):
                S.op("pe", lambda e, a=a: e.transpose(out=mp[:, a, :n], in_=mixg[:n, a * 128:(a + 1) * 128], identity=ident_bf[:n, :n]),
                     reads=[mixg.b, ident_bf.b], writes=[pb[3]])
            S.op("act", lambda e: e.activation(out=mixTg[:, :, :n], in_=mp[:, :, :n], func=AF.Copy), reads=[pb[3]], writes=[mixTg.b])
            for bank in range(2):
                for a in range(3):
                    S.op("pe", lambda e, bank=bank, a=a: e.matmul(PS[:n, 5 + bank, :], lhsT=mixTg[:, a, :n],
                                                                 rhs=WoG[:, a, bank * 512:(bank + 1) * 512], start=(a == 0), stop=(a == 2)),
                         reads=[mixTg.b, WoG.b], writes=[pb[5 + bank]])
            S.op("dve", lambda e: e.tensor_tensor(out=x[:n, :], in0=psf(5, 2)[:n, :], in1=x[:n, :], op=ALU.add),
                 reads=[pb[5], pb[6], x.b], writes=[x.b])

        def p3_H(t):
            tok0, n = tile_info(t)
            sm = 0 if t < NPT else 1
            dec = decL[PAR[t]]
            qin, kin, ken, vg = qinL[PAR[t]], kinL[PAR[t]], kenL[PAR[t]], vgL[PAR[t]]
            if sm == 0:
                KVg = psf(4)[0:48, 0:384].rearrange("p (h v) -> p h v", h=4)
                for h in range(4):
                    S.op("pe", lambda e, h=h: e.matmul(KVg[:, h, :], lhsT=ken[:n, h * 48:(h + 1) * 48], rhs=vg[:n, h, :], start=True, stop=True),
                         reads=[ken.b, vg.b], writes=[pb[4]])
                for h in range(4):
                    S.op("dve", lambda e, h=h: e.scalar_tensor_tensor(out=Sg[:, h, :], in0=Sg[:, h, :], scalar=dec[:, h, 1:2],
                                                                     in1=KVg[:, h, :], op0=ALU.mult, op1=ALU.add),
                         reads=[Sg.b, dec.b, pb[4]], writes=[Sg.b])
                S.op("act", lambda e: e.activation(out=Sgb[:], in_=Sg[:], func=AF.Copy), reads=[Sg.b], writes=[Sgb.b])
                if t == NPT - 1:
                    S.dma("sp", o_gla_p[l, :, :], Sg[:].rearrange("p h v -> p (h v)"), reads=[Sg.b])
            else:
                for s in range(NS):
                    bank = 4 + (s % 2)
                    KVg = psf(bank)[0:48, 0:384].rearrange("p (h v) -> p h v", h=4)
                    S.op("dve", lambda e, s=s: e.tensor_scalar(out=kesg[:, :], in0=ken[:64, :], scalar1=seqind[:64, s:s + 1], scalar2=None,
                                                               op0=ALU.mult), reads=[ken.b, cc.b], writes=[kesg.b])
                    for h in range(4):
                        S.op("pe", lambda e, h=h, KVg=KVg: e.matmul(KVg[:, h, :], lhsT=kesg[:64, h * 48:(h + 1) * 48], rhs=vg[:64, h, :],
                                                                   start=True, stop=True), reads=[kesg.b, vg.b], writes=[pb[bank]])
                    for h in range(4):
                        S.op("dve", lambda e, s=s, h=h, KVg=KVg: e.scalar_tensor_tensor(
                            out=S0g[:, s, h, :], in0=S0g[:, s, h, :], scalar=dec[:, h, s:s + 1], in1=KVg[:, h, :],
                            op0=ALU.mult, op1=ALU.add), reads=[S0g.b, dec.b, pb[bank]], writes=[S0g.b])
                S.dma("sp", o_gla_s[l, :, :], S0g[:].rearrange("p s h v -> p (s h v)"), reads=[S0g.b])

        p3_B1(ORD[0])
        p3_A(ORD[0])
        p3_B2(ORD[0])
        p3_B1(ORD[1])
        for i, t in enumerate(ORD):
            nx = ORD[i + 1] if i + 1 < NT else None
            nx2 = ORD[i + 2] if i + 2 < NT else None
            p3_C(t)
            if i > 0:
                norm_tile(ORD[i - 1], gB, bank=5)
            p3_H(t)
            if nx is not None:
                p3_A(nx)
            p3_L(t)
            if nx is not None:
                p3_B2(nx)
            if nx2 is not None:
                p3_B1(nx2)
            p3_G(t)
        norm_tile(ORD[-1], gB, bank=5)
        S.barrier()
        A.release(m3)
        if stop_after == "P3":
            return True

        m5 = A.mark()
        ring = [(chunked(A.alloc("Wa%d" % i, [128, 8, 512], BF16), 8), chunked(A.alloc("Wgt%d" % i, [128, 8, 512], BF16), 8),
                 chunked(A.alloc("Wo%d" % i, [128, 4, D], BF16), 4)) for i in range(2)]
        asb = [A.alloc("asb%d" % i, [128, 514], F32) for i in range(2)]
        cv = [A.alloc("cv%d" % i, [128, 512], F32) for i in range(2)]
        ge = A.alloc("ge", [128, 512], F32)
        actT = [A.alloc("actT%d" % i, [128, 4, 512], BF16) for i in range(2)]
        carry = A.alloc("carry", [128, NFC, 2], F32)
        a_s = A.alloc("a_s", [128, NS, 6], F32)
        cvs = [A.alloc("cvs%d" % i, [128, NS, 4], F32) for i in range(2)]
        cn = A.alloc("cn", [128, NFC, 34], F32)
        cs0 = A.alloc("cs0", [128, NFC, NS, 2], F32)
        cw = prv("cw").rearrange("p (c k) -> p c k", c=NFC)
        cb = prv("cb")
        S.dma("sp", cs0[:].rearrange("p c s j -> p (c s j)"), conv_s0[l, :, :], writes=[cs0.b])
        S.op("pool", lambda e: e.memset(carry[:], 0.0), writes=[carry.b])
        it_box = [0]

        def ffn_chunk(cl, c0, G, tg, tk0, nt_, tiles, AT, Wa, Wgt):
            c = c0 + cl
            ba = (cl % 2) * 2
            aps = PS[:, ba, :]
            gps = PS[:, ba + 1, :]
            hrd = [hTb[tt] for tt in tiles]
            for kc in range(8):
                S.op("pe", lambda e, kc=kc: e.matmul(aps[:, :nt_], lhsT=Wa[:, kc, cl * 128:(cl + 1) * 128],
                                                    rhs=hT[:, kc, tk0:tk0 + nt_], start=(kc == 0), stop=(kc == 7)),
                     reads=[Wa.b] + hrd, writes=[pb[ba]])
            for kc in range(8):
                S.op("pe", lambda e, kc=kc: e.matmul(gps[:, :nt_], lhsT=Wgt[:, kc, cl * 128:(cl + 1) * 128],
                                                    rhs=hT[:, kc, tk0:tk0 + nt_], start=(kc == 0), stop=(kc == 7)),
                     reads=[Wgt.b] + hrd, writes=[pb[ba + 1]])
            w0 = cw[:, c, 0:1]
            w1 = cw[:, c, 1:2]
            w2 = cw[:, c, 2:3]
            bb = cb[:, c:c + 1]
            if tg < 4:
                a_t = asb[cl % 2]
                c_t = cv[cl % 2]
                S.op("act", lambda e: e.activation(out=a_t[:, 0:2], in_=carry[:, c, :], func=AF.Copy), reads=[carry.b], writes=[a_t.b])
                S.op("act", lambda e: e.activation(out=a_t[:, 2:514], in_=aps[:, :], func=AF.Copy), reads=[pb[ba]], writes=[a_t.b])
                S.op("act", lambda e: e.activation(out=carry[:, c, :], in_=a_t[:, 512:514], func=AF.Copy), reads=[a_t.b], writes=[carry.b])
                if tg == 3:
                    S.op("act", lambda e: e.activation(out=cn[:, c, 32:34], in_=a_t[:, 512:514], func=AF.Copy), reads=[a_t.b], writes=[cn.b])
                S.op("dve", lambda e: e.tensor_scalar(out=c_t[:, :], in0=aps[:, :], scalar1=w2, scalar2=bb, op0=ALU.mult, op1=ALU.add),
                     reads=[pb[ba], prm.b], writes=[c_t.b])
                S.op("dve", lambda e: e.scalar_tensor_tensor(out=c_t[:, :], in0=a_t[:, 1:513], scalar=w1, in1=c_t[:, :],
                                                             op0=ALU.mult, op1=ALU.add), reads=[a_t.b, prm.b, c_t.b], writes=[c_t.b], safe=True)
                S.op("dve", lambda e: e.scalar_tensor_tensor(out=c_t[:, :], in0=a_t[:, 0:512], scalar=w0, in1=c_t[:, :],
                                                             op0=ALU.mult, op1=ALU.add), reads=[a_t.b, prm.b, c_t.b], writes=[c_t.b], safe=True)
                S.op("act", lambda e: e.activation(out=ge[:, :], in_=c_t[:, :], func=AF.Gelu_apprx_tanh), reads=[c_t.b], writes=[ge.b])
                S.op("dve", lambda e: e.tensor_tensor(out=AT[:, cl, :], in0=gps[:, :], in1=ge[:, :], op=ALU.mult),
                     reads=[pb[ba + 1], ge.b], writes=[AT.b])
            else:
                c_t = cvs[cl % 2]
                S.op("act", lambda e: e.activation(out=a_s[:, :, 0:2], in_=cs0[:, c, :, :], func=AF.Copy), reads=[cs0.b], writes=[a_s.b])
                S.op("act", lambda e: e.activation(out=a_s[:, :, 2:6], in_=aps[:, 0:64].rearrange("p (s t) -> p s t", s=NS),
                                                   func=AF.Copy), reads=[pb[ba]], writes=[a_s.b])
                S.op("act", lambda e: e.activation(out=cn[:, c, 0:32].rearrange("p (s j) -> p s j", s=NS), in_=a_s[:, :, 4:6], func=AF.Copy),
                     reads=[a_s.b], writes=[cn.b])
                S.op("dve", lambda e: e.tensor_scalar(out=c_t[:], in0=a_s[:, :, 2:6], scalar1=w2, scalar2=bb, op0=ALU.mult, op1=ALU.add),
                     reads=[a_s.b, prm.b], writes=[c_t.b])
                S.op("dve", lambda e: e.scalar_tensor_tensor(out=c_t[:], in0=a_s[:, :, 1:5], scalar=w1, in1=c_t[:],
                                                             op0=ALU.mult, op1=ALU.add), reads=[a_s.b, prm.b, c_t.b], writes=[c_t.b])
                S.op("dve", lambda e: e.scalar_tensor_tensor(out=c_t[:], in0=a_s[:, :, 0:4], scalar=w0, in1=c_t[:],
                                                             op0=ALU.mult, op1=ALU.add), reads=[a_s.b, prm.b, c_t.b], writes=[c_t.b])
                S.op("act", lambda e: e.activation(out=ge[:, 0:64], in_=c_t[:].rearrange("p s t -> p (s t)"),
                                                   func=AF.Gelu_apprx_tanh), reads=[c_t.b], writes=[ge.b])
                S.op("dve", lambda e: e.tensor_tensor(out=AT[:, cl, 0:64], in0=gps[:, 0:64], in1=ge[:, 0:64], op=ALU.mult),
                     reads=[pb[ba + 1], ge.b], writes=[AT.b])

        def ffn_down(ti, tt, G, AT, Wo):
            n = 128 if tt < NPT else 64
            bd = 4 + (ti % 2) * 2
            for bank in range(2):
                for cl in range(G):
                    S.op("pe", lambda e, bank=bank, cl=cl: e.matmul(
                        PS[:n, bd + bank, :], lhsT=AT[:, cl, ti * 128:ti * 128 + n], rhs=Wo[:, cl, bank * 512:(bank + 1) * 512],
                        start=(cl == 0), stop=(cl == G - 1)), reads=[AT.b, Wo.b], writes=[pb[bd + bank]])
            xx = xt[tt]
            S.op("dve", lambda e: e.tensor_tensor(out=xx[:n, :], in0=psf(bd, 2)[:n, :], in1=xx[:n, :], op=ALU.add),
                 reads=[pb[bd], pb[bd + 1], xx.b], writes=[xx.b])

        def ffn_load(gi):
            c0, G = FGROUPS[gi]
            Wa, Wgt, Wo = ring[gi % 2]
            f0 = c0 * 128
            for kc in range(8):
                load_cast(Wa[:, kc, 0:G * 128], ffn_w_in[l, kc * 128:(kc + 1) * 128, f0:f0 + G * 128], Wa.b, kc)
                load_cast(Wgt[:, kc, 0:G * 128], ffn_w_in[l, kc * 128:(kc + 1) * 128, DFF + f0:DFF + f0 + G * 128], Wgt.b, kc)
            for cl in range(G):
                load_cast(Wo[:, cl, :], ffn_w_out[l, f0 + cl * 128:f0 + (cl + 1) * 128, :], Wo.b, cl)

        def ffn_group(gi, c0, G):
            Wa, Wgt, Wo = ring[gi % 2]
            if gi + 1 < len(FGROUPS):
                ffn_load(gi + 1)
            pend = []
            for tg in range(5):
                tk0 = tg * 512
                nt_ = 512 if tg < 4 else 64
                tiles = list(range(tg * 4, tg * 4 + 4)) if tg < 4 else [NPT]
                AT = actT[it_box[0] % 2]
                it_box[0] += 1
                for cl in range(G):
                    ffn_chunk(cl, c0, G, tg, tk0, nt_, tiles, AT, Wa, Wgt)
                    if pend:
                        ffn_down(*pend.pop(0))
                while pend:
                    ffn_down(*pend.pop(0))
                pend = [(ti, tt, G, AT, Wo) for ti, tt in enumerate(tiles)]
            while pend:
                ffn_down(*pend.pop(0))

        ffn_load(0)
        for gi, (c0, G) in enumerate(FGROUPS):
            ffn_group(gi, c0, G)
        S.dma("sp", o_conv[l, :, :], cn[:].rearrange("p c j -> p (c j)"), reads=[cn.b])
        S.barrier()
        A.release(m5)

    for l in range(DEPTH):
        if layer(l):
            break

    if stop_after is None:
        S.dma("sp", gA[:], gvecs[4, :, :], writes=[gA.b])
        m6 = A.mark()
        yo = [A.alloc("yo%d" % i, [128, D], F32) for i in range(2)]
        for t in range(NT):
            tok0, n = tile_info(t)
            x = xt[t]
            yy = yo[t % 2]
            norm_stats(t)
            k = t % 2
            S.op("dve", lambda e, x=x, n=n, yy=yy, k=k: e.scalar_tensor_tensor(out=yy[:n, :], in0=x[:n, :], scalar=rs[:n, 2 * k + 1:2 * k + 2],
                                                                              in1=gA[:n, :], op0=ALU.mult, op1=ALU.mult),
                 reads=[x.b, rsb[k], gA.b], writes=[yy.b])
            dst = y_p[tok0:tok0 + n, :] if t < NPT else y_s[:, :]
            S.dma("sp", dst, yy[:n, :], reads=[yy.b])
        A.release(m6)
    S.finalize()
    return nc, S


_CACHE = {}


def kernel(x_prompt, x_sample, state_ret, state_s5_re, state_s5_im, state_gla, state_ffn_conv,
           norm_mix_g, w_in, ret_norm_g, ret_norm_b,
           s5_lambda_re, s5_lambda_im, s5_log_dt, s5_b_re, s5_b_im, s5_c_re, s5_c_im, s5_d,
           s5_glu_w, s5_glu_b, gla_gate_w, gla_gate_b, gla_norm_g, w_out,
           norm_ffn_g, ffn_w_in, ffn_conv_w, ffn_conv_b, ffn_w_out, norm_final_g, _stop_after=None, _ncores=8, _trace=False):
    f32 = np.float32
    a = lambda v: np.ascontiguousarray(np.asarray(v, dtype=f32))
    x_prompt, x_sample = a(x_prompt), a(x_sample)
    state_ret, state_s5_re, state_s5_im = a(state_ret), a(state_s5_re), a(state_s5_im)
    state_gla, state_ffn_conv = a(state_gla), a(state_ffn_conv)
    w_in, w_out, ffn_w_in, ffn_w_out, s5_glu_w = a(w_in), a(w_out), a(ffn_w_in), a(ffn_w_out), a(s5_glu_w)
    n_cores = 8
    key = _stop_after
    if key not in _CACHE:
        _CACHE[key] = build_nc(_stop_after)
    nc, _ = _CACHE[key]
    cc, cr, rope_t = _const_tables()

    gvecs = np.stack([np.broadcast_to(a(v)[None, :], (128, D)) for v in
                      (norm_mix_g[0], norm_ffn_g[0], norm_mix_g[1], norm_ffn_g[1], norm_final_g)]).astype(f32)
    retgb = np.stack([np.concatenate([np.broadcast_to(a(ret_norm_g[l])[None, :], (128, 384)),
                                      np.broadcast_to(a(ret_norm_b[l])[None, :], (128, 384))], axis=1)
                      for l in range(DEPTH)]).astype(f32)
    glag = np.stack([np.broadcast_to(a(gla_norm_g[l])[None, :], (128, 96)) for l in range(DEPTH)]).astype(f32)
    prm = np.zeros((DEPTH, 128, PR_N), dtype=f32)

    def put(l, name, arr):
        o, w = PR_OFF[name]
        arr = np.asarray(arr, dtype=f32).reshape(arr.shape[0], -1)
        prm[l, :arr.shape[0], o:o + arr.shape[1]] = arr

    for l in range(DEPTH):
        def gp_layout(v):
            return a(v).reshape(8, 2, 64).transpose(1, 2, 0).reshape(128, 8)
        put(l, "lamr", gp_layout(s5_lambda_re[l]))
        put(l, "lami", gp_layout(s5_lambda_im[l]))
        put(l, "ldt", gp_layout(np.broadcast_to(a(s5_log_dt[l])[:, None], (16, 64))))
        for nm, src in (("br", s5_b_re), ("bi", s5_b_im)):
            v = a(src[l]).reshape(8, 2, 64, 16).transpose(1, 2, 0, 3).reshape(128, 8 * 16)
            put(l, nm, v)
        for nm, src in (("ctr", s5_c_re), ("cti", s5_c_im)):
            v = a(src[l]).reshape(8, 2, 16, 64)
            blk = np.zeros((2, 64, 8, 2, 16), dtype=f32)
            for g2 in range(2):
                blk[g2, :, :, g2, :] = v[:, g2, :, :].transpose(2, 0, 1)
            put(l, nm, blk.reshape(128, 8 * 32))
        put(l, "dsk", a(s5_d[l]).reshape(2, 128).T)
        put(l, "glub", a(s5_glu_b[l]).reshape(2, 128).T)
        put(l, "gatew", np.concatenate([a(gla_gate_w[l]), a(gla_gate_b[l])[None, :]], axis=0))
        put(l, "cw", a(ffn_conv_w[l]).reshape(3, NFC, 128).transpose(2, 1, 0).reshape(128, NFC * 3))
        put(l, "cb", a(ffn_conv_b[l]).reshape(NFC, 128).T)

    in_maps = []
    for c in range(n_cores):
        sl = slice(c * NS, (c + 1) * NS)
        rs0 = state_ret[:, sl].reshape(DEPTH, NS, 3, 2, 64, 64).transpose(0, 3, 4, 1, 2, 5).reshape(DEPTH, 128, NS * 192)
        gs0 = state_gla[:, sl].transpose(0, 3, 1, 2, 4).reshape(DEPTH, 48, NS * 384)
        h0 = np.stack([state_s5_re[:, sl], state_s5_im[:, sl]], axis=1)
        h0 = h0.reshape(DEPTH, 2, NS, 8, 2, 64).transpose(0, 1, 4, 5, 3, 2).reshape(DEPTH, 2, 128, 8 * NS)
        cs0 = state_ffn_conv[:, sl].reshape(DEPTH, NS, 2, NFC, 128).transpose(0, 4, 3, 1, 2).reshape(DEPTH, 128, NFC * NS * 2)
        in_maps.append({
            "xp": x_prompt[c], "xs": x_sample[sl].reshape(64, D),
            "w_in": w_in, "w_out": w_out, "ffn_w_in": ffn_w_in, "ffn_w_out": ffn_w_out, "glu_w": s5_glu_w,
            "gvecs": gvecs, "retgb": retgb, "glag": glag, "prm": prm, "cc": cc, "cr": cr, "rope": rope_t,
            "ret_s0": np.ascontiguousarray(rs0), "gla_s0": np.ascontiguousarray(gs0),
            "s5_h0": np.ascontiguousarray(h0), "conv_s0": np.ascontiguousarray(cs0),
        })
    if _trace:
        res = run_bass_kernel_spmd(nc, in_maps[:_ncores], core_ids=list(range(_ncores)), trace=True)
        print("EXEC_TIME_NS", res.exec_time_ns)
    else:
        res = run_bass_kernel_spmd(nc, in_maps[:_ncores], core_ids=list(range(_ncores)))
    R = list(res.results)
    while len(R) < n_cores:
        R.append(R[0])
    y_prompt = np.stack([R[c]["y_p"] for c in range(n_cores)]).astype(f32)
    y_sample = np.concatenate([R[c]["y_s"].reshape(NS, LS, D) for c in range(n_cores)], axis=0).astype(f32)
    ret_p = np.stack([R[c]["o_ret_p"].reshape(DEPTH, 2, 64, 3, 64).transpose(0, 3, 1, 2, 4).reshape(DEPTH, 6, 64, 64)
                      for c in range(n_cores)], axis=1)
    ret_s = np.concatenate([R[c]["o_ret_s"].reshape(DEPTH, 2, 64, NS, 3, 64).transpose(0, 3, 4, 1, 2, 5).reshape(DEPTH, NS, 6, 64, 64)
                            for c in range(n_cores)], axis=1)
    s5p = np.stack([R[c]["o_s5_p"].reshape(DEPTH, 2, 2, 64, 8).transpose(0, 1, 4, 2, 3).reshape(DEPTH, 2, 16, 64)
                    for c in range(n_cores)], axis=2)
    s5s = np.concatenate([R[c]["o_s5_s"].reshape(DEPTH, 2, 2, 64, 8, NS).transpose(0, 1, 5, 4, 2, 3).reshape(DEPTH, 2, NS, 16, 64)
                          for c in range(n_cores)], axis=2)
    gla_p = np.stack([R[c]["o_gla_p"].reshape(DEPTH, 48, 4, 96).transpose(0, 2, 1, 3) for c in range(n_cores)], axis=1)
    gla_s = np.concatenate([R[c]["o_gla_s"].reshape(DEPTH, 48, NS, 4, 96).transpose(0, 2, 3, 1, 4) for c in range(n_cores)], axis=1)
    cv = [R[c]["o_conv"].reshape(DEPTH, 128, NFC, 34) for c in range(n_cores)]
    conv_p = np.stack([v[:, :, :, 32:34].transpose(0, 3, 2, 1).reshape(DEPTH, 2, DFF) for v in cv], axis=1)
    conv_s = np.concatenate([v[:, :, :, 0:32].reshape(DEPTH, 128, NFC, NS, 2).transpose(0, 3, 4, 2, 1).reshape(DEPTH, NS, 2, DFF)
                             for v in cv], axis=1)
    c_ = np.ascontiguousarray
    return (c_(y_prompt), c_(y_sample), c_(ret_p.astype(f32)), c_(ret_s.astype(f32)),
            c_(s5p[:, 0].astype(f32)), c_(s5s[:, 0].astype(f32)), c_(s5p[:, 1].astype(f32)), c_(s5s[:, 1].astype(f32)),
            c_(gla_p.astype(f32)), c_(gla_s.astype(f32)), c_(conv_p.astype(f32)), c_(conv_s.astype(f32)))
```

```python
import math
import numpy as np
import concourse.bass as bass
import concourse.mybir as mybir
from concourse.bass_utils import run_bass_kernel_spmd

F32 = mybir.dt.float32
BF16 = mybir.dt.bfloat16
AF = mybir.ActivationFunctionType
ALU = mybir.AluOpType
AX = mybir.AxisListType

D = 1024
SEQ = 2048
NPT = 16
NT = 17
NTOK = 2112
NS = 16
LS = 4
DEPTH = 2
PAST = 16384
INC = 2960
DFF = 2816
NFC = 22
EPS = 1e-6
GAM = [1.0 - 2.0 ** (-5.0 - h) for h in range(6)]
FGROUPS = [(0, 4), (4, 4), (8, 4), (12, 4), (16, 3), (19, 3)]
HSLOT = [0, 3, 1, 4, 2, 5]

ENGS = ("pe", "dve", "act", "pool", "sp")
import os
REORDER = os.environ.get('K_REORDER', '1') == '1'
WINDOW = int(os.environ.get('K_WINDOW', '3000'))
LAT = float(os.environ.get('K_LAT', '0.8'))
CPW = float(os.environ.get('K_CPW', '0.05'))
STRICT = os.environ.get('K_STRICT', '1') == '1'


class Buf:
    __slots__ = ("name", "excl", "members")

    def __init__(self, name, excl=False, members=None):
        self.name = name
        self.excl = excl
        self.members = members


def _expand(bufs):
    out = []
    for b in bufs:
        if b.members:
            out.extend(b.members)
        else:
            out.append(b)
    return out


class _Probe:
    def __getattr__(self, name):
        def f(*a, **k):
            self.__dict__["call"] = (name, a, k)
            return self
        return f


def _free_elems(ap):
    n = 1
    for s in list(ap.shape)[1:]:
        n *= int(s)
    return n


def _estimate_cost(eng, fn):
    try:
        p = _Probe()
        fn(p)
        name, a, k = p.__dict__["call"]
        out = k.get("out", a[0] if a else None)
        if eng == "pe":
            if name == "transpose":
                return 0.11
            rhs = k.get("rhs")
            n = _free_elems(rhs) if rhs is not None else 128
            f32 = rhs is not None and rhs.dtype == F32
            return n / 2400.0 * (4.0 if f32 else 1.0) + 0.02
        sz = _free_elems(out) if out is not None else 128
        if eng == "dve":
            return 1.1e-3 * sz + 0.07
        if eng == "act":
            return 1.0e-3 * sz + 0.08
        if eng == "pool":
            if k.get("op", None) == ALU.pow:
                return 0.7
            return 2.4e-3 * sz + 0.06
    except Exception:
        pass
    return {"pe": 0.15, "dve": 0.4, "act": 0.45, "pool": 0.7}.get(eng, 0.5)


class Op:
    __slots__ = ("eng", "fn", "reads", "writes", "is_dma", "deps", "raw", "needs_inc", "cnt", "dsem", "dval", "safe", "cost",
                 "tw", "ar")

    def __init__(self, eng, fn, reads, writes, is_dma, safe=False):
        self.safe = safe
        self.cost = 0.5
        self.tw = frozenset(writes)
        self.ar = frozenset(reads)
        self.eng = eng
        self.fn = fn
        self.reads = reads
        self.writes = writes
        self.is_dma = is_dma
        self.deps = None
        self.raw = ()
        self.needs_inc = False
        self.cnt = 0
        self.dsem = None
        self.dval = 0


class Sched:
    def __init__(self, nc, n_dma_sems=28, same_engine_sync=False):
        self.nc = nc
        self.ops = []
        self.n_dma_sems = n_dma_sems
        self.same_engine_sync = same_engine_sync
        self.barriers = []
        self.do_reorder = False
        self.window = 600

    def op(self, eng, fn, reads=(), writes=(), safe=False):
        reads = _expand(reads)
        writes = _expand(writes)
        rd = tuple(b for b in reads if not b.excl)
        wr = tuple(writes) + tuple(b for b in reads if b.excl)
        o = Op(eng, fn, rd, wr, False, safe)
        o.tw = frozenset(writes)
        o.ar = frozenset(reads)
        o.cost = _estimate_cost(eng, fn)
        self.ops.append(o)

    def dma(self, eng, out_ap, in_ap, reads=(), writes=(), **kw):
        def fn(e, out_ap=out_ap, in_ap=in_ap, kw=kw):
            return e.dma_start(out=out_ap, in_=in_ap, **kw)
        o = Op(eng, fn, tuple(_expand(reads)), tuple(_expand(writes)), True)
        try:
            nb = _free_elems(out_ap) * (4 if out_ap.dtype == F32 else 2) * int(out_ap.shape[0])
        except Exception:
            nb = 1 << 20
        o.cost = 2.0 + nb / 1.5e5
        self.ops.append(o)

    def barrier(self):
        self.barriers.append(len(self.ops))

    def _conflict_deps(self, ops):
        last_writer = {}
        readers = {}
        deps = []
        for i, op in enumerate(ops):
            d = set()
            for b in op.reads:
                j = last_writer.get(b)
                if j is not None:
                    d.add(j)
            for b in op.writes:
                j = last_writer.get(b)
                if j is not None:
                    d.add(j)
                d.update(readers.get(b, ()))
            for b in op.reads:
                readers.setdefault(b, []).append(i)
            for b in op.writes:
                last_writer[b] = i
                readers[b] = []
            d.discard(i)
            deps.append(d)
        return deps

    def _list_schedule(self, ops, window=600, lat=LAT):
        n = len(ops)
        deps = self._conflict_deps(ops)
        users = [[] for _ in range(n)]
        ndep = [0] * n
        for i, d in enumerate(deps):
            ndep[i] = len(d)
            for j in d:
                users[j].append(i)
        cp = [0.0] * n
        for i in range(n - 1, -1, -1):
            m = 0.0
            for u in users[i]:
                v = cp[u] + lat
                if v > m:
                    m = v
            cp[i] = ops[i].cost + m
        finish = [0.0] * n
        eng_free = {}
        done = [False] * n
        order = []
        head = 0
        ready = set(i for i in range(min(n, window)) if ndep[i] == 0)
        hi = min(n, window)
        while len(order) < n:
            best = None
            bt = None
            for i in ready:
                op = ops[i]
                t0 = eng_free.get(op.eng, 0.0)
                for j in deps[i]:
                    fj = finish[j] + (lat if ops[j].eng != op.eng or ops[j].is_dma else 0.0)
                    if fj > t0:
                        t0 = fj
                key = (t0 - CPW * cp[i], i)
                if bt is None or key < bt:
                    bt = key
                    best = i
                    bt0 = t0
            if best is None:
                raise RuntimeError("list scheduler stuck")
            i = best
            ready.discard(i)
            op = ops[i]
            t0 = bt0
            if op.is_dma:
                eng_free[op.eng] = t0 + 0.6
                finish[i] = t0 + op.cost
            else:
                finish[i] = t0 + op.cost
                eng_free[op.eng] = finish[i]
            done[i] = True
            order.append(i)
            for u in users[i]:
                ndep[u] -= 1
                if ndep[u] == 0 and u < hi:
                    ready.add(u)
            while head < n and done[head]:
                head += 1
            nhi = min(n, head + window)
            for u in range(hi, nhi):
                if ndep[u] == 0:
                    ready.add(u)
            hi = max(hi, nhi)
        return [ops[i] for i in order]

    def reorder(self, window=600):
        bounds = sorted(set(self.barriers))
        segs = []
        prev = 0
        for b in bounds + [len(self.ops)]:
            segs.append(self.ops[prev:b])
            prev = b
        new_ops = []
        new_barriers = []
        for k, seg in enumerate(segs):
            if k > 0:
                new_barriers.append(len(new_ops))
            new_ops.extend(self._list_schedule(seg, window) if seg else [])
        self.ops = new_ops
        self.barriers = new_barriers

    def finalize(self):
        nc = self.nc
        if self.do_reorder:
            self.reorder(self.window)
        ops = self.ops
        engobj = {"pe": nc.tensor, "dve": nc.vector, "act": nc.scalar, "pool": nc.gpsimd, "sp": nc.sync}
        barrier_set = set(self.barriers)
        last_writer = {}
        readers = {}
        last_on_eng = {}
        open_dmas = []
        pending_barrier = {}
        dma_sem_last = [None] * self.n_dma_sems
        n_sw = self.n_dma_sems // 2
        pools = {"pool": list(range(0, n_sw)), "sp": list(range(n_sw, self.n_dma_sems)),
                 "act": list(range(n_sw, self.n_dma_sems))}
        dma_rr = {"pool": 0, "sp": 0, "act": 0}
        for i, op in enumerate(ops):
            deps = set()
            if i in barrier_set:
                bd = set(last_on_eng.values())
                for j in dma_sem_last:
                    if j is not None:
                        bd.add(j)
                last_writer = {}
                readers = {}
                for e in ENGS:
                    pending_barrier[e] = bd
            pb = pending_barrier.pop(op.eng, None)
            if pb is not None:
                deps.update(pb)
            raw = set()
            for b in op.reads:
                j = last_writer.get(b)
                if j is not None:
                    deps.add(j)
                    raw.add(j)
            op.raw = raw
            for b in op.writes:
                j = last_writer.get(b)
                if j is not None:
                    deps.add(j)
                r = readers.get(b)
                if r:
                    deps.update(r)
            if op.is_dma:
                pl = pools[op.eng]
                s = pl[dma_rr[op.eng] % len(pl)]
                dma_rr[op.eng] += 1
                op.dsem = s
                prev = dma_sem_last[s]
                if prev is not None:
                    deps.add(prev)
                    op.dval = ops[prev].dval + 16
                else:
                    op.dval = 16
                dma_sem_last[s] = i
                open_dmas.append(i)
            for b in op.reads:
                r = readers.setdefault(b, [])
                if not op.is_dma:
                    r[:] = [j for j in r if ops[j].is_dma or ops[j].eng != op.eng]
                r.append(i)
            for b in op.writes:
                last_writer[b] = i
                readers[b] = []
            deps.discard(i)
            op.deps = deps
            if not op.is_dma:
                last_on_eng[op.eng] = i

        def skip_same(d, op, j):
            if d.is_dma or op.is_dma or d.eng != op.eng:
                return False
            if d.eng == "pe":
                return True
            if STRICT:
                return not ((d.tw & op.ar) or (d.tw & op.tw) or (d.ar & op.tw))
            return j not in op.raw

        for op in ops:
            for j in op.deps:
                d = ops[j]
                if d.is_dma or skip_same(d, op, j):
                    continue
                d.needs_inc = True
        cnt = {e: 0 for e in ENGS}
        for op in ops:
            if op.is_dma:
                continue
            if op.needs_inc:
                cnt[op.eng] += 1
            op.cnt = cnt[op.eng]
        esem = {e: nc.alloc_semaphore("s_" + e) for e in ENGS}
        dsem = [nc.alloc_semaphore("d%d" % k) for k in range(self.n_dma_sems)]
        know = {e: {} for e in ENGS}
        know_dma = {e: {} for e in ENGS}
        op_know = [None] * len(ops)
        n_wait = 0
        for i, op in enumerate(ops):
            e = op.eng
            eo = engobj[e]
            k = know[e]
            kd = know_dma[e]
            for j in sorted(op.deps):
                d = ops[j]
                if d.is_dma:
                    if kd.get(d.dsem, 0) >= d.dval:
                        continue
                    eo.wait_ge(dsem[d.dsem], d.dval)
                    n_wait += 1
                    kd[d.dsem] = d.dval
                else:
                    if skip_same(d, op, j):
                        continue
                    if k.get(d.eng, 0) >= d.cnt:
                        continue
                    eo.wait_ge(esem[d.eng], d.cnt)
                    n_wait += 1
                    k[d.eng] = d.cnt
                    ok = op_know[j]
                    if ok is not None:
                        for e2, c2 in ok.items():
                            if k.get(e2, 0) < c2:
                                k[e2] = c2
            ins = op.fn(eo)
            if op.is_dma:
                ins.then_inc(dsem[op.dsem], 16)
            elif op.needs_inc:
                ins.then_inc(esem[e], 1)
                op_know[i] = dict(k)
        sp = nc.sync
        for s in range(self.n_dma_sems):
            j = dma_sem_last[s]
            if j is not None:
                sp.wait_ge(dsem[s], ops[j].dval)
        for e in ENGS:
            if e != "sp" and cnt[e] > 0:
                sp.wait_ge(esem[e], cnt[e])
        self.n_wait = n_wait
        self.counts = cnt
        return self


class Tl:
    def __init__(self, h, name):
        self.h = h
        self.b = Buf(name)

    def __getitem__(self, idx):
        return self.h[idx]


class Arena:
    def __init__(self, nc, nbytes):
        self.nc = nc
        lo, hi = nc.bump_sbuf(nbytes)
        self.lo = lo
        self.hi = hi
        self.cur = lo
        self.n = 0

    def alloc(self, name, shape, dt):
        per = 1
        for s in shape[1:]:
            per *= s
        per *= 4 if dt == F32 else 2
        per = (per + 31) // 32 * 32
        off = self.cur
        if off + per > self.hi:
            raise RuntimeError("arena overflow at %s: need %d, have %d" % (name, per, self.hi - off))
        self.cur += per
        self.n += 1
        h = self.nc.alloc_sbuf_tensor_at("%s_%d" % (name, self.n), list(shape), dt, offset=off)
        return Tl(h, name)

    def mark(self):
        return self.cur

    def release(self, m):
        self.cur = m


def _const_tables():
    f32 = np.float32
    half = 32
    inv = np.power(f32(10000.0), -(np.arange(half, dtype=f32) / f32(half))).astype(f32)
    pos = np.zeros((NT, 128), dtype=f32)
    for t in range(NPT):
        pos[t] = np.arange(t * 128, (t + 1) * 128)
    pos[NPT, :64] = PAST + (np.arange(64) % 4)
    ang = (pos[:, :, None] * inv[None, None, :]).astype(f32)
    rope = np.zeros((128, NT, 2, half), dtype=f32)
    rope[:, :, 0, :] = np.cos(ang).astype(f32).transpose(1, 0, 2)
    rope[:, :, 1, :] = np.sin(ang).astype(f32).transpose(1, 0, 2)
    g = np.array(GAM, dtype=np.float64)
    j = np.arange(128)
    dif = j[None, :] - j[:, None]
    rmask_p = np.zeros((128, 6, 128))
    for h in range(6):
        rmask_p[:, HSLOT[h], :] = np.where(dif >= 0, 0.125 * g[h] ** np.maximum(dif, 0), 0.0)
    same = (j[:, None] // 4) == (j[None, :] // 4)
    rmask_s = np.zeros((128, 6, 64))
    for h in range(6):
        m = np.where((dif >= 0) & same, 0.125 * g[h] ** np.maximum(dif, 0), 0.0)
        rmask_s[:64, HSLOT[h], :] = m[:64, :64]
    qdec_p = np.zeros((128, 3, 128))
    qdec_s = np.zeros((128, 3, 64))
    sdec = np.zeros((128, 2, 3))
    for h in range(6):
        r0 = (h % 2) * 64
        qdec_p[r0:r0 + 64, h // 2, :] = (g[h] ** (j + 1.0))[None, :]
        qdec_s[r0:r0 + 64, h // 2, :] = (g[h] ** ((j[:64] % 4) + 1.0))[None, :]
        sdec[r0:r0 + 64, 0, h // 2] = g[h] ** 128
        sdec[r0:r0 + 64, 1, h // 2] = g[h] ** 4
    kdec = np.zeros((128, 2, 6))
    for h in range(6):
        kdec[:, 0, h] = 0.125 * g[h] ** (127.0 - j)
        kdec[:64, 1, h] = 0.125 * g[h] ** (3.0 - (j[:64] % 4))
    tri = np.zeros((128, 2, 128))
    upp = np.zeros((128, 2, 128))
    tri[:, 0, :] = (dif >= 0)
    upp[:, 0, :] = (dif < 0)
    tri[:64, 1, :64] = ((dif >= 0) & same)[:64, :64]
    upp[:64, 1, :64] = ((dif < 0) & same)[:64, :64]
    seqind = np.zeros((128, 16))
    seqind[:64, :] = (j[:64, None] // 4) == np.arange(16)[None, :]
    ident = np.eye(128)
    cc = np.concatenate([a.reshape(128, -1) for a in (ident, tri, upp, seqind)], axis=1).astype(f32)
    cr = np.concatenate([a.reshape(128, -1) for a in (rmask_p, rmask_s, qdec_p, qdec_s, kdec, sdec)],
                        axis=1).astype(f32)
    rope_t = np.ascontiguousarray(rope.reshape(128, NT, 64).transpose(1, 0, 2)).astype(f32)
    return np.ascontiguousarray(cc), np.ascontiguousarray(cr), rope_t


CC_OFF = {"ident": (0, 128), "tri": (128, 256), "upp": (384, 256), "seqind": (640, 16)}
CC_N = 656
CR_OFF = {}
_o = 0
for _n, _w in (("rmask_p", 768), ("rmask_s", 384), ("qdec_p", 384), ("qdec_s", 192),
               ("kdec", 12), ("sdec", 6)):
    CR_OFF[_n] = (_o, _w)
    _o += _w
CR_N = _o
PR_OFF = {}
_o = 0
for _n, _w in (("lamr", 8), ("lami", 8), ("ldt", 8), ("br", 128), ("bi", 128), ("ctr", 256), ("cti", 256),
               ("dsk", 2), ("glub", 2), ("gatew", 192), ("cw", 66), ("cb", 22)):
    PR_OFF[_n] = (_o, _w)
    _o += _w
PR_N = _o


def build_nc(stop_after=None, reorder=REORDER, window=WINDOW):
    nc = bass.Bass("TRN2", target_bir_lowering=False)
    S = Sched(nc)
    S.do_reorder = reorder
    S.window = window

    def din(name, shape):
        return nc.dram_tensor(name, list(shape), F32, kind="ExternalInput").ap()

    def dout(name, shape):
        return nc.dram_tensor(name, list(shape), F32, kind="ExternalOutput").ap()

    xp = din("xp", [SEQ, D])
    xs = din("xs", [64, D])
    w_in = din("w_in", [DEPTH, D, INC])
    w_out = din("w_out", [DEPTH, D, D])
    ffn_w_in = din("ffn_w_in", [DEPTH, D, 2 * DFF])
    ffn_w_out = din("ffn_w_out", [DEPTH, DFF, D])
    glu_w = din("glu_w", [DEPTH, 256, 256])
    gvecs = din("gvecs", [5, 128, D])
    retgb = din("retgb", [DEPTH, 128, 768])
    glag = din("glag", [DEPTH, 128, 96])
    prm_d = din("prm", [DEPTH, 128, PR_N])
    cc_d = din("cc", [128, CC_N])
    cr_d = din("cr", [128, CR_N])
    rope_d = din("rope", [NT, 128, 64])
    ret_s0 = din("ret_s0", [DEPTH, 128, NS * 3 * 64])
    gla_s0 = din("gla_s0", [DEPTH, 48, NS * 4 * 96])
    s5_h0 = din("s5_h0", [DEPTH, 2, 128, 8 * NS])
    conv_s0 = din("conv_s0", [DEPTH, 128, NFC * NS * 2])

    y_p = dout("y_p", [SEQ, D])
    y_s = dout("y_s", [64, D])
    o_ret_p = dout("o_ret_p", [DEPTH, 128, 192])
    o_ret_s = dout("o_ret_s", [DEPTH, 128, NS * 192])
    o_s5_p = dout("o_s5_p", [DEPTH, 2, 128, 8])
    o_s5_s = dout("o_s5_s", [DEPTH, 2, 128, 8 * NS])
    o_gla_p = dout("o_gla_p", [DEPTH, 48, 384])
    o_gla_s = dout("o_gla_s", [DEPTH, 48, NS * 384])
    o_conv = dout("o_conv", [DEPTH, 128, NFC * 34])

    A = Arena(nc, 209000)
    PS = nc.alloc_psum_tensor("ps", [128, 8, 512], F32)
    PSb = PS.bitcast(BF16)
    pb = [Buf("psum%d" % i, excl=True) for i in range(8)]

    def psf(b0, nb=1):
        return PS[:, b0:b0 + nb, :].rearrange("p b c -> p (b c)")

    def psbf(b0):
        return PSb[:, b0, :]

    xt = [A.alloc("x%d" % t, [128, D], F32) for t in range(NT)]
    hT = A.alloc("hT", [128, 8, NTOK], BF16)
    hTb = [Buf("hT%d" % t) for t in range(NT)]
    cc = A.alloc("cc", [128, CC_N], F32)
    prm = A.alloc("prm", [128, PR_N], F32)
    gA = A.alloc("gA", [128, D], F32)
    ident_bf = A.alloc("ident_bf", [128, 128], BF16)
    tri_bf = A.alloc("tri_bf", [128, 2, 128], BF16)
    ntri_bf = A.alloc("ntri_bf", [128, 2, 128], BF16)
    hb = A.alloc("hb", [128, D], BF16)
    junk = A.alloc("junk", [128, D], BF16)
    ss = A.alloc("ss", [128, 2], F32)
    rs = A.alloc("rs", [128, 4], F32)
    ssb = [Buf("ss0"), Buf("ss1")]
    rsb = [Buf("rs0"), Buf("rs1")]
    mhalf = A.alloc("mhalf", [128, 8], F32)

    def ccv(name):
        o, w = CC_OFF[name]
        return cc[:, o:o + w]

    ident_f = ccv("ident")
    tri_f = ccv("tri").rearrange("p (a b) -> p a b", a=2)
    upp_f = ccv("upp").rearrange("p (a b) -> p a b", a=2)
    seqind = ccv("seqind")

    def prv(name):
        o, w = PR_OFF[name]
        return prm[:, o:o + w]

    S.dma("sp", cc[:], cc_d[:, :], writes=[cc.b])
    S.op("dve", lambda e: e.tensor_copy(out=ident_bf[:], in_=ident_f), reads=[cc.b], writes=[ident_bf.b])
    S.op("dve", lambda e: e.tensor_copy(out=tri_bf[:], in_=tri_f), reads=[cc.b], writes=[tri_bf.b])
    S.op("dve", lambda e: e.tensor_scalar(out=ntri_bf[:], in0=tri_f, scalar1=-1.0, scalar2=None, op0=ALU.mult),
         reads=[cc.b], writes=[ntri_bf.b])
    S.op("pool", lambda e: e.memset(mhalf[:], -0.5), writes=[mhalf.b])

    def tile_info(t):
        return (t * 128, 128 if t < NPT else 64)

    ORD = list(range(NT))
    PAR = {t: i % 2 for i, t in enumerate(ORD)}

    hbL = [hb, junk]

    def norm_stats(t):
        tok0, n = tile_info(t)
        x = xt[t]
        k = t % 2
        S.op("act", lambda e: e.activation(out=hbL[k][:n, :], in_=x[:n, :], func=AF.Square, accum_out=ss[:n, k:k + 1]),
             reads=[x.b], writes=[hbL[k].b, ssb[k]])
        S.op("dve", lambda e: e.tensor_scalar(out=rs[:n, 2 * k:2 * k + 1], in0=ss[:n, k:k + 1], scalar1=1.0 / D, scalar2=EPS,
                                              op0=ALU.mult, op1=ALU.add), reads=[ssb[k]], writes=[rsb[k]])
        S.op("pool", lambda e: e.tensor_tensor(out=rs[:n, 2 * k + 1:2 * k + 2], in0=rs[:n, 2 * k:2 * k + 1], in1=mhalf[:n, 0:1], op=ALU.pow),
             reads=[rsb[k], mhalf.b], writes=[rsb[k]])

    def norm_apply(t, gt, bank=7):
        tok0, n = tile_info(t)
        x = xt[t]
        k = t % 2
        hbk = hbL[k]
        S.op("dve", lambda e: e.scalar_tensor_tensor(out=hbk[:n, :], in0=x[:n, :], scalar=rs[:n, 2 * k + 1:2 * k + 2], in1=gt[:n, :],
                                                     op0=ALU.mult, op1=ALU.mult),
             reads=[x.b, rsb[k], gt.b], writes=[hbk.b])
        pv = psbf(bank).rearrange("p (k c) -> p k c", k=8)
        for kc in range(8):
            S.op("pe", lambda e, kc=kc: e.transpose(out=pv[:, kc, :n], in_=hbk[:n, kc * 128:(kc + 1) * 128],
                                                    identity=ident_bf[:n, :n]),
                 reads=[hbk.b, ident_bf.b], writes=[pb[bank]])
        S.op("act", lambda e: e.activation(out=hT[:, :, tok0:tok0 + n], in_=pv[:, :, :n], func=AF.Copy),
             reads=[pb[bank]], writes=[hTb[t]])

    def norm_tile(t, gt, bank=7):
        norm_stats(t)
        norm_apply(t, gt, bank)

    def chunked(tl, n):
        tl.b = Buf(tl.b.name, members=[Buf("%s_%d" % (tl.b.name, i)) for i in range(n)])
        return tl

    def load_cast(dst_ap, src_ap, wbuf, idx):
        S.dma("pool", dst_ap, src_ap, writes=[wbuf.members[idx]], max_dma_last_dim=8192)

    lamb = A.alloc("lamb", [128, DEPTH, 24], F32)
    sp5L = [A.alloc("sp5_%d" % i, [128, 20, 8], F32) for i in range(DEPTH)]
    S.dma("sp", lamb[:], prm_d[:, :, 0:24].rearrange("l p c -> p l c"), writes=[lamb.b])

    def s5_scalars(l):
        sp5 = sp5L[l]
        def sv(i):
            return sp5[:, i, :]

        def s_op(eng, fn):
            S.op(eng, fn, reads=[sp5.b, lamb.b], writes=[sp5.b])

        lamr, lami, ldt = lamb[:, l, 0:8], lamb[:, l, 8:16], lamb[:, l, 16:24]
        s_op("dve", lambda e: e.tensor_scalar(out=sv(0), in0=lamr, scalar1=-1e-4, scalar2=None, op0=ALU.min))
        s_op("act", lambda e: e.activation(out=sv(1), in_=ldt, func=AF.Exp))
        s_op("dve", lambda e: e.tensor_tensor(out=sv(6), in0=sv(0), in1=sv(1), op=ALU.mult))
        s_op("act", lambda e: e.activation(out=sv(2), in_=sv(6), func=AF.Exp))
        s_op("dve", lambda e: e.tensor_tensor(out=sv(3), in0=lami, in1=sv(1), op=ALU.mult))
        s_op("dve", lambda e: e.tensor_scalar(out=sv(3), in0=sv(3), scalar1=1.0 / 16, scalar2=None, op0=ALU.mult))
        s_op("dve", lambda e: e.tensor_scalar(out=sv(7), in0=sv(3), scalar1=math.pi / 2, scalar2=None, op0=ALU.add))
        s_op("act", lambda e: e.activation(out=sv(5), in_=sv(3), func=AF.Sin))
        s_op("act", lambda e: e.activation(out=sv(4), in_=sv(7), func=AF.Sin))
        for _ in range(4):
            s_op("dve", lambda e: e.tensor_tensor(out=sv(6), in0=sv(4), in1=sv(4), op=ALU.mult))
            s_op("dve", lambda e: e.tensor_tensor(out=sv(7), in0=sv(5), in1=sv(5), op=ALU.mult))
            s_op("dve", lambda e: e.scalar_tensor_tensor(out=sv(5), in0=sv(4), scalar=2.0, in1=sv(5), op0=ALU.mult, op1=ALU.mult))
            s_op("dve", lambda e: e.tensor_tensor(out=sv(4), in0=sv(6), in1=sv(7), op=ALU.subtract))
        s_op("dve", lambda e: e.tensor_tensor(out=sv(8), in0=sv(2), in1=sv(4), op=ALU.mult))
        s_op("dve", lambda e: e.tensor_tensor(out=sv(9), in0=sv(2), in1=sv(5), op=ALU.mult))
        s_op("dve", lambda e: e.tensor_tensor(out=sv(6), in0=sv(0), in1=sv(0), op=ALU.mult))
        s_op("dve", lambda e: e.tensor_tensor(out=sv(7), in0=lami, in1=lami, op=ALU.mult))
        s_op("dve", lambda e: e.tensor_tensor(out=sv(6), in0=sv(6), in1=sv(7), op=ALU.add))
        s_op("dve", lambda e: e.reciprocal(out=sv(10), in_=sv(6)))
        s_op("dve", lambda e: e.tensor_scalar(out=sv(11), in0=sv(8), scalar1=-1.0, scalar2=None, op0=ALU.add))
        s_op("dve", lambda e: e.tensor_tensor(out=sv(6), in0=sv(11), in1=sv(0), op=ALU.mult))
        s_op("dve", lambda e: e.tensor_tensor(out=sv(7), in0=sv(9), in1=lami, op=ALU.mult))
        s_op("dve", lambda e: e.tensor_tensor(out=sv(6), in0=sv(6), in1=sv(7), op=ALU.add))
        s_op("dve", lambda e: e.tensor_tensor(out=sv(12), in0=sv(6), in1=sv(10), op=ALU.mult))
        s_op("dve", lambda e: e.tensor_tensor(out=sv(6), in0=sv(9), in1=sv(0), op=ALU.mult))
        s_op("dve", lambda e: e.tensor_tensor(out=sv(7), in0=sv(11), in1=lami, op=ALU.mult))
        s_op("dve", lambda e: e.tensor_tensor(out=sv(6), in0=sv(6), in1=sv(7), op=ALU.subtract))
        s_op("dve", lambda e: e.tensor_tensor(out=sv(13), in0=sv(6), in1=sv(10), op=ALU.mult))
        s_op("dve", lambda e: e.tensor_tensor(out=sv(6), in0=sv(2), in1=sv(2), op=ALU.mult))
        s_op("dve", lambda e: e.reciprocal(out=sv(16), in_=sv(6)))
        s_op("dve", lambda e: e.tensor_tensor(out=sv(14), in0=sv(8), in1=sv(16), op=ALU.mult))
        s_op("dve", lambda e: e.scalar_tensor_tensor(out=sv(15), in0=sv(9), scalar=-1.0, in1=sv(16), op0=ALU.mult, op1=ALU.mult))

    for l in range(DEPTH):
        s5_scalars(l)

    def layer(l):
        S.dma("sp", prm[:], prm_d[l, :, :], writes=[prm.b])
        S.dma("sp", gA[:], gvecs[2 * l, :, :], writes=[gA.b])
        if l == 0:
            for t in range(NT):
                tok0, n = tile_info(t)
                srcx = xp[tok0:tok0 + n, :] if t < NPT else xs[:, :]
                S.dma("sp", xt[t][:n, :], srcx, writes=[xt[t].b])
        norm_stats(0)
        norm_stats(1)

        m1 = A.mark()
        cr = A.alloc("cr", [128, CR_N], F32)

        def crv(name):
            o, w = CR_OFF[name]
            return cr[:, o:o + w]

        ropeL = [A.alloc("rope%d" % i, [128, 2, 32], F32) for i in range(2)]
        rmask = [crv("rmask_p").rearrange("p (h i) -> p h i", h=6), crv("rmask_s").rearrange("p (h i) -> p h i", h=6)]
        qdec = [crv("qdec_p").rearrange("p (a i) -> p a i", a=3), crv("qdec_s").rearrange("p (a i) -> p a i", a=3)]
        kdec = crv("kdec").rearrange("p (a h) -> p a h", a=2)
        sdec = crv("sdec").rearrange("p (a h) -> p a h", a=2)
        Wr = chunked(A.alloc("Wr", [128, 8, 1536], BF16), 8)
        WoR = chunked(A.alloc("WoR", [128, 3, D], BF16), 3)
        S0r = A.alloc("S0r", [128, NS, 3, 64], F32)
        rgb = A.alloc("rgb", [128, 768], F32)
        qkrot = A.alloc("qkrot", [128, 12, 2, 32], BF16)
        tmp = [A.alloc("rt%d" % i, [128, 12, 32], F32) for i in range(4)]
        kendL = [A.alloc("kend%d" % i, [128, 6, 64], BF16) for i in range(2)]
        vsbL = [A.alloc("vsb%d" % i, [128, 6, 64], BF16) for i in range(2)]
        gsL = [A.alloc("gs%d" % i, [128, 384], F32) for i in range(2)]
        bsL = [A.alloc("bs%d" % i, [128, 384], F32) for i in range(2)]
        sg = A.alloc("sg", [128, 384], F32)
        qT = A.alloc("qT", [128, 3, 128], BF16)
        kT = A.alloc("kT", [128, 3, 128], BF16)
        qsT = A.alloc("qsT", [128, 3, 128], BF16)
        qkrotL = [qkrot, A.alloc("qkrot1", [128, 12, 2, 32], BF16)]
        qTL = [qT, A.alloc("qT1", [128, 3, 128], BF16)]
        kTL = [kT, A.alloc("kT1", [128, 3, 128], BF16)]
        qsTL = [qsT, A.alloc("qsT1", [128, 3, 128], BF16)]
        scT = A.alloc("scT", [128, 6, 128], BF16)
        on1 = A.alloc("on1", [128, 6, 64], F32)
        on2 = A.alloc("on2", [128, 6, 64], F32)

        class Alias:
            def __init__(self, base, ap):
                self.b = base.b
                self.ap = ap

            def __getitem__(self, idx):
                return self.ap[idx]

        sq = Alias(on2, on2[:].rearrange("p h v -> p (h v)"))
        qsTf = Alias(tmp[2], tmp[2][:].rearrange("p a b -> p (a b)")[:, 0:192].rearrange("p (a i) -> p a i", a=3))
        ocT = Alias(tmp[0], tmp[0][0:64].rearrange("p a b -> p (a b)").rearrange("p (h i) -> p h i", h=6))
        kes = Alias(scT, scT[0:64, 0:3, :].rearrange("p a i -> p (a i)"))
        st = A.alloc("st", [128, 40], F32)
        mixr = A.alloc("mixr", [128, 384], BF16)
        mixT = A.alloc("mixT", [128, 3, 128], BF16)
        Sr = A.alloc("Sr", [128, 3, 64], F32)
        Srb = A.alloc("Srb", [128, 3, 64], BF16)

        S.dma("sp", cr[:], cr_d[:, :], writes=[cr.b])
        for kc in range(8):
            load_cast(Wr[:, kc, :], w_in[l, kc * 128:(kc + 1) * 128, 0:1536], Wr.b, kc)
        for kc in range(3):
            load_cast(WoR[:, kc, :], w_out[l, kc * 128:(kc + 1) * 128, :], WoR.b, kc)
        S.dma("sp", S0r[:].rearrange("p s a v -> p (s a v)"), ret_s0[l, :, :], writes=[S0r.b])
        S.dma("sp", rgb[:], retgb[l, :, :], writes=[rgb.b])
        S.op("pool", lambda e: e.memset(Sr[:], 0.0), writes=[Sr.b])
        S.op("pool", lambda e: e.memset(Srb[:], 0.0), writes=[Srb.b])

        for t in range(NT):
            if 1 <= t + 1 < NT and t + 1 >= 2:
                norm_stats(t + 1)
            norm_apply(t, gA, bank=6 + (t % 2))

        def p1_A(t):
            tok0, n = tile_info(t)
            for bank in range(3):
                for kc in range(8):
                    S.op("pe", lambda e, bank=bank, kc=kc: e.matmul(
                        PS[:n, bank, :], lhsT=hT[:, kc, tok0:tok0 + n], rhs=Wr[:, kc, bank * 512:(bank + 1) * 512],
                        start=(kc == 0), stop=(kc == 7)), reads=[hTb[t], Wr.b], writes=[pb[bank]])

        def p1_B(t):
            tok0, n = tile_info(t)
            sm = 0 if t < NPT else 1
            kend, vsb, gs, bs = kendL[PAR[t]], vsbL[PAR[t]], gsL[PAR[t]], bsL[PAR[t]]
            qkrot = qkrotL[PAR[t]]
            flat = psf(0, 3)
            qk = flat[:n, 0:768].rearrange("p (h c d) -> p h c d", h=12, c=2)
            x1 = qk[:, :, 0, :]
            x2 = qk[:, :, 1, :]
            rp = ropeL[PAR[t]]
            S.dma("sp", rp[:].rearrange("p c h -> p (c h)"), rope_d[t, :, :], writes=[rp.b])
            cosb = rp[:n, 0:1, :].broadcast_to([n, 12, 32])
            sinb = rp[:n, 1:2, :].broadcast_to([n, 12, 32])
            rd = [pb[0], pb[1], rp.b]
            S.op("dve", lambda e: e.tensor_tensor(out=tmp[0][:n], in0=x1, in1=cosb, op=ALU.mult), reads=rd, writes=[tmp[0].b])
            S.op("dve", lambda e: e.tensor_tensor(out=tmp[1][:n], in0=x2, in1=sinb, op=ALU.mult), reads=rd, writes=[tmp[1].b])
            S.op("dve", lambda e: e.tensor_tensor(out=tmp[2][:n], in0=x1, in1=sinb, op=ALU.mult), reads=rd, writes=[tmp[2].b])
            S.op("dve", lambda e: e.tensor_tensor(out=tmp[3][:n], in0=x2, in1=cosb, op=ALU.mult), reads=rd, writes=[tmp[3].b])
            S.op("act", lambda e: e.activation(out=vsb[:n].rearrange("p h d -> p (h d)"), in_=flat[:n, 768:1152], func=AF.Copy),
                 reads=[pb[1], pb[2]], writes=[vsb.b])
            S.op("act", lambda e: e.activation(out=sg[:n, :], in_=flat[:n, 1152:1536], func=AF.Silu),
                 reads=[pb[2]], writes=[sg.b])
            S.op("pool", lambda e: e.tensor_tensor(out=qkrot[:n, :, 0, :], in0=tmp[0][:n], in1=tmp[1][:n], op=ALU.subtract),
                 reads=[tmp[0].b, tmp[1].b], writes=[qkrot.b])
            S.op("pool", lambda e: e.tensor_tensor(out=qkrot[:n, :, 1, :], in0=tmp[2][:n], in1=tmp[3][:n], op=ALU.add),
                 reads=[tmp[2].b, tmp[3].b], writes=[qkrot.b])
            qkf = qkrot[:].rearrange("p h c d -> p (h c d)")
            krot = qkf[:n, 384:768].rearrange("p (h d) -> p h d", h=6)
            S.op("dve", lambda e: e.tensor_tensor(out=kend[:n], in0=krot, in1=kdec[:n, sm, :, None].broadcast_to([n, 6, 64]),
                                                  op=ALU.mult), reads=[qkrot.b, cr.b], writes=[kend.b])
            S.op("pool", lambda e: e.tensor_tensor(out=gs[:n, :], in0=sg[:n, :], in1=rgb[:n, 0:384], op=ALU.mult),
                 reads=[sg.b, rgb.b], writes=[gs.b])
            S.op("pool", lambda e: e.tensor_tensor(out=bs[:n, :], in0=sg[:n, :], in1=rgb[:n, 384:768], op=ALU.mult),
                 reads=[sg.b, rgb.b], writes=[bs.b])

        def p1_C(t):
            tok0, n = tile_info(t)
            sm = 0 if t < NPT else 1
            vsb = vsbL[PAR[t]]
            qkrot, qT, kT, qsT = qkrotL[PAR[t]], qTL[PAR[t]], kTL[PAR[t]], qsTL[PAR[t]]
            qkf = qkrot[:].rearrange("p h c d -> p (h c d)")
            tp = psbf(3)[:, 0:768].rearrange("p (a i) -> p a i", a=6)
            for a in range(6):
                S.op("pe", lambda e, a=a: e.transpose(out=tp[:, a, :n], in_=qkf[:n, a * 128:(a + 1) * 128],
                                                      identity=ident_bf[:n, :n]),
                     reads=[qkrot.b, ident_bf.b], writes=[pb[3]])
            S.op("act", lambda e: e.activation(out=kT[:, :, :n], in_=tp[:, 3:6, :n], func=AF.Copy), reads=[pb[3]], writes=[kT.b])
            S.op("act", lambda e: e.activation(out=qT[:, :, :n], in_=tp[:, 0:3, :n], func=AF.Copy), reads=[pb[3]], writes=[qT.b])
            if sm == 0:
                S.op("dve", lambda e: e.tensor_tensor(out=qsT[:, :, :n], in0=tp[:, 0:3, :n], in1=qdec[0][:, :, :n], op=ALU.mult),
                     reads=[pb[3], cr.b], writes=[qsT.b])
            else:
                S.op("dve", lambda e: e.tensor_tensor(out=qsTf[:, :, :n], in0=tp[:, 0:3, :n], in1=qdec[1][:, :, :n], op=ALU.mult),
                     reads=[pb[3], cr.b], writes=[qsTf.b])
            sc = psf(4, 2).rearrange("p (h i) -> p h i", h=8)
            for h in range(6):
                r0 = (h % 2) * 64
                sl_ = (h % 2) * 4 + h // 2
                S.op("pe", lambda e, h=h, r0=r0, sl_=sl_: e.matmul(sc[:n, sl_, :n], lhsT=kT[r0:r0 + 64, h // 2, :n],
                                                                  rhs=qT[r0:r0 + 64, h // 2, :n], start=True, stop=True),
                     reads=[kT.b, qT.b], writes=[pb[4 + h % 2]])
            for par in range(2):
                S.op("dve", lambda e, par=par: e.tensor_tensor(out=scT[:n, 3 * par:3 * par + 3, :n], in0=sc[:n, 4 * par:4 * par + 3, :n],
                                                              in1=rmask[sm][:n, 3 * par:3 * par + 3, :n], op=ALU.mult),
                     reads=[pb[4 + par], cr.b], writes=[scT.b])
            O = psf(6)[:, 0:384].rearrange("p (h v) -> p h v", h=6)
            if sm == 1:
                OCTs = [psf(3)[:, 0:192].rearrange("p (h i) -> p h i", h=3), psf(2)[:, 0:192].rearrange("p (h i) -> p h i", h=3)]
                obank = [3, 2]
                for s in range(NS):
                    for h in range(6):
                        r0 = (h % 2) * 64
                        S.op("pe", lambda e, s=s, h=h, r0=r0: e.matmul(
                            OCTs[h % 2][0:64, h // 2, 4 * s:4 * s + 4], lhsT=S0r[r0:r0 + 64, s, h // 2, :],
                            rhs=qsTf[r0:r0 + 64, h // 2, 4 * s:4 * s + 4], start=True, stop=True),
                            reads=[S0r.b, qsTf.b], writes=[pb[obank[h % 2]]])
                for par in range(2):
                    S.op("act", lambda e, par=par: e.activation(out=ocT[:, par * 3:par * 3 + 3, :], in_=OCTs[par][0:64, :, :], func=AF.Copy),
                         reads=[pb[obank[par]]], writes=[ocT.b])
            for h in range(6):
                r0 = (h % 2) * 64
                S.op("pe", lambda e, h=h: e.matmul(O[:n, h, :], lhsT=scT[:n, HSLOT[h], :n], rhs=vsb[:n, h, :], start=True, stop=False),
                     reads=[scT.b, vsb.b], writes=[pb[6]])
                if sm == 0:
                    S.op("pe", lambda e, h=h, r0=r0: e.matmul(O[:n, h, :], lhsT=qsT[r0:r0 + 64, h // 2, :n],
                                                             rhs=Srb[r0:r0 + 64, h // 2, :], start=False, stop=True),
                         reads=[qsT.b, Srb.b], writes=[pb[6]])
                else:
                    S.op("pe", lambda e, h=h: e.matmul(O[:n, h, :], lhsT=ocT[0:64, HSLOT[h], 0:64], rhs=ident_f[0:64, 0:64],
                                                      start=False, stop=True),
                         reads=[ocT.b, cc.b], writes=[pb[6]])

        def p1_H(t):
            tok0, n = tile_info(t)
            sm = 0 if t < NPT else 1
            kend, vsb = kendL[PAR[t]], vsbL[PAR[t]]
            if sm == 0:
                KV = psf(7)[:, 192:384].rearrange("p (a v) -> p a v", a=3)
                for h in range(6):
                    r0 = (h % 2) * 64
                    S.op("pe", lambda e, h=h, r0=r0: e.matmul(KV[r0:r0 + 64, h // 2, :], lhsT=kend[:n, h, :], rhs=vsb[:n, h, :],
                                                             start=True, stop=True), reads=[kend.b, vsb.b], writes=[pb[7]])
                S.op("dve", lambda e: e.tensor_tensor(out=Sr[:], in0=Sr[:], in1=sdec[:, 0, :, None].broadcast_to([128, 3, 64]),
                                                      op=ALU.mult), reads=[Sr.b, cr.b], writes=[Sr.b])
                S.op("dve", lambda e: e.tensor_tensor(out=Sr[:], in0=KV, in1=Sr[:], op=ALU.add), reads=[pb[7], Sr.b], writes=[Sr.b])
                S.op("act", lambda e: e.activation(out=Srb[:], in_=Sr[:], func=AF.Copy), reads=[Sr.b], writes=[Srb.b])
                if t == NPT - 1:
                    S.dma("sp", o_ret_p[l, :, :], Sr[:].rearrange("p a v -> p (a v)"), reads=[Sr.b])
            else:
                kendf = kend[:].rearrange("p h d -> p (h d)")
                for g4 in range(4):
                    KV = psf(4, 2)[:, 0:768].rearrange("p (s a v) -> p s a v", s=4, a=3)
                    for sl in range(4):
                        s = g4 * 4 + sl
                        S.op("dve", lambda e, s=s: e.tensor_scalar(out=kes[:, :], in0=kendf[:64, :], scalar1=seqind[:64, s:s + 1],
                                                                   scalar2=None, op0=ALU.mult),
                             reads=[kend.b, cc.b], writes=[kes.b])
                        for h in range(6):
                            r0 = (h % 2) * 64
                            bank = 4 + (sl * 3 + h // 2) // 8
                            S.op("pe", lambda e, sl=sl, h=h, r0=r0, KV=KV: e.matmul(
                                KV[r0:r0 + 64, sl, h // 2, :], lhsT=kes[:64, h * 64:(h + 1) * 64], rhs=vsb[:64, h, :],
                                start=True, stop=True), reads=[kes.b, vsb.b], writes=[pb[bank]])
                    S0g_ = S0r[:, g4 * 4:(g4 + 1) * 4, :, :]
                    S.op("dve", lambda e, S0g_=S0g_: e.tensor_tensor(
                        out=S0g_, in0=S0g_, in1=sdec[:, 1, None, :, None].broadcast_to([128, 4, 3, 64]), op=ALU.mult),
                        reads=[S0r.b, cr.b], writes=[S0r.b])
                    S.op("dve", lambda e, S0g_=S0g_, KV=KV: e.tensor_tensor(out=S0g_, in0=KV, in1=S0g_, op=ALU.add),
                         reads=[pb[4], pb[5], S0r.b], writes=[S0r.b])
                S.dma("sp", o_ret_s[l, :, :], S0r[:].rearrange("p s a v -> p (s a v)"), reads=[S0r.b])

        def p1_L(t):
            tok0, n = tile_info(t)
            gs, bs = gsL[PAR[t]], bsL[PAR[t]]
            O = psf(6)[:, 0:384].rearrange("p (h v) -> p h v", h=6)
            Of = psf(6)[:n, 0:384]
            S.op("act", lambda e: e.activation(out=sq[:n, :], in_=Of, func=AF.Square), reads=[pb[6]], writes=[sq.b])
            S.op("dve", lambda e: e.tensor_reduce(out=st[:n, 0:6], in_=O[:n], axis=AX.X, op=ALU.add), reads=[pb[6]], writes=[st.b])
            S.op("dve", lambda e: e.tensor_reduce(out=st[:n, 6:12], in_=sq[:n, :].rearrange("p (h v) -> p h v", h=6),
                                                  axis=AX.X, op=ALU.add), reads=[sq.b], writes=[st.b])
            S.op("dve", lambda e: e.tensor_scalar(out=st[:n, 12:18], in0=st[:n, 0:6], scalar1=1.0 / 64, scalar2=None, op0=ALU.mult),
                 reads=[st.b], writes=[st.b])
            S.op("dve", lambda e: e.tensor_tensor(out=st[:n, 18:24], in0=st[:n, 12:18], in1=st[:n, 12:18], op=ALU.mult),
                 reads=[st.b], writes=[st.b])
            S.op("dve", lambda e: e.scalar_tensor_tensor(out=st[:n, 24:30], in0=st[:n, 6:12], scalar=1.0 / 64, in1=st[:n, 18:24],
                                                         op0=ALU.mult, op1=ALU.subtract), reads=[st.b], writes=[st.b])
            S.op("dve", lambda e: e.tensor_scalar(out=st[:n, 30:36], in0=st[:n, 24:30], scalar1=EPS, scalar2=None, op0=ALU.add),
                 reads=[st.b], writes=[st.b])
            S.op("pool", lambda e: e.tensor_tensor(out=st[:n, 24:30], in0=st[:n, 30:36], in1=mhalf[:n, 0:6], op=ALU.pow),
                 reads=[st.b, mhalf.b], writes=[st.b])
            S.op("dve", lambda e: e.tensor_tensor(out=on1[:n], in0=O[:n], in1=st[:n, 12:18, None].broadcast_to([n, 6, 64]),
                                                  op=ALU.subtract), reads=[pb[6], st.b], writes=[on1.b])
            S.op("dve", lambda e: e.tensor_tensor(out=on2[:n], in0=on1[:n], in1=st[:n, 24:30, None].broadcast_to([n, 6, 64]),
                                                  op=ALU.mult), reads=[on1.b, st.b], writes=[on2.b])
            on2f = on2[:].rearrange("p h v -> p (h v)")
            on1f = on1[:].rearrange("p h v -> p (h v)")
            S.op("dve", lambda e: e.tensor_tensor(out=on1f[:n], in0=on2f[:n], in1=gs[:n, :], op=ALU.mult),
                 reads=[on2.b, gs.b], writes=[on1.b], safe=True)
            S.op("dve", lambda e: e.tensor_tensor(out=mixr[:n, :], in0=on1f[:n], in1=bs[:n, :], op=ALU.add),
                 reads=[on1.b, bs.b], writes=[mixr.b], safe=True)

        def p1_G(t):
            tok0, n = tile_info(t)
            x = xt[t]
            mp = psbf(7)[:, 0:384].rearrange("p (a i) -> p a i", a=3)
            for a in range(3):
                S.op("pe", lambda e, a=a: e.transpose(out=mp[:, a, :n], in_=mixr[:n, a * 128:(a + 1) * 128],
                                                      identity=ident_bf[:n, :n]), reads=[mixr.b, ident_bf.b], writes=[pb[7]])
            S.op("act", lambda e: e.activation(out=mixT[:, :, :n], in_=mp[:, :, :n], func=AF.Copy), reads=[pb[7]], writes=[mixT.b])
            for bank in range(2):
                for a in range(3):
                    S.op("pe", lambda e, bank=bank, a=a: e.matmul(PS[:n, 4 + bank, :], lhsT=mixT[:, a, :n],
                                                                 rhs=WoR[:, a, bank * 512:(bank + 1) * 512],
                                                                 start=(a == 0), stop=(a == 2)),
                         reads=[mixT.b, WoR.b], writes=[pb[4 + bank]])
            S.op("dve", lambda e: e.tensor_tensor(out=x[:n, :], in0=psf(4, 2)[:n, :], in1=x[:n, :], op=ALU.add),
                 reads=[pb[4], pb[5], x.b], writes=[x.b])

        p1_A(ORD[0])
        p1_B(ORD[0])
        for i, t in enumerate(ORD):
            nx = ORD[i + 1] if i + 1 < NT else None
            p1_C(t)
            p1_H(t)
            if nx is not None:
                p1_A(nx)
            p1_L(t)
            if nx is not None:
                p1_B(nx)
            p1_G(t)
        S.barrier()
        A.release(m1)
        if stop_after is not None and stop_after.startswith("P1"):
            return True

        m2 = A.mark()
        W5 = chunked(A.alloc("W5", [128, 8, 256], BF16), 8)
        Wo5 = chunked(A.alloc("Wo5", [128, 2, D], BF16), 2)
        gluw = chunked(A.alloc("gluw", [128, 2, 256], BF16), 2)
        Tfr = A.alloc("Tfr", [128, 8, 128], F32)
        Tfi = A.alloc("Tfi", [128, 8, 128], F32)
        nTfi = A.alloc("nTfi", [128, 8, 128], F32)
        Tinv = A.alloc("Tinv", [128, 8, 2, 128], F32)
        Tfs = [A.alloc("Tfs%d" % i, [128, 8, 64], F32) for i in range(3)]
        Tinvs = A.alloc("Tinvs", [64, 8, 2, 128], F32)
        Zt = A.alloc("Zt", [128, 8, 2, 32], BF16)
        Bc = A.alloc("Bc", [128, 2, 2, 128], BF16)
        Ctb = A.alloc("Ctb", [128, 2, 8, 32], BF16)
        H0 = A.alloc("H0", [128, 2, 8, NS], F32)
        Hst = A.alloc("Hst", [128, 2, 8], F32)
        Hb = [Buf("H%d" % g) for g in range(8)]
        Hout = A.alloc("Hout", [128, 2, 8, NS], F32)

        for kc in range(8):
            load_cast(W5[:, kc, :], w_in[l, kc * 128:(kc + 1) * 128, 1536:1792], W5.b, kc)
        for kc in range(2):
            load_cast(Wo5[:, kc, :], w_out[l, 384 + kc * 128:384 + (kc + 1) * 128, :], Wo5.b, kc)
            load_cast(gluw[:, kc, :], glu_w[l, kc * 128:(kc + 1) * 128, :], gluw.b, kc)
        S.dma("sp", H0[:].rearrange("p c g s -> p c (g s)"), s5_h0[l].rearrange("c p n -> p c n"), writes=[H0.b])

        m2s = A.mark()
        Pir = A.alloc("Pir", [128, 8, 128], F32)
        Pii = A.alloc("Pii", [128, 8, 128], F32)
        pt = [A.alloc("pt%d" % i, [128, 8, 64], F32) for i in range(4)]
        bt2 = [A.alloc("pu%d" % i, [128, 8, 64], F32) for i in range(4)]
        bt = [A.alloc("bt%d" % i, [128, 8, 16], F32) for i in range(4)]
        Tis = [A.alloc("Tis%d" % i, [128, 8, 64], F32) for i in range(2)]

        sp5 = sp5L[l]

        def sv(i):
            return sp5[:, i, :]


        br = prv("br").rearrange("p (g h) -> p g h", g=8)
        bi = prv("bi").rearrange("p (g h) -> p g h", g=8)

        def bc16(i):
            return sp5[:, i, :, None].broadcast_to([128, 8, 16])

        S.op("pool", lambda e: e.memset(Zt[:], 0.0), writes=[Zt.b])
        rb = [sp5.b, prm.b]
        S.op("dve", lambda e: e.tensor_tensor(out=bt[0][:], in0=br, in1=bc16(12), op=ALU.mult), reads=rb, writes=[bt[0].b])
        S.op("dve", lambda e: e.tensor_tensor(out=bt[1][:], in0=bi, in1=bc16(13), op=ALU.mult), reads=rb, writes=[bt[1].b])
        S.op("dve", lambda e: e.tensor_tensor(out=bt[2][:], in0=bi, in1=bc16(12), op=ALU.mult), reads=rb, writes=[bt[2].b])
        S.op("dve", lambda e: e.tensor_tensor(out=bt[3][:], in0=br, in1=bc16(13), op=ALU.mult), reads=rb, writes=[bt[3].b])
        for (p0, c0) in ((0, 0), (64, 16)):
            S.op("dve", lambda e, p0=p0, c0=c0: e.tensor_tensor(out=Zt[p0:p0 + 64, :, 0, c0:c0 + 16], in0=bt[0][p0:p0 + 64],
                                                               in1=bt[1][p0:p0 + 64], op=ALU.subtract),
                 reads=[bt[0].b, bt[1].b, Zt.b], writes=[Zt.b])
            S.op("dve", lambda e, p0=p0, c0=c0: e.tensor_tensor(out=Zt[p0:p0 + 64, :, 1, c0:c0 + 16], in0=bt[2][p0:p0 + 64],
                                                               in1=bt[3][p0:p0 + 64], op=ALU.add),
                 reads=[bt[2].b, bt[3].b, Zt.b], writes=[Zt.b])
        BCp = psf(0).rearrange("p (a c i) -> p a c i", a=2, c=2)
        for gp in range(8):
            for ri in range(2):
                q0 = 32 * (gp % 4)
                S.op("pe", lambda e, gp=gp, ri=ri, q0=q0: e.matmul(BCp[q0:q0 + 32, gp // 4, ri, :], lhsT=Zt[:, gp, ri, :],
                                                                  rhs=ident_bf[:, :], start=True, stop=True,
                                                                  tile_position=(0, q0)),
                     reads=[Zt.b, ident_bf.b], writes=[pb[0]])
        S.op("act", lambda e: e.activation(out=Bc[:], in_=BCp, func=AF.Copy), reads=[pb[0]], writes=[Bc.b])
        S.op("dve", lambda e: e.tensor_copy(out=Ctb[:, 0, :, :], in_=prv("ctr").rearrange("p (g c) -> p g c", g=8)),
             reads=[prm.b], writes=[Ctb.b])
        S.op("dve", lambda e: e.tensor_copy(out=Ctb[:, 1, :, :], in_=prv("cti").rearrange("p (g c) -> p g c", g=8)),
             reads=[prm.b], writes=[Ctb.b])

        def powers(Pr, Pi, ir, ii, en="dve"):
            ptx = pt if en == "dve" else bt2
            S.op(en, lambda e: e.tensor_copy(out=Pr[:, :, 0], in_=sv(ir)), reads=[sp5.b], writes=[Pr.b])
            S.op(en, lambda e: e.tensor_copy(out=Pi[:, :, 0], in_=sv(ii)), reads=[sp5.b], writes=[Pi.b])
            nn = 1
            while nn < 128:
                a_r = Pr[:, :, 0:nn]
                a_i = Pi[:, :, 0:nn]
                c_r = Pr[:, :, nn - 1:nn].broadcast_to([128, 8, nn])
                c_i = Pi[:, :, nn - 1:nn].broadcast_to([128, 8, nn])
                rw = [Pr.b, Pi.b]
                S.op(en, lambda e, a_r=a_r, c_r=c_r, nn=nn: e.tensor_tensor(out=ptx[0][:, :, 0:nn], in0=a_r, in1=c_r, op=ALU.mult),
                     reads=rw, writes=[ptx[0].b])
                S.op(en, lambda e, a_i=a_i, c_i=c_i, nn=nn: e.tensor_tensor(out=ptx[1][:, :, 0:nn], in0=a_i, in1=c_i, op=ALU.mult),
                     reads=rw, writes=[ptx[1].b])
                S.op(en, lambda e, a_r=a_r, c_i=c_i, nn=nn: e.tensor_tensor(out=ptx[2][:, :, 0:nn], in0=a_r, in1=c_i, op=ALU.mult),
                     reads=rw, writes=[ptx[2].b])
                S.op(en, lambda e, a_i=a_i, c_r=c_r, nn=nn: e.tensor_tensor(out=ptx[3][:, :, 0:nn], in0=a_i, in1=c_r, op=ALU.mult),
                     reads=rw, writes=[ptx[3].b])
                S.op(en, lambda e, nn=nn: e.tensor_tensor(out=Pr[:, :, nn:2 * nn], in0=ptx[0][:, :, 0:nn], in1=ptx[1][:, :, 0:nn],
                                                             op=ALU.subtract), reads=[ptx[0].b, ptx[1].b, Pr.b], writes=[Pr.b])
                S.op(en, lambda e, nn=nn: e.tensor_tensor(out=Pi[:, :, nn:2 * nn], in0=ptx[2][:, :, 0:nn], in1=ptx[3][:, :, 0:nn],
                                                             op=ALU.add), reads=[ptx[2].b, ptx[3].b, Pi.b], writes=[Pi.b])
                nn *= 2

        powers(Tfr, Tfi, 8, 9)
        powers(Pir, Pii, 14, 15, en="pool")
        S.op("dve", lambda e: e.tensor_scalar(out=nTfi[:], in0=Tfi[:], scalar1=-1.0, scalar2=None, op0=ALU.mult),
             reads=[Tfi.b], writes=[nTfi.b])
        for i, src in enumerate((Tfr, Tfi, nTfi)):
            S.op("dve", lambda e, i=i, src=src: e.tensor_copy(
                out=Tfs[i][:].rearrange("p g (s t) -> p g s t", s=NS),
                in_=src[:, :, None, 0:4].broadcast_to([128, 8, NS, 4])), reads=[src.b], writes=[Tfs[i].b])
        for i, src in enumerate((Pir, Pii)):
            S.op("dve", lambda e, i=i, src=src: e.tensor_copy(
                out=Tis[i][:].rearrange("p g (s t) -> p g s t", s=NS),
                in_=src[:, :, None, 0:4].broadcast_to([128, 8, NS, 4])), reads=[src.b], writes=[Tis[i].b])
        for ri, src in enumerate((Pir, Pii)):
            for gq in range(2):
                bank = 1 + ri * 2 + gq
                tpv = psf(bank).rearrange("p (g i) -> p g i", g=4)
                for gl in range(4):
                    gp = gq * 4 + gl
                    S.op("pe", lambda e, tpv=tpv, gl=gl, gp=gp, src=src: e.transpose(out=tpv[:, gl, :], in_=src[:, gp, :],
                                                                                   identity=ident_f[:, :]),
                         reads=[src.b, cc.b], writes=[pb[bank]])
                S.op("act", lambda e, tpv=tpv, gq=gq, ri=ri: e.activation(out=Tinv[:, gq * 4:(gq + 1) * 4, ri, :], in_=tpv,
                                                                         func=AF.Copy), reads=[pb[bank]], writes=[Tinv.b])
        for ri in range(2):
            for gq in range(2):
                bank = 5 + ri
                tpv = psf(bank).rearrange("p (g i) -> p g i", g=4)
                for gl in range(4):
                    gp = gq * 4 + gl
                    S.op("pe", lambda e, tpv=tpv, gl=gl, gp=gp, ri=ri: e.transpose(out=tpv[0:64, gl, :], in_=Tis[ri][:, gp, :],
                                                                                  identity=ident_f[:, :]),
                         reads=[Tis[ri].b, cc.b], writes=[pb[bank]])
                S.op("act", lambda e, tpv=tpv, gq=gq, ri=ri: e.activation(out=Tinvs[:, gq * 4:(gq + 1) * 4, ri, :],
                                                                         in_=tpv[0:64, :, :], func=AF.Copy),
                     reads=[pb[bank]], writes=[Tinvs.b])
        S.barrier()
        A.release(m2s)
        uTb = A.alloc("uTb", [128, 2, 128], BF16)
        uTfL = [A.alloc("uTf%d" % i, [128, 2, 128], F32) for i in range(2)]
        xpL = [[A.alloc("xp%d_%d" % (i, k), [128, 8, 128], BF16) for k in range(4)] for i in range(2)]
        ab = [[A.alloc("ab%d_%d" % (k, i), [128, 128], F32) for i in range(4)] for k in range(2)]
        hrT = A.alloc("hrT", [128, 8, 128], BF16)
        nhiT = A.alloc("nhiT", [128, 8, 128], BF16)
        ysb = A.alloc("ysb", [128, 2, 128], F32)
        y5 = A.alloc("y5", [128, 2, 128], F32)
        y5b = A.alloc("y5b", [128, 2, 128], BF16)
        sig = A.alloc("sig", [128, 2, 128], F32)
        o5T = A.alloc("o5T", [128, 2, 128], BF16)
        BcF = A.alloc("BcF", [128, 2, 4, 256], BF16)

        class Alias2:
            def __init__(self, base, ap):
                self.b = base.b
                self.ap = ap

            def __getitem__(self, idx):
                return self.ap[idx]

        GH = [Alias2(ab[0][i], ab[0][i][:].rearrange("p (g i) -> p g i", g=2)) for i in range(2)]
        ws = [Alias2(ab[1][i], ab[1][i][:].rearrange("p (g i) -> p g i", g=2)) for i in range(4)]
        dsk = prv("dsk")
        glub = prv("glub")

        S.op("pool", lambda e: e.memset(Hst[:], 0.0), writes=Hb)
        S.op("pool", lambda e: e.memset(BcF[:], 0.0), writes=[BcF.b])
        for gq in range(4):
            S.op("act", lambda e, gq=gq: e.activation(out=BcF[32 * gq:32 * gq + 32, :, gq, :],
                                                      in_=Bc[32 * gq:32 * gq + 32, :, :, :].rearrange("p a c i -> p a (c i)"),
                                                      func=AF.Copy), reads=[Bc.b, BcF.b], writes=[BcF.b])

        def p2_Xa(t):
            tok0, n = tile_info(t)
            sm = 0 if t < NPT else 1
            uTf = uTfL[PAR[t]]
            suT = psf(0)[:, 0:256].rearrange("p (a i) -> p a i", a=2)
            for half in range(2):
                for kc in range(8):
                    S.op("pe", lambda e, half=half, kc=kc: e.matmul(suT[:, half, :n], lhsT=W5[:, kc, half * 128:(half + 1) * 128],
                                                                   rhs=hT[:, kc, tok0:tok0 + n], start=(kc == 0), stop=(kc == 7)),
                         reads=[W5.b, hTb[t]], writes=[pb[0]])
            S.op("act", lambda e: e.activation(out=uTb[:, :, :n], in_=suT[:, :, :n], func=AF.Copy), reads=[pb[0]], writes=[uTb.b])
            S.op("act", lambda e: e.activation(out=uTf[:, :, :n], in_=suT[:, :, :n], func=AF.Copy), reads=[pb[0]], writes=[uTf.b])

        def p2_Xb(t):
            tok0, n = tile_info(t)
            sm = 0 if t < NPT else 1
            xa, xb, xc, xd = xpL[PAR[t]]
            TI = Tinv if sm == 0 else Tinvs
            for hh in range(2):
                X = psf(1, 2).rearrange("p (g c i) -> p g c i", g=4, c=2)
                for nb in range(2):
                    S.op("pe", lambda e, hh=hh, nb=nb: e.matmul(
                        PS[:n, 1 + nb, :], lhsT=uTb[:, hh, :n],
                        rhs=BcF[:, hh, 2 * nb:2 * nb + 2, :].rearrange("p g i -> p (g i)"), start=True, stop=True),
                        reads=[uTb.b, BcF.b], writes=[pb[1 + nb]])
                rdx = [pb[1], pb[2], TI.b]
                gs = slice(hh * 4, hh * 4 + 4)
                S.op("dve", lambda e, X=X, gs=gs: e.tensor_tensor(out=xa[:n, gs, :], in0=X[:n, :, 0, :], in1=TI[:n, gs, 0, :], op=ALU.mult),
                     reads=rdx, writes=[xa.b])
                S.op("dve", lambda e, X=X, gs=gs: e.tensor_tensor(out=xb[:n, gs, :], in0=X[:n, :, 1, :], in1=TI[:n, gs, 1, :], op=ALU.mult),
                     reads=rdx, writes=[xb.b])
                S.op("dve", lambda e, X=X, gs=gs: e.tensor_tensor(out=xc[:n, gs, :], in0=X[:n, :, 0, :], in1=TI[:n, gs, 1, :], op=ALU.mult),
                     reads=rdx, writes=[xc.b])
                S.op("dve", lambda e, X=X, gs=gs: e.tensor_tensor(out=xd[:n, gs, :], in0=X[:n, :, 1, :], in1=TI[:n, gs, 0, :], op=ALU.mult),
                     reads=rdx, writes=[xd.b])

        def p2_cum(t, q):
            tok0, n = tile_info(t)
            sm = 0 if t < NPT else 1
            xa, xb, xc, xd = xpL[PAR[t]]
            bank = 5 + (q % 3)
            G = psf(bank).rearrange("p (g c i) -> p g c i", g=2, c=2)
            for gl in range(2):
                gp = 2 * q + gl
                for ri, (u, v, r2) in enumerate(((xa, xb, ntri_bf), (xc, xd, tri_bf))):
                    S.op("pe", lambda e, gl=gl, gp=gp, ri=ri, u=u: e.matmul(G[:, gl, ri, :n], lhsT=u[:n, gp, :],
                                                                          rhs=tri_bf[:n, sm, :n], start=True, stop=False),
                         reads=[u.b, tri_bf.b], writes=[pb[bank]])
                    S.op("pe", lambda e, gl=gl, gp=gp, ri=ri, v=v, r2=r2: e.matmul(G[:, gl, ri, :n], lhsT=v[:n, gp, :],
                                                                                  rhs=r2[:n, sm, :n], start=False, stop=True),
                         reads=[v.b, r2.b], writes=[pb[bank]])

        def p2_H(t, q):
            tok0, n = tile_info(t)
            sm = 0 if t < NPT else 1
            bank = 5 + (q % 3)
            G = psf(bank).rearrange("p (g c i) -> p g c i", g=2, c=2)
            if sm == 0:
                for gl in range(2):
                    gp = 2 * q + gl
                    a_, b_, c_, d_ = ab[gp % 2]
                    hr_s = Hst[:, 0, gp:gp + 1]
                    hi_s = Hst[:, 1, gp:gp + 1]
                    rd = [pb[bank], Hb[gp], Tfr.b, Tfi.b, nTfi.b]
                    S.op("dve", lambda e, gl=gl, gp=gp, a_=a_, hr_s=hr_s: e.scalar_tensor_tensor(
                        out=a_[:, :], in0=G[:, gl, 0, :], scalar=hr_s, in1=Tfr[:, gp, :], op0=ALU.add, op1=ALU.mult),
                        reads=rd, writes=[a_.b])
                    S.op("dve", lambda e, gl=gl, gp=gp, b_=b_, hi_s=hi_s: e.scalar_tensor_tensor(
                        out=b_[:, :], in0=G[:, gl, 1, :], scalar=hi_s, in1=Tfi[:, gp, :], op0=ALU.add, op1=ALU.mult),
                        reads=rd, writes=[b_.b])
                    S.op("dve", lambda e, gl=gl, gp=gp, c_=c_, hr_s=hr_s: e.scalar_tensor_tensor(
                        out=c_[:, :], in0=G[:, gl, 0, :], scalar=hr_s, in1=nTfi[:, gp, :], op0=ALU.add, op1=ALU.mult),
                        reads=rd, writes=[c_.b])
                    S.op("dve", lambda e, gl=gl, gp=gp, d_=d_, hi_s=hi_s: e.scalar_tensor_tensor(
                        out=d_[:, :], in0=G[:, gl, 1, :], scalar=hi_s, in1=Tfr[:, gp, :], op0=ALU.add, op1=ALU.mult),
                        reads=rd, writes=[d_.b])
                    S.op("pool", lambda e, gp=gp, a_=a_, b_=b_: e.tensor_tensor(out=Hst[:, 0, gp:gp + 1], in0=a_[:, 127:128],
                                                                               in1=b_[:, 127:128], op=ALU.subtract),
                         reads=[a_.b, b_.b, Hb[gp]], writes=[Hb[gp]])
                    S.op("pool", lambda e, gp=gp, c_=c_, d_=d_: e.tensor_tensor(out=Hst[:, 1, gp:gp + 1], in0=d_[:, 127:128],
                                                                               in1=c_[:, 127:128], op=ALU.subtract),
                         reads=[c_.b, d_.b, Hb[gp]], writes=[Hb[gp]])
                    S.op("pool", lambda e, gp=gp, a_=a_, b_=b_: e.tensor_tensor(out=hrT[:, gp, :], in0=a_[:, :], in1=b_[:, :],
                                                                               op=ALU.subtract),
                         reads=[a_.b, b_.b], writes=[hrT.b])
                    S.op("pool", lambda e, gp=gp, c_=c_, d_=d_: e.tensor_tensor(out=nhiT[:, gp, :], in0=c_[:, :], in1=d_[:, :],
                                                                               op=ALU.subtract),
                         reads=[c_.b, d_.b], writes=[nhiT.b])
            else:
                gs = slice(2 * q, 2 * q + 2)
                Gr = G[:, :, 0, 0:64].rearrange("p g (s t) -> p g s t", s=NS)
                Gi = G[:, :, 1, 0:64].rearrange("p g (s t) -> p g s t", s=NS)
                h0r = H0[:, 0, gs, :, None].broadcast_to([128, 2, NS, 4])
                h0i = H0[:, 1, gs, :, None].broadcast_to([128, 2, NS, 4])
                S.op("dve", lambda e: e.tensor_tensor(out=GH[0][:].rearrange("p g (s t) -> p g s t", s=NS),
                                                      in0=Gr, in1=h0r, op=ALU.add),
                     reads=[pb[bank], H0.b], writes=[GH[0].b])
                S.op("dve", lambda e: e.tensor_tensor(out=GH[1][:].rearrange("p g (s t) -> p g s t", s=NS),
                                                      in0=Gi, in1=h0i, op=ALU.add),
                     reads=[pb[bank], H0.b], writes=[GH[1].b])
                rdt = [GH[0].b, GH[1].b, Tfs[0].b, Tfs[1].b, Tfs[2].b]
                S.op("dve", lambda e: e.tensor_tensor(out=ws[0][:], in0=GH[0][:], in1=Tfs[0][:, gs, :], op=ALU.mult),
                     reads=rdt, writes=[ws[0].b])
                S.op("dve", lambda e: e.tensor_tensor(out=ws[1][:], in0=GH[1][:], in1=Tfs[1][:, gs, :], op=ALU.mult),
                     reads=rdt, writes=[ws[1].b])
                S.op("dve", lambda e: e.tensor_tensor(out=ws[2][:], in0=GH[0][:], in1=Tfs[2][:, gs, :], op=ALU.mult),
                     reads=rdt, writes=[ws[2].b])
                S.op("dve", lambda e: e.tensor_tensor(out=ws[3][:], in0=GH[1][:], in1=Tfs[0][:, gs, :], op=ALU.mult),
                     reads=rdt, writes=[ws[3].b])
                S.op("pool", lambda e: e.tensor_tensor(out=hrT[:, gs, 0:64], in0=ws[0][:], in1=ws[1][:], op=ALU.subtract),
                     reads=[ws[0].b, ws[1].b], writes=[hrT.b])
                S.op("pool", lambda e: e.tensor_tensor(out=nhiT[:, gs, 0:64], in0=ws[2][:], in1=ws[3][:], op=ALU.subtract),
                     reads=[ws[2].b, ws[3].b], writes=[nhiT.b])

                def last(w):
                    return w[:].rearrange("p g (s t) -> p g s t", s=NS)[:, :, :, 3]
                S.op("pool", lambda e: e.tensor_tensor(out=Hout[:, 0, gs, :], in0=last(ws[0]), in1=last(ws[1]), op=ALU.subtract),
                     reads=[ws[0].b, ws[1].b], writes=[Hout.b])
                S.op("pool", lambda e: e.tensor_tensor(out=Hout[:, 1, gs, :], in0=last(ws[3]), in1=last(ws[2]), op=ALU.subtract),
                     reads=[ws[2].b, ws[3].b], writes=[Hout.b])

        def p2_Y1(t):
            tok0, n = tile_info(t)
            sm = 0 if t < NPT else 1
            if sm == 0 and t == NPT - 1:
                S.dma("sp", o_s5_p[l].rearrange("c p g -> p c g"), Hst[:], reads=Hb)
            if sm == 1:
                S.dma("sp", o_s5_s[l].rearrange("c p n -> p c n"), Hout[:].rearrange("p c g s -> p c (g s)"), reads=[Hout.b])
            yT = psf(0)[:, 256:512].rearrange("p (a i) -> p a i", a=2)
            for gp in range(8):
                q0 = 32 * (gp % 4)
                S.op("pe", lambda e, gp=gp, q0=q0: e.matmul(yT[q0:q0 + 32, gp // 4, :n], lhsT=Ctb[:, 0, gp, :], rhs=hrT[:, gp, :n],
                                                           start=True, stop=False, tile_position=(0, q0)),
                     reads=[Ctb.b, hrT.b], writes=[pb[0]])
                S.op("pe", lambda e, gp=gp, q0=q0: e.matmul(yT[q0:q0 + 32, gp // 4, :n], lhsT=Ctb[:, 1, gp, :], rhs=nhiT[:, gp, :n],
                                                           start=False, stop=True, tile_position=(0, q0)),
                     reads=[Ctb.b, nhiT.b], writes=[pb[0]])

        def p2_Y2a(t):
            tok0, n = tile_info(t)
            uTf = uTfL[PAR[t]]
            yT = psf(0)[:, 256:512].rearrange("p (a i) -> p a i", a=2)
            for half in range(2):
                S.op("dve", lambda e, half=half: e.scalar_tensor_tensor(out=ysb[:, half, :n], in0=uTf[:, half, :n],
                                                                       scalar=dsk[:, half:half + 1], in1=yT[:, half, :n],
                                                                       op0=ALU.mult, op1=ALU.add),
                     reads=[uTf.b, prm.b, pb[0]], writes=[ysb.b])
            S.op("act", lambda e: e.activation(out=y5[:, :, :n], in_=ysb[:, :, :n], func=AF.Gelu_apprx_tanh), reads=[ysb.b], writes=[y5.b])
            S.op("act", lambda e: e.activation(out=y5b[:, :, :n], in_=y5[:, :, :n], func=AF.Copy), reads=[y5.b], writes=[y5b.b])
            zT = psf(0)[:, 256:512].rearrange("p (a i) -> p a i", a=2)
            for ho in range(2):
                for kc in range(2):
                    S.op("pe", lambda e, ho=ho, kc=kc: e.matmul(zT[:, ho, :n], lhsT=gluw[:, kc, ho * 128:(ho + 1) * 128],
                                                               rhs=y5b[:, kc, :n], start=(kc == 0), stop=(kc == 1)),
                         reads=[gluw.b, y5b.b], writes=[pb[0]])
            for ho in range(2):
                S.op("act", lambda e, ho=ho: e.activation(out=sig[:, ho, :n], in_=zT[:, ho, :n], func=AF.Sigmoid,
                                                         bias=glub[:, ho:ho + 1], scale=1.0),
                     reads=[pb[0], prm.b], writes=[sig.b])

        def p2_Y2b(t):
            tok0, n = tile_info(t)
            S.op("dve", lambda e: e.tensor_tensor(out=o5T[:, :, :n], in0=y5[:, :, :n], in1=sig[:, :, :n], op=ALU.mult),
                 reads=[y5.b, sig.b], writes=[o5T.b])
            for bank in range(2):
                for a in range(2):
                    S.op("pe", lambda e, bank=bank, a=a: e.matmul(PS[:n, 3 + bank, :], lhsT=o5T[:, a, :n],
                                                                 rhs=Wo5[:, a, bank * 512:(bank + 1) * 512],
                                                                 start=(a == 0), stop=(a == 1)),
                         reads=[o5T.b, Wo5.b], writes=[pb[3 + bank]])

        def p2_Y2c(t):
            tok0, n = tile_info(t)
            x = xt[t]
            S.op("dve", lambda e: e.tensor_tensor(out=x[:n, :], in0=psf(3, 2)[:n, :], in1=x[:n, :], op=ALU.add),
                 reads=[pb[3], pb[4], x.b], writes=[x.b])

        p2_Xa(ORD[0])
        p2_Xb(ORD[0])
        for q in range(3):
            p2_cum(ORD[0], q)
        for i, t in enumerate(ORD):
            nx = ORD[i + 1] if i + 1 < NT else None
            pv_ = ORD[i - 1] if i > 0 else None
            p2_H(t, 0)
            p2_cum(t, 3)
            if pv_ is not None:
                p2_Y2a(pv_)
            p2_H(t, 1)
            if nx is not None:
                p2_Xa(nx)
            p2_H(t, 2)
            if pv_ is not None:
                p2_Y2b(pv_)
            p2_H(t, 3)
            if pv_ is not None:
                p2_Y2c(pv_)
            p2_Y1(t)
            if nx is not None:
                p2_Xb(nx)
                for q in range(3):
                    p2_cum(nx, q)
        p2_Y2a(ORD[-1])
        p2_Y2b(ORD[-1])
        p2_Y2c(ORD[-1])
        S.barrier()
        A.release(m2)
        if stop_after == "P2":
            return True

        m3 = A.mark()
        gB = A.alloc("gB", [128, D], F32)
        S.dma("sp", gB[:], gvecs[2 * l + 1, :, :], writes=[gB.b])
        Wg = chunked(A.alloc("Wg", [128, 8, 1168], BF16), 8)
        WoG = chunked(A.alloc("WoG", [128, 3, D], BF16), 3)
        S0g = A.alloc("S0g", [48, NS, 4, 96], F32)
        glg = A.alloc("glg", [128, 96], F32)
        glrT = A.alloc("glrT", [32, 128], F32)
        lg = A.alloc("lg", [128, 192], F32)
        ex = [A.alloc("ex%d" % i, [128, 192], F32) for i in range(3)]
        qin = A.alloc("qin", [128, 192], BF16)
        kin = A.alloc("kin", [128, 192], BF16)
        ken = A.alloc("ken", [128, 192], BF16)
        vg = A.alloc("vg", [128, 4, 96], BF16)
        sgg = A.alloc("sgg", [128, 384], F32)
        qkT = A.alloc("qkT", [48, 8, 128], BF16)
        qinL = [qin, A.alloc("qin1", [128, 192], BF16)]
        kinL = [kin, A.alloc("kin1", [128, 192], BF16)]
        kenL = [ken, A.alloc("ken1", [128, 192], BF16)]
        vgL = [vg, A.alloc("vg1", [128, 4, 96], BF16)]
        qTf = A.alloc("qTf", [48, 4, 64], F32)
        scg = A.alloc("scg", [128, 4, 128], BF16)
        Sg = A.alloc("Sg", [48, 4, 96], F32)
        Sgb = A.alloc("Sgb", [48, 4, 96], BF16)
        dec = A.alloc("dec", [48, 4, NS], F32)
        ocg = A.alloc("ocg", [96, 4, 64], F32)
        og1 = A.alloc("og1", [128, 4, 96], F32)
        og2 = A.alloc("og2", [128, 4, 96], F32)
        sgt = A.alloc("sgt", [128, 16], F32)
        mixg = A.alloc("mixg", [128, 384], BF16)
        mixTg = A.alloc("mixTg", [128, 3, 128], BF16)
        kesg = A.alloc("kesg", [64, 192], BF16)
        qkTL = [qkT, A.alloc("qkT1", [48, 8, 128], BF16)]
        scgL = [scg, A.alloc("scg1", [128, 4, 128], BF16)]
        mixgL = [mixg, A.alloc("mixg1", [128, 384], BF16)]
        mixTgL = [mixTg, A.alloc("mixTg1", [128, 3, 128], BF16)]
        gatew = prv("gatew")

        for kc in range(8):
            load_cast(Wg[:, kc, :], w_in[l, kc * 128:(kc + 1) * 128, 1792:2960], Wg.b, kc)
        for kc in range(3):
            load_cast(WoG[:, kc, :], w_out[l, 640 + kc * 128:640 + (kc + 1) * 128, :], WoG.b, kc)
        S.dma("sp", S0g[:].rearrange("p s h v -> p (s h v)"), gla_s0[l, :, :], writes=[S0g.b])
        S.dma("sp", glg[:], glag[l, :, :], writes=[glg.b])
        S.op("pool", lambda e: e.memset(Sg[:], 0.0), writes=[Sg.b])
        S.op("pool", lambda e: e.memset(Sgb[:], 0.0), writes=[Sgb.b])
        S.op("pool", lambda e: e.memset(glrT[:], 1.0), writes=[glrT.b])

        sggL = [sgg, A.alloc("sgg1", [128, 384], F32)]
        decL = [dec, A.alloc("dec1", [48, 4, NS], F32)]
        widths = [512, 512, 144]

        def p3_A(t):
            tok0, n = tile_info(t)
            for bank in range(3):
                for kc in range(8):
                    S.op("pe", lambda e, bank=bank, kc=kc: e.matmul(
                        PS[:n, bank, 0:widths[bank]], lhsT=hT[:, kc, tok0:tok0 + n],
                        rhs=Wg[:, kc, bank * 512:bank * 512 + widths[bank]], start=(kc == 0), stop=(kc == 7)),
                        reads=[hTb[t], Wg.b], writes=[pb[bank]])

        def p3_B1(t):
            tok0, n = tile_info(t)
            sm = 0 if t < NPT else 1
            dec = decL[PAR[t]]
            gT = psf(2)[0:16, 256:384]
            for kc in range(8):
                S.op("pe", lambda e, kc=kc: e.matmul(gT[:, :n], lhsT=Wg[:, kc, 1152:1168], rhs=hT[:, kc, tok0:tok0 + n],
                                                    start=(kc == 0), stop=(kc == 7)), reads=[hTb[t], Wg.b], writes=[pb[2]])
            S.op("dve", lambda e: e.tensor_copy(out=glrT[0:16, :n], in_=gT[:, :n]), reads=[pb[2]], writes=[glrT.b])
            xg = psf(3)[:, 0:192]
            S.op("pe", lambda e: e.matmul(xg[:n, :], lhsT=glrT[0:17, :n], rhs=gatew[0:17, :], start=True, stop=True),
                 reads=[glrT.b, prm.b], writes=[pb[3]])
            S.op("act", lambda e: e.activation(out=ex[0][:n, :], in_=xg[:n, :], func=AF.Exp, scale=-1.0), reads=[pb[3]], writes=[ex[0].b])
            S.op("act", lambda e: e.activation(out=ex[1][:n, :], in_=ex[0][:n, :], func=AF.Ln, bias=1.0, scale=1.0),
                 reads=[ex[0].b], writes=[ex[1].b])
            S.op("dve", lambda e: e.tensor_scalar(out=lg[:n, :], in0=ex[1][:n, :], scalar1=-1.0 / 16, scalar2=None, op0=ALU.mult),
                 reads=[ex[1].b], writes=[lg.b])
            bcu = psf(4)[:, 0:384].rearrange("p (a d) -> p a d", a=2)
            S.op("pe", lambda e: e.matmul(bcu[:n, 0, :], lhsT=tri_f[:n, sm, :n], rhs=lg[:n, :], start=True, stop=True),
                 reads=[cc.b, lg.b], writes=[pb[4]])
            S.op("pe", lambda e: e.matmul(bcu[:n, 1, :], lhsT=upp_f[:n, sm, :n], rhs=lg[:n, :], start=True, stop=True),
                 reads=[cc.b, lg.b], writes=[pb[4]])
            bE = psf(3)[0:48, 192:192 + 64].rearrange("p (h s) -> p h s", h=4)
            ncol = 2 if sm == 0 else NS
            for h in range(4):
                if sm == 0:
                    rhs_ap = cc[:n, CC_OFF["tri"][0] + 126:CC_OFF["tri"][0] + 128]
                else:
                    rhs_ap = seqind[:n, :]
                S.op("pe", lambda e, h=h, rhs_ap=rhs_ap: e.matmul(bE[:, h, 0:ncol], lhsT=lg[:n, h * 48:(h + 1) * 48], rhs=rhs_ap,
                                                                 start=True, stop=True), reads=[lg.b, cc.b], writes=[pb[3]])
            S.op("act", lambda e: e.activation(out=ex[0][:n, :], in_=bcu[:n, 0, :], func=AF.Exp), reads=[pb[4]], writes=[ex[0].b])
            S.op("act", lambda e: e.activation(out=ex[1][:n, :], in_=bcu[:n, 0, :], func=AF.Exp, scale=-1.0), reads=[pb[4]], writes=[ex[1].b])
            S.op("act", lambda e: e.activation(out=ex[2][:n, :], in_=bcu[:n, 1, :], func=AF.Exp), reads=[pb[4]], writes=[ex[2].b])
            S.op("act", lambda e: e.activation(out=dec[:, :, 0:ncol], in_=bE[:, :, 0:ncol], func=AF.Exp), reads=[pb[3]], writes=[dec.b])

        def p3_B2(t):
            tok0, n = tile_info(t)
            sgg_ = sggL[PAR[t]]
            qin, kin, ken, vg = qinL[PAR[t]], kinL[PAR[t]], kenL[PAR[t]], vgL[PAR[t]]
            flat = psf(0, 3)
            S.op("dve", lambda e: e.scalar_tensor_tensor(out=qin[:n, :], in0=flat[:n, 0:192], scalar=48.0 ** -0.5, in1=ex[0][:n, :],
                                                         op0=ALU.mult, op1=ALU.mult), reads=[pb[0], ex[0].b], writes=[qin.b])
            S.op("dve", lambda e: e.tensor_tensor(out=kin[:n, :], in0=flat[:n, 192:384], in1=ex[1][:n, :], op=ALU.mult),
                 reads=[pb[0], ex[1].b], writes=[kin.b])
            S.op("dve", lambda e: e.tensor_tensor(out=ken[:n, :], in0=flat[:n, 192:384], in1=ex[2][:n, :], op=ALU.mult),
                 reads=[pb[0], ex[2].b], writes=[ken.b])
            S.op("act", lambda e: e.activation(out=vg[:n].rearrange("p h v -> p (h v)"), in_=flat[:n, 384:768], func=AF.Copy),
                 reads=[pb[0], pb[1]], writes=[vg.b])
            S.op("act", lambda e: e.activation(out=sgg_[:n, :], in_=flat[:n, 768:1152], func=AF.Exp, scale=-1.0),
                 reads=[pb[1], pb[2]], writes=[sgg_.b])
            S.op("act", lambda e: e.activation(out=sgg_[:n, :], in_=sgg_[:n, :], func=AF.Ln, bias=1.0, scale=1.0),
                 reads=[sgg_.b], writes=[sgg_.b])
            S.op("act", lambda e: e.activation(out=sgg_[:n, :], in_=sgg_[:n, :], func=AF.Exp, scale=-1.0),
                 reads=[sgg_.b], writes=[sgg_.b])
            S.op("dve", lambda e: e.tensor_tensor(out=sgg_[:n, :], in0=flat[:n, 768:1152], in1=sgg_[:n, :], op=ALU.mult),
                 reads=[pb[1], pb[2], sgg_.b], writes=[sgg_.b])
            S.op("pool", lambda e: e.tensor_tensor(out=sgg_[:n, :].rearrange("p (h v) -> p h v", h=4),
                                                   in0=sgg_[:n, :].rearrange("p (h v) -> p h v", h=4),
                                                   in1=glg[:n, None, :].broadcast_to([n, 4, 96]), op=ALU.mult),
                 reads=[sgg_.b, glg.b], writes=[sgg_.b])

        def p3_C(t):
            tok0, n = tile_info(t)
            sm = 0 if t < NPT else 1
            qin, kin, ken, vg = qinL[PAR[t]], kinL[PAR[t]], kenL[PAR[t]], vgL[PAR[t]]
            qkT, scg = qkTL[PAR[t]], scgL[PAR[t]]
            tpg = psbf(5).rearrange("p (a i) -> p a i", a=8)
            for h in range(4):
                S.op("pe", lambda e, h=h: e.transpose(out=tpg[0:48, h, :n], in_=qin[:n, h * 48:(h + 1) * 48], identity=ident_bf[:n, :n]),
                     reads=[qin.b, ident_bf.b], writes=[pb[5]])
                S.op("pe", lambda e, h=h: e.transpose(out=tpg[0:48, 4 + h, :n], in_=kin[:n, h * 48:(h + 1) * 48], identity=ident_bf[:n, :n]),
                     reads=[kin.b, ident_bf.b], writes=[pb[5]])
            S.op("act", lambda e: e.activation(out=qkT[:, :, :n], in_=tpg[0:48, :, :n], func=AF.Copy), reads=[pb[5]], writes=[qkT.b])
            if sm == 1:
                S.op("dve", lambda e: e.tensor_copy(out=qTf[:, :, :n], in_=tpg[0:48, 0:4, :n]), reads=[pb[5]], writes=[qTf.b])
            scp = psf(6).rearrange("p (h i) -> p h i", h=4)
            for h in range(4):
                S.op("pe", lambda e, h=h: e.matmul(scp[:n, h, :n], lhsT=qkT[0:48, 4 + h, :n], rhs=qkT[0:48, h, :n], start=True, stop=True),
                     reads=[qkT.b], writes=[pb[6]])
            S.op("dve", lambda e: e.tensor_tensor(out=scg[:n, :, :n], in0=scp[:n, :, :n],
                                                  in1=tri_f[:n, sm:sm + 1, :n].broadcast_to([n, 4, n]), op=ALU.mult),
                 reads=[pb[6], cc.b], writes=[scg.b])
            Og = psf(7)[:, 0:384].rearrange("p (h v) -> p h v", h=4)
            if sm == 1:
                OCT = psf(4)[:, 0:256].rearrange("p (h i) -> p h i", h=4)
                for s in range(NS):
                    for h in range(4):
                        S.op("pe", lambda e, s=s, h=h: e.matmul(OCT[0:96, h, 4 * s:4 * s + 4], lhsT=S0g[0:48, s, h, :],
                                                               rhs=qTf[0:48, h, 4 * s:4 * s + 4], start=True, stop=True),
                             reads=[S0g.b, qTf.b], writes=[pb[4]])
                S.op("act", lambda e: e.activation(out=ocg[:], in_=OCT[0:96, :, :], func=AF.Copy), reads=[pb[4]], writes=[ocg.b])
            for h in range(4):
                S.op("pe", lambda e, h=h: e.matmul(Og[:n, h, :], lhsT=scg[:n, h, :n], rhs=vg[:n, h, :], start=True, stop=False),
                     reads=[scg.b, vg.b], writes=[pb[7]])
                if sm == 0:
                    S.op("pe", lambda e, h=h: e.matmul(Og[:n, h, :], lhsT=qkT[0:48, h, :n], rhs=Sgb[0:48, h, :], start=False, stop=True),
                         reads=[qkT.b, Sgb.b], writes=[pb[7]])
                else:
                    S.op("pe", lambda e, h=h: e.matmul(Og[:n, h, :], lhsT=ocg[0:96, h, 0:64], rhs=ident_f[0:96, 0:96], start=False, stop=True),
                         reads=[ocg.b, cc.b], writes=[pb[7]])

        def p3_L(t):
            tok0, n = tile_info(t)
            sgg_ = sggL[PAR[t]]
            mixg = mixgL[PAR[t]]
            Og = psf(7)[:, 0:384].rearrange("p (h v) -> p h v", h=4)
            Ogf = psf(7)[:n, 0:384]
            S.op("act", lambda e: e.activation(out=og1[:n].rearrange("p h v -> p (h v)"), in_=Ogf, func=AF.Square), reads=[pb[7]], writes=[og1.b])
            S.op("dve", lambda e: e.tensor_reduce(out=sgt[:n, 0:4], in_=og1[:n], axis=AX.X, op=ALU.add), reads=[og1.b], writes=[sgt.b])
            S.op("dve", lambda e: e.tensor_scalar(out=sgt[:n, 4:8], in0=sgt[:n, 0:4], scalar1=1.0 / 96, scalar2=EPS, op0=ALU.mult, op1=ALU.add),
                 reads=[sgt.b], writes=[sgt.b])
            S.op("pool", lambda e: e.tensor_tensor(out=sgt[:n, 8:12], in0=sgt[:n, 4:8], in1=mhalf[:n, 0:4], op=ALU.pow),
                 reads=[sgt.b, mhalf.b], writes=[sgt.b])
            S.op("dve", lambda e: e.tensor_tensor(out=og2[:n], in0=Og[:n], in1=sgt[:n, 8:12, None].broadcast_to([n, 4, 96]), op=ALU.mult),
                 reads=[pb[7], sgt.b], writes=[og2.b])
            S.op("dve", lambda e: e.tensor_tensor(out=mixg[:n, :], in0=og2[:n].rearrange("p h v -> p (h v)"), in1=sgg_[:n, :], op=ALU.mult),
                 reads=[og2.b, sgg_.b], writes=[mixg.b], safe=True)

        def p3_G(t):
            tok0, n = tile_info(t)
            x = xt[t]
            mixg, mixTg = mixgL[PAR[t]], mixTgL[PAR[t]]
            mp = psbf(3)[:, 512:896].rearrange("p (a i) -> p a i", a=3)
            for a in range(3):
                S.op("pe", lambda e, a=a: e.transpose(out=mp[:, a, :n], in_=mixg[:n, a * 128:(a + 1) * 128], identity=ident_bf[:n, :n]),
                     reads=[mixg.b, ident_bf.b], writes=[pb[3]])
            S.op("act", lambda e: e.activation(out=mixTg[:, :, :n], in_=mp[:, :, :n], func=AF.Copy), reads=[pb[3]], writes=[mixTg.b])
            for bank in range(2):
                for a in range(3):
                    S.op("pe", lambda e, bank=bank, a=a: e.matmul(PS[:n, 5 + bank, :], lhsT=mixTg[:, a, :n],
                                                                 rhs=WoG[:, a, bank * 512:(bank + 1) * 512], start=(a == 0), stop=(a == 2)),
                         reads=[mixTg.b, WoG.b], writes=[pb[5 + bank]])
            S.op("dve", lambda e: e.tensor_tensor(out=x[:n, :], in0=psf(5, 2)[:n, :], in1=x[:n, :], op=ALU.add),
                 reads=[pb[5], pb[6], x.b], writes=[x.b])

        def p3_H(t):
            tok0, n = tile_info(t)
            sm = 0 if t < NPT else 1
            dec = decL[PAR[t]]
            qin, kin, ken, vg = qinL[PAR[t]], kinL[PAR[t]], kenL[PAR[t]], vgL[PAR[t]]
            if sm == 0:
                KVg = psf(4)[0:48, 0:384].rearrange("p (h v) -> p h v", h=4)
                for h in range(4):
                    S.op("pe", lambda e, h=h: e.matmul(KVg[:, h, :], lhsT=ken[:n, h * 48:(h + 1) * 48], rhs=vg[:n, h, :], start=True, stop=True),
                         reads=[ken.b, vg.b], writes=[pb[4]])
                for h in range(4):
                    S.op("dve", lambda e, h=h: e.scalar_tensor_tensor(out=Sg[:, h, :], in0=Sg[:, h, :], scalar=dec[:, h, 1:2],
                                                                     in1=KVg[:, h, :], op0=ALU.mult, op1=ALU.add),
                         reads=[Sg.b, dec.b, pb[4]], writes=[Sg.b])
                S.op("act", lambda e: e.activation(out=Sgb[:], in_=Sg[:], func=AF.Copy), reads=[Sg.b], writes=[Sgb.b])
                if t == NPT - 1:
                    S.dma("sp", o_gla_p[l, :, :], Sg[:].rearrange("p h v -> p (h v)"), reads=[Sg.b])
            else:
                for s in range(NS):
                    bank = 4 + (s % 2)
                    KVg = psf(bank)[0:48, 0:384].rearrange("p (h v) -> p h v", h=4)
                    S.op("dve", lambda e, s=s: e.tensor_scalar(out=kesg[:, :], in0=ken[:64, :], scalar1=seqind[:64, s:s + 1], scalar2=None,
                                                               op0=ALU.mult), reads=[ken.b, cc.b], writes=[kesg.b])
                    for h in range(4):
                        S.op("pe", lambda e, h=h, KVg=KVg: e.matmul(KVg[:, h, :], lhsT=kesg[:64, h * 48:(h + 1) * 48], rhs=vg[:64, h, :],
                                                                   start=True, stop=True), reads=[kesg.b, vg.b], writes=[pb[bank]])
                    for h in range(4):
                        S.op("dve", lambda e, s=s, h=h, KVg=KVg: e.scalar_tensor_tensor(
                            out=S0g[:, s, h, :], in0=S0g[:, s, h, :], scalar=dec[:, h, s:s + 1], in1=KVg[:, h, :],
                            op0=ALU.mult, op1=ALU.add), reads=[S0g.b, dec.b, pb[bank]], writes=[S0g.b])
                S.dma("sp", o_gla_s[l, :, :], S0g[:].rearrange("p s h v -> p (s h v)"), reads=[S0g.b])

        p3_B1(ORD[0])
        p3_A(ORD[0])
        p3_B2(ORD[0])
        p3_B1(ORD[1])
        for i, t in enumerate(ORD):
            nx = ORD[i + 1] if i + 1 < NT else None
            nx2 = ORD[i + 2] if i + 2 < NT else None
            p3_C(t)
            if i > 0:
                norm_tile(ORD[i - 1], gB, bank=5)
            p3_H(t)
            if nx is not None:
                p3_A(nx)
            p3_L(t)
            if nx is not None:
                p3_B2(nx)
            if nx2 is not None:
                p3_B1(nx2)
            p3_G(t)
        norm_tile(ORD[-1], gB, bank=5)
        S.barrier()
        A.release(m3)
        if stop_after == "P3":
            return True

        m5 = A.mark()
        ring = [(chunked(A.alloc("Wa%d" % i, [128, 8, 512], BF16), 8), chunked(A.alloc("Wgt%d" % i, [128, 8, 512], BF16), 8),
                 chunked(A.alloc("Wo%d" % i, [128, 4, D], BF16), 4)) for i in range(2)]
        asb = [A.alloc("asb%d" % i, [128, 514], F32) for i in range(2)]
        cv = [A.alloc("cv%d" % i, [128, 512], F32) for i in range(2)]
        ge = A.alloc("ge", [128, 512], F32)
        actT = [A.alloc("actT%d" % i, [128, 4, 512], BF16) for i in range(2)]
        carry = A.alloc("carry", [128, NFC, 2], F32)
        a_s = A.alloc("a_s", [128, NS, 6], F32)
        cvs = [A.alloc("cvs%d" % i, [128, NS, 4], F32) for i in range(2)]
        cn = A.alloc("cn", [128, NFC, 34], F32)
        cs0 = A.alloc("cs0", [128, NFC, NS, 2], F32)
        cw = prv("cw").rearrange("p (c k) -> p c k", c=NFC)
        cb = prv("cb")
        S.dma("sp", cs0[:].rearrange("p c s j -> p (c s j)"), conv_s0[l, :, :], writes=[cs0.b])
        S.op("pool", lambda e: e.memset(carry[:], 0.0), writes=[carry.b])
        it_box = [0]

        def ffn_chunk(cl, c0, G, tg, tk0, nt_, tiles, AT, Wa, Wgt):
            c = c0 + cl
            ba = (cl % 2) * 2
            aps = PS[:, ba, :]
            gps = PS[:, ba + 1, :]
            hrd = [hTb[tt] for tt in tiles]
            for kc in range(8):
                S.op("pe", lambda e, kc=kc: e.matmul(aps[:, :nt_], lhsT=Wa[:, kc, cl * 128:(cl + 1) * 128],
                                                    rhs=hT[:, kc, tk0:tk0 + nt_], start=(kc == 0), stop=(kc == 7)),
                     reads=[Wa.b] + hrd, writes=[pb[ba]])
            for kc in range(8):
                S.op("pe", lambda e, kc=kc: e.matmul(gps[:, :nt_], lhsT=Wgt[:, kc, cl * 128:(cl + 1) * 128],
                                                    rhs=hT[:, kc, tk0:tk0 + nt_], start=(kc == 0), stop=(kc == 7)),
                     reads=[Wgt.b] + hrd, writes=[pb[ba + 1]])
            w0 = cw[:, c, 0:1]
            w1 = cw[:, c, 1:2]
            w2 = cw[:, c, 2:3]
            bb = cb[:, c:c + 1]
            if tg < 4:
                a_t = asb[cl % 2]
                c_t = cv[cl % 2]
                S.op("act", lambda e: e.activation(out=a_t[:, 0:2], in_=carry[:, c, :], func=AF.Copy), reads=[carry.b], writes=[a_t.b])
                S.op("act", lambda e: e.activation(out=a_t[:, 2:514], in_=aps[:, :], func=AF.Copy), reads=[pb[ba]], writes=[a_t.b])
                S.op("act", lambda e: e.activation(out=carry[:, c, :], in_=a_t[:, 512:514], func=AF.Copy), reads=[a_t.b], writes=[carry.b])
                if tg == 3:
                    S.op("act", lambda e: e.activation(out=cn[:, c, 32:34], in_=a_t[:, 512:514], func=AF.Copy), reads=[a_t.b], writes=[cn.b])
                S.op("dve", lambda e: e.tensor_scalar(out=c_t[:, :], in0=a_t[:, 2:514], scalar1=w2, scalar2=bb, op0=ALU.mult, op1=ALU.add),
                     reads=[a_t.b, prm.b], writes=[c_t.b])
                S.op("dve", lambda e: e.scalar_tensor_tensor(out=c_t[:, :], in0=a_t[:, 1:513], scalar=w1, in1=c_t[:, :],
                                                             op0=ALU.mult, op1=ALU.add), reads=[a_t.b, prm.b, c_t.b], writes=[c_t.b], safe=True)
                S.op("dve", lambda e: e.scalar_tensor_tensor(out=c_t[:, :], in0=a_t[:, 0:512], scalar=w0, in1=c_t[:, :],
                                                             op0=ALU.mult, op1=ALU.add), reads=[a_t.b, prm.b, c_t.b], writes=[c_t.b], safe=True)
                S.op("act", lambda e: e.activation(out=ge[:, :], in_=c_t[:, :], func=AF.Gelu_apprx_tanh), reads=[c_t.b], writes=[ge.b])
                S.op("dve", lambda e: e.tensor_tensor(out=AT[:, cl, :], in0=gps[:, :], in1=ge[:, :], op=ALU.mult),
                     reads=[pb[ba + 1], ge.b], writes=[AT.b])
            else:
                c_t = cvs[cl % 2]
                S.op("act", lambda e: e.activation(out=a_s[:, :, 0:2], in_=cs0[:, c, :, :], func=AF.Copy), reads=[cs0.b], writes=[a_s.b])
                S.op("act", lambda e: e.activation(out=a_s[:, :, 2:6], in_=aps[:, 0:64].rearrange("p (s t) -> p s t", s=NS),
                                                   func=AF.Copy), reads=[pb[ba]], writes=[a_s.b])
                S.op("act", lambda e: e.activation(out=cn[:, c, 0:32].rearrange("p (s j) -> p s j", s=NS), in_=a_s[:, :, 4:6], func=AF.Copy),
                     reads=[a_s.b], writes=[cn.b])
                S.op("dve", lambda e: e.tensor_scalar(out=c_t[:], in0=a_s[:, :, 2:6], scalar1=w2, scalar2=bb, op0=ALU.mult, op1=ALU.add),
                     reads=[a_s.b, prm.b], writes=[c_t.b])
                S.op("dve", lambda e: e.scalar_tensor_tensor(out=c_t[:], in0=a_s[:, :, 1:5], scalar=w1, in1=c_t[:],
                                                             op0=ALU.mult, op1=ALU.add), reads=[a_s.b, prm.b, c_t.b], writes=[c_t.b])
                S.op("dve", lambda e: e.scalar_tensor_tensor(out=c_t[:], in0=a_s[:, :, 0:4], scalar=w0, in1=c_t[:],
                                                             op0=ALU.mult, op1=ALU.add), reads=[a_s.b, prm.b, c_t.b], writes=[c_t.b])
                S.op("act", lambda e: e.activation(out=ge[:, 0:64], in_=c_t[:].rearrange("p s t -> p (s t)"),
                                                   func=AF.Gelu_apprx_tanh), reads=[c_t.b], writes=[ge.b])
                S.op("dve", lambda e: e.tensor_tensor(out=AT[:, cl, 0:64], in0=gps[:, 0:64], in1=ge[:, 0:64], op=ALU.mult),
                     reads=[pb[ba + 1], ge.b], writes=[AT.b])

        def ffn_down(ti, tt, G, AT, Wo):
            n = 128 if tt < NPT else 64
            bd = 4 + (ti % 2) * 2
            for bank in range(2):
                for cl in range(G):
                    S.op("pe", lambda e, bank=bank, cl=cl: e.matmul(
                        PS[:n, bd + bank, :], lhsT=AT[:, cl, ti * 128:ti * 128 + n], rhs=Wo[:, cl, bank * 512:(bank + 1) * 512],
                        start=(cl == 0), stop=(cl == G - 1)), reads=[AT.b, Wo.b], writes=[pb[bd + bank]])
            xx = xt[tt]
            S.op("dve", lambda e: e.tensor_tensor(out=xx[:n, :], in0=psf(bd, 2)[:n, :], in1=xx[:n, :], op=ALU.add),
                 reads=[pb[bd], pb[bd + 1], xx.b], writes=[xx.b])

        def ffn_load(gi):
            c0, G = FGROUPS[gi]
            Wa, Wgt, Wo = ring[gi % 2]
            f0 = c0 * 128
            for kc in range(8):
                load_cast(Wa[:, kc, 0:G * 128], ffn_w_in[l, kc * 128:(kc + 1) * 128, f0:f0 + G * 128], Wa.b, kc)
                load_cast(Wgt[:, kc, 0:G * 128], ffn_w_in[l, kc * 128:(kc + 1) * 128, DFF + f0:DFF + f0 + G * 128], Wgt.b, kc)
            for cl in range(G):
                load_cast(Wo[:, cl, :], ffn_w_out[l, f0 + cl * 128:f0 + (cl + 1) * 128, :], Wo.b, cl)

        def ffn_group(gi, c0, G):
            Wa, Wgt, Wo = ring[gi % 2]
            if gi + 1 < len(FGROUPS):
                ffn_load(gi + 1)
            pend = []
            for tg in range(5):
                tk0 = tg * 512
                nt_ = 512 if tg < 4 else 64
                tiles = list(range(tg * 4, tg * 4 + 4)) if tg < 4 else [NPT]
                AT = actT[it_box[0] % 2]
                it_box[0] += 1
                for cl in range(G):
                    ffn_chunk(cl, c0, G, tg, tk0, nt_, tiles, AT, Wa, Wgt)
                    if pend:
                        ffn_down(*pend.pop(0))
                while pend:
                    ffn_down(*pend.pop(0))
                pend = [(ti, tt, G, AT, Wo) for ti, tt in enumerate(tiles)]
            while pend:
                ffn_down(*pend.pop(0))

        ffn_load(0)
        for gi, (c0, G) in enumerate(FGROUPS):
            ffn_group(gi, c0, G)
        S.dma("sp", o_conv[l, :, :], cn[:].rearrange("p c j -> p (c j)"), reads=[cn.b])
        S.barrier()
        A.release(m5)

    for l in range(DEPTH):
        if layer(l):
            break

    if stop_after is None:
        S.dma("sp", gA[:], gvecs[4, :, :], writes=[gA.b])
        m6 = A.mark()
        yo = [A.alloc("yo%d" % i, [128, D], F32) for i in range(2)]
        for t in range(NT):
            tok0, n = tile_info(t)
            x = xt[t]
            yy = yo[t % 2]
            norm_stats(t)
            k = t % 2
            S.op("dve", lambda e, x=x, n=n, yy=yy, k=k: e.scalar_tensor_tensor(out=yy[:n, :], in0=x[:n, :], scalar=rs[:n, 2 * k + 1:2 * k + 2],
                                                                              in1=gA[:n, :], op0=ALU.mult, op1=ALU.mult),
                 reads=[x.b, rsb[k], gA.b], writes=[yy.b])
            dst = y_p[tok0:tok0 + n, :] if t < NPT else y_s[:, :]
            S.dma("sp", dst, yy[:n, :], reads=[yy.b])
        A.release(m6)
    S.finalize()
    return nc, S


_CACHE = {}


def kernel(x_prompt, x_sample, state_ret, state_s5_re, state_s5_im, state_gla, state_ffn_conv,
           norm_mix_g, w_in, ret_norm_g, ret_norm_b,
           s5_lambda_re, s5_lambda_im, s5_log_dt, s5_b_re, s5_b_im, s5_c_re, s5_c_im, s5_d,
           s5_glu_w, s5_glu_b, gla_gate_w, gla_gate_b, gla_norm_g, w_out,
           norm_ffn_g, ffn_w_in, ffn_conv_w, ffn_conv_b, ffn_w_out, norm_final_g, _stop_after=None, _ncores=8, _trace=False):
    f32 = np.float32
    a = lambda v: np.ascontiguousarray(np.asarray(v, dtype=f32))
    x_prompt, x_sample = a(x_prompt), a(x_sample)
    state_ret, state_s5_re, state_s5_im = a(state_ret), a(state_s5_re), a(state_s5_im)
    state_gla, state_ffn_conv = a(state_gla), a(state_ffn_conv)
    w_in, w_out, ffn_w_in, ffn_w_out, s5_glu_w = a(w_in), a(w_out), a(ffn_w_in), a(ffn_w_out), a(s5_glu_w)
    n_cores = 8
    key = _stop_after
    if key not in _CACHE:
        _CACHE[key] = build_nc(_stop_after)
    nc, _ = _CACHE[key]
    cc, cr, rope_t = _const_tables()

    gvecs = np.stack([np.broadcast_to(a(v)[None, :], (128, D)) for v in
                      (norm_mix_g[0], norm_ffn_g[0], norm_mix_g[1], norm_ffn_g[1], norm_final_g)]).astype(f32)
    retgb = np.stack([np.concatenate([np.broadcast_to(a(ret_norm_g[l])[None, :], (128, 384)),
                                      np.broadcast_to(a(ret_norm_b[l])[None, :], (128, 384))], axis=1)
                      for l in range(DEPTH)]).astype(f32)
    glag = np.stack([np.broadcast_to(a(gla_norm_g[l])[None, :], (128, 96)) for l in range(DEPTH)]).astype(f32)
    prm = np.zeros((DEPTH, 128, PR_N), dtype=f32)

    def put(l, name, arr):
        o, w = PR_OFF[name]
        arr = np.asarray(arr, dtype=f32).reshape(arr.shape[0], -1)
        prm[l, :arr.shape[0], o:o + arr.shape[1]] = arr

    for l in range(DEPTH):
        def gp_layout(v):
            return a(v).reshape(8, 2, 64).transpose(1, 2, 0).reshape(128, 8)
        put(l, "lamr", gp_layout(s5_lambda_re[l]))
        put(l, "lami", gp_layout(s5_lambda_im[l]))
        put(l, "ldt", gp_layout(np.broadcast_to(a(s5_log_dt[l])[:, None], (16, 64))))
        for nm, src in (("br", s5_b_re), ("bi", s5_b_im)):
            v = a(src[l]).reshape(8, 2, 64, 16).transpose(1, 2, 0, 3).reshape(128, 8 * 16)
            put(l, nm, v)
        for nm, src in (("ctr", s5_c_re), ("cti", s5_c_im)):
            v = a(src[l]).reshape(8, 2, 16, 64)
            blk = np.zeros((2, 64, 8, 2, 16), dtype=f32)
            for g2 in range(2):
                blk[g2, :, :, g2, :] = v[:, g2, :, :].transpose(2, 0, 1)
            put(l, nm, blk.reshape(128, 8 * 32))
        put(l, "dsk", a(s5_d[l]).reshape(2, 128).T)
        put(l, "glub", a(s5_glu_b[l]).reshape(2, 128).T)
        put(l, "gatew", np.concatenate([a(gla_gate_w[l]), a(gla_gate_b[l])[None, :]], axis=0))
        put(l, "cw", a(ffn_conv_w[l]).reshape(3, NFC, 128).transpose(2, 1, 0).reshape(128, NFC * 3))
        put(l, "cb", a(ffn_conv_b[l]).reshape(NFC, 128).T)

    in_maps = []
    for c in range(n_cores):
        sl = slice(c * NS, (c + 1) * NS)
        rs0 = state_ret[:, sl].reshape(DEPTH, NS, 3, 2, 64, 64).transpose(0, 3, 4, 1, 2, 5).reshape(DEPTH, 128, NS * 192)
        gs0 = state_gla[:, sl].transpose(0, 3, 1, 2, 4).reshape(DEPTH, 48, NS * 384)
        h0 = np.stack([state_s5_re[:, sl], state_s5_im[:, sl]], axis=1)
        h0 = h0.reshape(DEPTH, 2, NS, 8, 2, 64).transpose(0, 1, 4, 5, 3, 2).reshape(DEPTH, 2, 128, 8 * NS)
        cs0 = state_ffn_conv[:, sl].reshape(DEPTH, NS, 2, NFC, 128).transpose(0, 4, 3, 1, 2).reshape(DEPTH, 128, NFC * NS * 2)
        in_maps.append({
            "xp": x_prompt[c], "xs": x_sample[sl].reshape(64, D),
            "w_in": w_in, "w_out": w_out, "ffn_w_in": ffn_w_in, "ffn_w_out": ffn_w_out, "glu_w": s5_glu_w,
            "gvecs": gvecs, "retgb": retgb, "glag": glag, "prm": prm, "cc": cc, "cr": cr, "rope": rope_t,
            "ret_s0": np.ascontiguousarray(rs0), "gla_s0": np.ascontiguousarray(gs0),
            "s5_h0": np.ascontiguousarray(h0), "conv_s0": np.ascontiguousarray(cs0),
        })
    if _trace:
        res = run_bass_kernel_spmd(nc, in_maps[:_ncores], core_ids=list(range(_ncores)), trace=True)
        print("EXEC_TIME_NS", res.exec_time_ns)
    else:
        res = run_bass_kernel_spmd(nc, in_maps[:_ncores], core_ids=list(range(_ncores)))
    R = list(res.results)
    while len(R) < n_cores:
        R.append(R[0])
    y_prompt = np.stack([R[c]["y_p"] for c in range(n_cores)]).astype(f32)
    y_sample = np.concatenate([R[c]["y_s"].reshape(NS, LS, D) for c in range(n_cores)], axis=0).astype(f32)
    ret_p = np.stack([R[c]["o_ret_p"].reshape(DEPTH, 2, 64, 3, 64).transpose(0, 3, 1, 2, 4).reshape(DEPTH, 6, 64, 64)
                      for c in range(n_cores)], axis=1)
    ret_s = np.concatenate([R[c]["o_ret_s"].reshape(DEPTH, 2, 64, NS, 3, 64).transpose(0, 3, 4, 1, 2, 5).reshape(DEPTH, NS, 6, 64, 64)
                            for c in range(n_cores)], axis=1)
    s5p = np.stack([R[c]["o_s5_p"].reshape(DEPTH, 2, 2, 64, 8).transpose(0, 1, 4, 2, 3).reshape(DEPTH, 2, 16, 64)
                    for c in range(n_cores)], axis=2)
    s5s = np.concatenate([R[c]["o_s5_s"].reshape(DEPTH, 2, 2, 64, 8, NS).transpose(0, 1, 5, 4, 2, 3).reshape(DEPTH, 2, NS, 16, 64)
                          for c in range(n_cores)], axis=2)
    gla_p = np.stack([R[c]["o_gla_p"].reshape(DEPTH, 48, 4, 96).transpose(0, 2, 1, 3) for c in range(n_cores)], axis=1)
    gla_s = np.concatenate([R[c]["o_gla_s"].reshape(DEPTH, 48, NS, 4, 96).transpose(0, 2, 3, 1, 4) for c in range(n_cores)], axis=1)
    cv = [R[c]["o_conv"].reshape(DEPTH, 128, NFC, 34) for c in range(n_cores)]
    conv_p = np.stack([v[:, :, :, 32:34].transpose(0, 3, 2, 1).reshape(DEPTH, 2, DFF) for v in cv], axis=1)
    conv_s = np.concatenate([v[:, :, :, 0:32].reshape(DEPTH, 128, NFC, NS, 2).transpose(0, 3, 4, 2, 1).reshape(DEPTH, NS, 2, DFF)
                             for v in cv], axis=1)
    c_ = np.ascontiguousarray
    return (c_(y_prompt), c_(y_sample), c_(ret_p.astype(f32)), c_(ret_s.astype(f32)),
            c_(s5p[:, 0].astype(f32)), c_(s5s[:, 0].astype(f32)), c_(s5p[:, 1].astype(f32)), c_(s5s[:, 1].astype(f32)),
            c_(gla_p.astype(f32)), c_(gla_s.astype(f32)), c_(conv_p.astype(f32)), c_(conv_s.astype(f32)))
```
